# Optimizing a Trainium2 kernel written in Bass

```python
import jax, jax.numpy as jnp
from jax import lax
import numpy as np

D_MODEL = 2048
BATCH = 4
SEQ = 4096
DEPTH = 1
DEC_BATCH = 16
DEC_SEQ = 2048
PAST_LEN = 128

MLA_HEADS = 8
MLA_NOPE = 128
MLA_ROPE = 64
MLA_QK = MLA_NOPE + MLA_ROPE
MLA_V = 128
Q_LORA = 512
KV_LORA = 256
Q_BLOCK = 128

RET_HEADS = 8
RET_DK = 128
RET_DV = 128
RET_CHUNK = 128

MIX_WIDTH = MLA_HEADS * MLA_V + RET_HEADS * RET_DV
SPLIT_SIZES = (Q_LORA, KV_LORA, MLA_ROPE, RET_HEADS * RET_DK, RET_HEADS * RET_DK, RET_HEADS * RET_DV, RET_HEADS * RET_DV)
IN_WIDTH = Q_LORA + KV_LORA + MLA_ROPE + 2 * RET_HEADS * RET_DK + 2 * RET_HEADS * RET_DV

PEER_HEADS = 8
PEER_NKEYS = 128
PEER_EXPERTS = PEER_NKEYS * PEER_NKEYS
PEER_DKEY = 128
PEER_TOPK = 16
PEER_BLOCK = 128

ROPE_BASE = 10000.0
EPS = 1e-6

kernel_name = "hybrid_mla_retention_peer_adaln_encoder"


def _rms_norm(x, w):
    xf = x.astype(jnp.float32)
    y = xf * lax.rsqrt(jnp.mean(xf * xf, axis=-1, keepdims=True) + EPS)
    return (y * w.astype(jnp.float32)).astype(x.dtype)


def _rope_tables(seq_len, dim, dtype):
    inv = 1.0 / (ROPE_BASE ** (jnp.arange(0, dim, 2, dtype=jnp.float32) / dim))
    ang = jnp.arange(seq_len, dtype=jnp.float32)[:, None] * inv[None, :]
    return jnp.cos(ang).astype(dtype), jnp.sin(ang).astype(dtype)


def _apply_rope(x, cos, sin):
    half = x.shape[-1] // 2
    x1, x2 = x[..., :half], x[..., half:]
    return jnp.concatenate([x1 * cos - x2 * sin, x1 * sin + x2 * cos], axis=-1)


def _mla(c_q, c_kv, k_rope, q_a_norm, kv_a_norm, w_uq, w_uk, w_uv, q_norm, k_norm):
    B, S, _ = c_q.shape
    H = MLA_HEADS
    q = (_rms_norm(c_q, q_a_norm) @ w_uq).reshape(B, S, H, MLA_QK).transpose(0, 2, 1, 3)
    ckv = _rms_norm(c_kv, kv_a_norm)
    k_nope = (ckv @ w_uk).reshape(B, S, H, MLA_NOPE).transpose(0, 2, 1, 3)
    v = (ckv @ w_uv).reshape(B, S, H, MLA_V).transpose(0, 2, 1, 3)
    k_r = jnp.broadcast_to(k_rope[:, None], (B, H, S, MLA_ROPE))
    k = jnp.concatenate([k_nope, k_r], axis=-1)
    q = _rms_norm(q, q_norm)
    k = _rms_norm(k, k_norm)
    cos, sin = _rope_tables(S, MLA_ROPE, q.dtype)
    q = jnp.concatenate([q[..., :MLA_NOPE], _apply_rope(q[..., MLA_NOPE:], cos, sin)], axis=-1)
    k = jnp.concatenate([k[..., :MLA_NOPE], _apply_rope(k[..., MLA_NOPE:], cos, sin)], axis=-1)
    scale = MLA_QK ** -0.5
    nq = S // Q_BLOCK
    qb = q.reshape(B, H, nq, Q_BLOCK, MLA_QK).transpose(2, 0, 1, 3, 4)

    def attend(q_blk):
        s = jnp.einsum('bhqd,bhkd->bhqk', q_blk, k).astype(jnp.float32) * scale
        p = jax.nn.softmax(s, axis=-1).astype(v.dtype)
        return jnp.einsum('bhqk,bhkd->bhqd', p, v)

    o = lax.map(attend, qb)
    return o.transpose(1, 0, 3, 2, 4).reshape(B, S, H * MLA_V)


def _retention_direction(q, k, v, log_gamma, include_diag):
    C = q.shape[3]
    pos = jnp.arange(C, dtype=jnp.float32)
    rel = pos[:, None] - pos[None, :]
    mask = (rel >= 0) if include_diag else (rel > 0)
    decay = jnp.where(mask[None], jnp.exp(log_gamma[:, None, None] * jnp.maximum(rel, 0.0)[None]), 0.0)
    scores = jnp.einsum('bhncd,bhnjd->bhncj', q, k) * decay[:, None].astype(q.dtype)
    intra = jnp.einsum('bhncj,bhnje->bhnce', scores, v)
    k_decay = jnp.exp(log_gamma[:, None] * (C - 1.0 - pos)[None]).astype(k.dtype)
    kv = jnp.einsum('bhnjd,bhnje->bhnde', k * k_decay[:, None, :, None], v).astype(jnp.float32)
    chunk_decay = jnp.exp(log_gamma * C)

    def step(state, kv_c):
        return state * chunk_decay[None, :, None, None] + kv_c, state

    B, H = q.shape[0], q.shape[1]
    init = jnp.zeros((B, H, q.shape[-1], v.shape[-1]), jnp.float32)
    _, prev = lax.scan(step, init, kv.transpose(2, 0, 1, 3, 4))
    prev = prev.transpose(1, 2, 0, 3, 4).astype(q.dtype)
    q_decay = jnp.exp(log_gamma[:, None] * (pos + 1.0)[None]).astype(q.dtype)
    cross = jnp.einsum('bhncd,bhnde->bhnce', q * q_decay[:, None, :, None], prev)
    return intra + cross


def _retention(rq, rk, rv, rg, decay_logit, gn_w):
    B, S, _ = rq.shape
    H = RET_HEADS
    nc = S // RET_CHUNK

    def heads(t, d):
        return t.reshape(B, S, H, d).transpose(0, 2, 1, 3)

    cos, sin = _rope_tables(S, RET_DK, rq.dtype)
    q = _apply_rope(heads(rq, RET_DK), cos, sin)
    k = _apply_rope(heads(rk, RET_DK), cos, sin) * (RET_DK ** -0.5)
    v = heads(rv, RET_DV)

    def chunk(t):
        return t.reshape(B, H, nc, RET_CHUNK, t.shape[-1])

    def flip(t):
        return jnp.flip(t, axis=2)

    log_g = jax.nn.log_sigmoid(decay_logit.astype(jnp.float32))
    fwd = _retention_direction(chunk(q), chunk(k), chunk(v), log_g[0], True)
    bwd = _retention_direction(chunk(flip(q)), chunk(flip(k)), chunk(flip(v)), log_g[1], False)
    o = fwd.reshape(B, H, S, RET_DV) + flip(bwd.reshape(B, H, S, RET_DV))
    of = o.astype(jnp.float32)
    mu = jnp.mean(of, axis=-1, keepdims=True)
    var = jnp.mean(jnp.square(of - mu), axis=-1, keepdims=True)
    on = (of - mu) * lax.rsqrt(var + EPS)
    on = on.transpose(0, 2, 1, 3).reshape(B, S, H * RET_DV) * gn_w.astype(jnp.float32)
    return jax.nn.silu(rg) * on.astype(rg.dtype)


def _peer(h, peer_wq, peer_sub_keys, peer_u, peer_v):
    B, S, D = h.shape
    T = B * S
    ht = h.reshape(T, D)
    q = (ht @ peer_wq).reshape(T, PEER_HEADS, 2, PEER_DKEY)
    s = jnp.einsum('thpd,hpkd->thpk', q, peer_sub_keys).astype(jnp.float32)
    s1, i1 = lax.top_k(s[:, :, 0], PEER_TOPK)
    s2, i2 = lax.top_k(s[:, :, 1], PEER_TOPK)
    cand_s = (s1[..., :, None] + s2[..., None, :]).reshape(T, PEER_HEADS, PEER_TOPK * PEER_TOPK)
    cand_i = (i1[..., :, None] * PEER_NKEYS + i2[..., None, :]).reshape(T, PEER_HEADS, PEER_TOPK * PEER_TOPK)
    top_s, sel = lax.top_k(cand_s, PEER_TOPK)
    idx = jnp.take_along_axis(cand_i, sel, axis=-1)
    g = jax.nn.softmax(top_s, axis=-1).astype(h.dtype)
    nb = T // PEER_BLOCK

    def block(args):
        hb, ib, gb = args
        u = jnp.take(peer_u, ib, axis=0)
        a = jnp.einsum('td,thkd->thk', hb, u)
        w = gb * jax.nn.gelu(a)
        vs = jnp.take(peer_v, ib, axis=0)
        return jnp.einsum('thk,thkd->td', w, vs)

    out = lax.map(block, (ht.reshape(nb, PEER_BLOCK, D),
                          idx.reshape(nb, PEER_BLOCK, PEER_HEADS, PEER_TOPK),
                          g.reshape(nb, PEER_BLOCK, PEER_HEADS, PEER_TOPK)))
    return out.reshape(B, S, D)


def _layer(x, c, norm1_w, norm2_w, w_ada, b_ada, w_in, q_a_norm, kv_a_norm, w_uq, w_uk, w_uv,
           q_norm, k_norm, ret_decay_logit, ret_gn_w, w_o, peer_wq, peer_sub_keys, peer_u, peer_v):
    mod = (jax.nn.silu(c) @ w_ada + b_ada)[:, None, :]
    shift1, scale1, gate1, shift2, scale2, gate2 = jnp.split(mod, 6, axis=-1)
    h = _rms_norm(x, norm1_w) * (1 + scale1) + shift1
    proj = h @ w_in
    offs = np.cumsum(np.array(SPLIT_SIZES))[:-1].tolist()
    c_q, c_kv, k_rope, rq, rk, rv, rg = jnp.split(proj, offs, axis=-1)
    a_out = _mla(c_q, c_kv, k_rope, q_a_norm, kv_a_norm, w_uq, w_uk, w_uv, q_norm, k_norm)
    r_out = _retention(rq, rk, rv, rg, ret_decay_logit, ret_gn_w)
    mix = jnp.concatenate([a_out, r_out], axis=-1) @ w_o
    x = x + gate1 * mix
    h2 = _rms_norm(x, norm2_w) * (1 + scale2) + shift2
    x = x + gate2 * _peer(h2, peer_wq, peer_sub_keys, peer_u, peer_v)
    return x


def _trunk(x, c, params):
    for l in range(DEPTH):
        x = _layer(x, c, *[p[l] for p in params])
    return x


def setup_inputs(seed: int = 0) -> dict:
    key = jax.random.key(seed)
    ks = jax.random.split(key, 24)
    f32 = jnp.float32
    D = D_MODEL

    def nrm(k, shape, scale):
        return jax.random.normal(k, shape, f32) * scale

    g = 1.0 - 2.0 ** (-5.0 - jnp.arange(RET_HEADS, dtype=f32))
    base_logit = jnp.log(g) - jnp.log1p(-g)
    return {
        "x_prompt": nrm(ks[0], (BATCH, SEQ, D), 1.0),
        "x_sample": nrm(ks[1], (DEC_BATCH, DEC_SEQ, D), 1.0),
        "c_prompt": nrm(ks[2], (BATCH, D), 1.0),
        "c_sample": nrm(ks[3], (DEC_BATCH, D), 1.0),
        "norm1_w": 1.0 + nrm(ks[4], (DEPTH, D), 0.01),
        "norm2_w": 1.0 + nrm(ks[5], (DEPTH, D), 0.01),
        "w_ada": nrm(ks[6], (DEPTH, D, 6 * D), 0.5 * D ** -0.5),
        "b_ada": nrm(ks[7], (DEPTH, 6 * D), 0.01),
        "w_in": nrm(ks[8], (DEPTH, D, IN_WIDTH), D ** -0.5),
        "q_a_norm": 1.0 + nrm(ks[9], (DEPTH, Q_LORA), 0.01),
        "kv_a_norm": 1.0 + nrm(ks[10], (DEPTH, KV_LORA), 0.01),
        "w_uq": nrm(ks[11], (DEPTH, Q_LORA, MLA_HEADS * MLA_QK), Q_LORA ** -0.5),
        "w_uk": nrm(ks[12], (DEPTH, KV_LORA, MLA_HEADS * MLA_NOPE), KV_LORA ** -0.5),
        "w_uv": nrm(ks[13], (DEPTH, KV_LORA, MLA_HEADS * MLA_V), KV_LORA ** -0.5),
        "q_norm": 1.0 + nrm(ks[14], (DEPTH, MLA_QK), 0.01),
        "k_norm": 1.0 + nrm(ks[15], (DEPTH, MLA_QK), 0.01),
        "ret_decay_logit": jnp.broadcast_to(base_logit, (DEPTH, 2, RET_HEADS)) + nrm(ks[16], (DEPTH, 2, RET_HEADS), 0.01),
        "ret_gn_w": 1.0 + nrm(ks[17], (DEPTH, RET_HEADS * RET_DV), 0.01),
        "w_o": nrm(ks[18], (DEPTH, MIX_WIDTH, D), MIX_WIDTH ** -0.5),
        "peer_wq": nrm(ks[19], (DEPTH, D, PEER_HEADS * 2 * PEER_DKEY), D ** -0.5),
        "peer_sub_keys": nrm(ks[20], (DEPTH, PEER_HEADS, 2, PEER_NKEYS, PEER_DKEY), PEER_DKEY ** -0.5),
        "peer_u": nrm(ks[21], (DEPTH, PEER_EXPERTS, D), D ** -0.5),
        "peer_v": nrm(ks[22], (DEPTH, PEER_EXPERTS, D), 0.25),
    }


def reference(x_prompt, x_sample, c_prompt, c_sample, norm1_w, norm2_w, w_ada, b_ada, w_in,
              q_a_norm, kv_a_norm, w_uq, w_uk, w_uv, q_norm, k_norm, ret_decay_logit, ret_gn_w,
              w_o, peer_wq, peer_sub_keys, peer_u, peer_v):
    params = (norm1_w, norm2_w, w_ada, b_ada, w_in, q_a_norm, kv_a_norm, w_uq, w_uk, w_uv,
              q_norm, k_norm, ret_decay_logit, ret_gn_w, w_o, peer_wq, peer_sub_keys, peer_u, peer_v)
    y_prompt = _trunk(x_prompt, c_prompt, params)
    y_sample = _trunk(x_sample, c_sample, params)
    return (y_prompt, y_sample)
```

```python
from contextlib import ExitStack
import numpy as np
import concourse.bass as bass
import concourse.mybir as mybir
from concourse.bass_utils import run_bass_kernel_spmd

F32 = mybir.dt.float32
BF16 = mybir.dt.bfloat16
ALU = mybir.AluOpType
AF = mybir.ActivationFunctionType
AX = mybir.AxisListType

D = 2048
KC = 16
T = 512
H = 8
EPS = 1e-6
NCORES = 8


class Buf:
    __slots__ = ("name", "w", "r")

    def __init__(self, name=""):
        self.name = name
        self.w = {}
        self.r = {}


class Sched:
    NDMA = 16
    SAME = {"act": True, "dve": True, "pool": True, "pe": False, "sp": True}

    def __init__(self, nc, stack):
        self.nc = nc
        self.stack = stack
        self.eng = {"pe": nc.tensor, "act": nc.scalar, "dve": nc.vector, "pool": nc.gpsimd, "sp": nc.sync}
        self.csem, self.ccnt = {}, {}
        self.known = {k: {} for k in self.eng}
        self.nsem = 0
        for k in self.eng:
            self._new_csem(k)
        self.dsem = {k: [] for k in self.eng}
        self.drr = {k: 0 for k in self.eng}
        self.n_inst = 0

    def _alloc_sem(self, name):
        self.nsem += 1
        return self.stack.enter_context(self.nc.semaphore(f"{name}_{self.nsem}"))

    def _new_csem(self, k):
        self.csem[k] = self._alloc_sem("c" + k)
        self.ccnt[k] = 0

    def _wait(self, k, tok, raw=True, fam=None, force=False):
        sem, val, src = tok
        if not force:
            if src == fam and not raw:
                return False
            if src == k and not self.SAME[k]:
                return False
        kn = self.known[k]
        if kn.get(id(sem), 0) >= val:
            return True
        self.eng[k].wait_ge(sem, val)
        self.n_inst += 1
        kn[id(sem)] = val
        return True

    def _deps(self, k, reads, writes, fam):
        for b in reads:
            for t in b.w.values():
                self._wait(k, t, True, fam)
        for b in writes:
            for t in b.w.values():
                self._wait(k, t, False, fam)
            for t in b.r.values():
                self._wait(k, t, False, fam)

    def _commit(self, tok, reads, writes, fam):
        for b in writes:
            b.w = {i: t for i, t in b.w.items() if t[2] == fam}
            b.w[id(tok[0])] = tok
            b.r = {}
        for b in reads:
            if id(tok[0]) in b.w and b.w[id(tok[0])] is tok:
                continue
            b.r[id(tok[0])] = tok

    def op(self, k, fn, reads=(), writes=()):
        self._deps(k, reads, writes, k)
        ins = fn(self.eng[k])
        self.n_inst += 1
        if self.ccnt[k] >= 30000:
            self._new_csem(k)
        self.ccnt[k] += 1
        ins.then_inc(self.csem[k], 1)
        tok = (self.csem[k], self.ccnt[k], k)
        self._commit(tok, reads, writes, k)
        return tok

    def dma(self, k, out, in_, reads=(), writes=(), **kw):
        fam = "dma:" + k
        self._deps(k, reads, writes, fam)
        pool = self.dsem[k]
        if len(pool) < self.NDMA:
            pool.append([self._alloc_sem("d" + k), 0])
            ent = pool[-1]
        else:
            ent = pool[self.drr[k] % self.NDMA]
            self.drr[k] += 1
            self._wait(k, (ent[0], ent[1], fam), force=True)
        ent[1] += 16
        ins = self.eng[k].dma_start(out=out, in_=in_, **kw)
        ins.then_inc(ent[0], 16)
        self.n_inst += 1
        tok = (ent[0], ent[1], fam)
        self._commit(tok, reads, writes, fam)
        return tok

    def barrier(self, engines=None):
        toks = []
        for k in self.eng:
            if self.ccnt[k] > 0:
                toks.append((self.csem[k], self.ccnt[k], k))
            for ent in self.dsem[k]:
                if ent[1] > 0:
                    toks.append((ent[0], ent[1], "dma:" + k))
        for k in (engines or self.eng):
            for t in toks:
                self._wait(k, t, force=True)


def apx(ap, pattern):
    return bass.AP(ap.tensor, ap.offset, [list(ap.ap[0])] + [list(p) for p in pattern])


def build(SA, SS, upto=9, dbg_out=()):
    NOWN = SA + 2 * SS
    NTOK = NOWN + SA
    SEGS = [(0, SA), (SA, SS), (SA + SS, SS)]
    nc = bass.Bass("TRN2", target_bir_lowering=False)

    def din(name, shape, dt=F32):
        return nc.dram_tensor(name, list(shape), dt, kind="ExternalInput").ap()

    def dscr(name, shape, dt=BF16):
        return nc.dram_tensor(name, list(shape), dt).ap()

    xin = din("xin", [NTOK, D])
    c3 = din("c3", [3, D])
    norm1_w = din("norm1_w", [D]); norm2_w = din("norm2_w", [D])
    w_ada = din("w_ada", [D, 6 * D]); b_ada = din("b_ada", [6 * D])
    w_in = din("w_in", [D, 4928])
    q_a_norm = din("q_a_norm", [512]); kv_a_norm = din("kv_a_norm", [256])
    w_uq = din("w_uq", [512, 1536]); w_uk = din("w_uk", [256, 1024]); w_uv = din("w_uv", [256, 1024])
    q_norm = din("q_norm", [192]); k_norm = din("k_norm", [192])
    rdl = din("ret_decay_logit", [16]); gn_w = din("ret_gn_w", [1024])
    w_o = din("w_o", [D, D]); peer_wq = din("peer_wq", [D, D])
    sub_keys = din("peer_sub_keys", [16 * 128, 128])
    peer_u = din("peer_u", [16384, D]); peer_v = din("peer_v", [16384, D])
    cosM = din("cosM", [128, NTOK]); sinM = din("sinM", [128, NTOK])
    cosR = din("cosR", [128, NTOK]); sinR = din("sinR", [128, NTOK])
    cosRk = din("cosRk", [128, NTOK]); sinRk = din("sinRk", [128, NTOK])
    cst = din("cst", [128, 10, 128])
    cvec = din("cvec", [128, 4])
    posrow = din("posrow", [128, NTOK]); poscol = din("poscol", [128, NTOK // 128])
    yout = nc.dram_tensor("yout", [NOWN, D], F32, kind="ExternalOutput").ap()

    WIN = dscr("WIN", [D, 4928]); WUQ = dscr("WUQ", [512, 1536]); WUK = dscr("WUK", [256, 1024])
    WUV = dscr("WUV", [256, 1024]); WOB = dscr("WOB", [D, D]); WQB = dscr("WQB", [D, D])
    SKB = dscr("SKB", [2048, 128]); UB = dscr("UB", [16384, D]); VB = dscr("VB", [16384, D])
    UT = dscr("UT", [D, 16384])
    MOD = dscr("MOD", [3, 6 * D], F32)
    QmT = dscr("QmT", [H, 192, NOWN]); KmT = dscr("KmT", [H, 192, NTOK]); Vm = dscr("Vm", [NTOK, 1024])
    RQT = dscr("RQT", [H, 128, NOWN]); RKT = dscr("RKT", [H, 128, NTOK]); RV = dscr("RV", [NTOK, 1024])
    RGT = dscr("RGT", [H, 128, NOWN])
    MIXT = dscr("MIXT", [D, NOWN])
    XMID = dscr("XMID", [NOWN, D], F32)
    H2T = dscr("H2T", [D, NOWN])
    SC = dscr("SC", [NOWN, 16, 128], F32)
    SCS = dscr("SCS", [16, NOWN, 128], F32)
    GT = dscr("GT", [NOWN // 128, 128, 128, 128])

    with ExitStack() as top:
        S = Sched(nc, top)

        uid = [0]

        def sbt(st, name, shape, dt=F32):
            uid[0] += 1
            return st.enter_context(nc.sbuf_tensor(f"{name}_{uid[0]}", list(shape), dt))

        def pst(st, name, shape, dt=F32):
            uid[0] += 1
            return st.enter_context(nc.psum_tensor(f"{name}_{uid[0]}", list(shape), dt))

        cstf = sbt(top, "cstf", [128, 10, 128]); b_cst = Buf()
        S.dma("sp", cstf[:], cst, writes=[b_cst])
        cv = sbt(top, "cv", [128, 4]); b_cv = Buf()
        S.dma("sp", cv[:], cvec, writes=[b_cv])
        identf = cstf[:, 0, :]
        cstb = sbt(top, "cstb", [128, 4, 128], BF16); b_cstb = Buf()
        S.op("dve", lambda e: e.tensor_copy(cstb[:], cstf[:, 0:4, :]), reads=[b_cst], writes=[b_cstb])
        identb, onesb, PMb, PRb = cstb[:, 0, :], cstb[:, 1, :], cstb[:, 2, :], cstb[:, 3, :]
        n1T = sbt(top, "n1T", [128, KC]); n2T = sbt(top, "n2T", [128, KC])
        qanT = sbt(top, "qanT", [128, 4]); kvanT = sbt(top, "kvanT", [128, 2])
        qnT = sbt(top, "qnT", [128, 2]); knT = sbt(top, "knT", [128, 2])
        modT = sbt(top, "modT", [128, 6, KC, 3])
        A1 = sbt(top, "A1", [128, KC, 3]); A2 = sbt(top, "A2", [128, KC, 3])
        b_vec = Buf()
        with nc.allow_non_contiguous_dma(reason="tiny one-time parameter loads"):
            S.dma("sp", n1T[:], norm1_w.rearrange("(c p) -> p c", p=128), writes=[b_vec])
            S.dma("sp", n2T[:], norm2_w.rearrange("(c p) -> p c", p=128), writes=[b_vec])
            S.dma("sp", qanT[:], q_a_norm.rearrange("(c p) -> p c", p=128), writes=[b_vec])
            S.dma("sp", kvanT[:], kv_a_norm.rearrange("(c p) -> p c", p=128), writes=[b_vec])
            S.dma("sp", qnT[:, 0:1], q_norm[0:128].rearrange("(p o) -> p o", o=1), writes=[b_vec])
            S.dma("sp", qnT[0:64, 1:2], q_norm[128:192].rearrange("(p o) -> p o", o=1), writes=[b_vec])
            S.dma("sp", knT[:, 0:1], k_norm[0:128].rearrange("(p o) -> p o", o=1), writes=[b_vec])
            S.dma("sp", knT[0:64, 1:2], k_norm[128:192].rearrange("(p o) -> p o", o=1), writes=[b_vec])

        b_w = Buf()

        def cast_rows(dst, src, rows, step):
            for r0 in range(0, rows, step):
                S.dma("pool", dst[r0:r0 + step, :], src[r0:r0 + step, :], writes=[b_w])

        cast_rows(WIN, w_in, D, 256)
        cast_rows(WUQ, w_uq, 512, 256); cast_rows(WUK, w_uk, 256, 256); cast_rows(WUV, w_uv, 256, 256)
        cast_rows(WOB, w_o, D, 512); cast_rows(WQB, peer_wq, D, 512); cast_rows(SKB, sub_keys, 2048, 512)
        cast_rows(UB, peer_u, 16384, 512); cast_rows(VB, peer_v, 16384, 512)

        with ExitStack() as st:
            cT = sbt(st, "cT", [128, KC, 3]); b_cT = Buf()
            with nc.allow_non_contiguous_dma(reason="tiny c transpose"):
                for b in range(3):
                    S.dma("sp", cT[:, :, b], c3[b].rearrange("(c p) -> p c", p=128), writes=[b_cT])
            S.op("act", lambda e: e.activation(cT[:], cT[:], AF.Silu), reads=[b_cT], writes=[b_cT])
            bada = sbt(st, "bada", [3, 6 * D]); b_bada = Buf()
            S.dma("sp", bada[:], bass.AP(b_ada.tensor, 0, [[0, 3], [1, 6 * D]]), writes=[b_bada])
            wa = [sbt(st, f"wa{i}", [128, KC, 512]) for i in range(2)]
            b_wa = [Buf(), Buf()]
            psm = [pst(st, f"psm{i}", [128, 512]) for i in range(2)]
            b_psm = [Buf(), Buf()]
            modrow = sbt(st, "modrow", [3, 6 * D]); b_modrow = Buf()
            wav = w_ada.rearrange("(c p) n -> p c n", p=128)
            for blk in range(24):
                i = blk % 2
                S.dma("sp", wa[i][:], wav[:, :, blk * 512:(blk + 1) * 512], writes=[b_wa[i]])

                def mm(e, i=i):
                    for kc in range(KC):
                        ins = e.matmul(psm[i][0:3, :], cT[:, kc, :], wa[i][:, kc, :], start=(kc == 0), stop=(kc == KC - 1))
                    return ins
                S.op("pe", mm, reads=[b_cT, b_wa[i]], writes=[b_psm[i]])
                S.op("dve", lambda e, i=i, blk=blk: e.tensor_tensor(modrow[:, blk * 512:(blk + 1) * 512], psm[i][0:3, :],
                                                                    bada[:, blk * 512:(blk + 1) * 512], ALU.add),
                     reads=[b_psm[i], b_bada], writes=[b_modrow])
            b_MOD = Buf()
            S.dma("sp", MOD, modrow[:], reads=[b_modrow], writes=[b_MOD])
            b_modT = Buf()
            with nc.allow_non_contiguous_dma(reason="one-time mod transpose"):
                for g in range(6):
                    for b in range(3):
                        S.dma("sp", modT[:, g, :, b], MOD[b, g * D:(g + 1) * D].rearrange("(c p) -> p c", p=128),
                              reads=[b_MOD], writes=[b_modT])
            b_A = Buf()
            for (A, nT, g) in ((A1, n1T, 1), (A2, n2T, 4)):
                S.op("dve", lambda e, A=A, g=g: e.tensor_scalar(A[:], modT[:, g, :, :], 1.0, None, ALU.add),
                     reads=[b_modT], writes=[b_A])
                S.op("dve", lambda e, A=A, nT=nT: e.tensor_tensor(A[:], A[:], apx(nT[:, 0:1], [[1, KC], [0, 3]]), ALU.mult),
                     reads=[b_A, b_vec], writes=[b_A])
            S.barrier()
        B1 = modT[:, 0, :, :]
        B2 = modT[:, 3, :, :]

        def norm_transpose(st_, xt, b_xt, A, B, seg, hT, b_hT, col0, res):
            junk, b_junk, ss, b_ss, xn, b_xn, psb, b_psb = res
            S.op("act", lambda e: e.activation(junk[:], xt[:], AF.Square, accum_out=ss[:, 0:1]),
                 reads=[b_xt], writes=[b_junk, b_ss])
            S.op("dve", lambda e: e.tensor_scalar(ss[:, 1:2], ss[:, 0:1], 1.0 / D, EPS, ALU.mult, ALU.add),
                 reads=[b_ss], writes=[b_ss])
            S.op("act", lambda e: e.sqrt(ss[:, 1:2], ss[:, 1:2]), reads=[b_ss], writes=[b_ss])
            S.op("dve", lambda e: e.reciprocal(ss[:, 2:3], ss[:, 1:2]), reads=[b_ss], writes=[b_ss])
            S.op("dve", lambda e: e.tensor_scalar(xn[:], xt[:], ss[:, 2:3], None, ALU.mult),
                 reads=[b_xt, b_ss], writes=[b_xn])
            for q4 in range(4):
                pi = q4 % 2

                def tr(e, q4=q4, pi=pi):
                    for j in range(4):
                        kc = q4 * 4 + j
                        ins = e.transpose(psb[pi][:, j * 128:(j + 1) * 128], xn[:, kc * 128:(kc + 1) * 128], identb)
                    return ins
                S.op("pe", tr, reads=[b_xn, b_cstb], writes=[b_psb[pi]])
                for j in range(4):
                    kc = q4 * 4 + j
                    S.op("act", lambda e, kc=kc, j=j, pi=pi: e.activation(
                        hT[:, kc, col0:col0 + 128], psb[pi][:, j * 128:(j + 1) * 128], AF.Identity,
                        bias=B[:, kc, seg:seg + 1], scale=A[:, kc, seg:seg + 1]),
                        reads=[b_psb[pi], b_A, b_modT], writes=[b_hT])

        def nt_resources(st_):
            junk = sbt(st_, "nt_junk", [128, D], BF16); ss = sbt(st_, "nt_ss", [128, 4])
            xn = sbt(st_, "nt_xn", [128, D], BF16)
            psb = [pst(st_, f"nt_psb{i}", [128, 1024], BF16) for i in range(2)]
            return (junk, Buf(), ss, Buf(), xn, Buf(), psb, [Buf(), Buf()])

        def fm_rstd(chunks, b_in, nfeat, sqt, b_sq, ps, b_ps, rst, b_rst):
            for ci, (ap, rows) in enumerate(chunks):
                S.op("act", lambda e, ap=ap, rows=rows, ci=ci: e.activation(sqt[0:rows, ci, :], ap, AF.Square),
                     reads=b_in, writes=[b_sq])

            def mm(e):
                for ci, (ap, rows) in enumerate(chunks):
                    ins = e.matmul(ps[:], onesb[0:rows, :], sqt[0:rows, ci, :], start=(ci == 0), stop=(ci == len(chunks) - 1))
                return ins
            S.op("pe", mm, reads=[b_sq, b_cstb], writes=[b_ps])
            S.op("dve", lambda e: e.tensor_scalar(rst[:], ps[:], 1.0 / nfeat, EPS, ALU.mult, ALU.add), reads=[b_ps], writes=[b_rst])
            S.op("act", lambda e: e.sqrt(rst[:], rst[:]), reads=[b_rst], writes=[b_rst])
            S.op("dve", lambda e: e.reciprocal(rst[:], rst[:]), reads=[b_rst], writes=[b_rst])

        b_QmT, b_KmT, b_Vm, b_RQT, b_RKT, b_RV, b_RG = (Buf() for _ in range(7))
        with ExitStack() as st:
            ntres = nt_resources(st)
            xt = [sbt(st, f"xt{i}", [128, D]) for i in range(2)]; b_xt = [Buf(), Buf()]
            hT = sbt(st, "hT", [128, KC, T], BF16); b_hT = Buf()
            wt = [sbt(st, f"wt{i}", [128, KC, 512], BF16) for i in range(2)]; b_wt = [Buf(), Buf()]
            wuq = sbt(st, "wuq", [128, 4, 1536], BF16); wuk = sbt(st, "wuk", [128, 2, 1024], BF16)
            wuv = sbt(st, "wuv", [128, 2, 1024], BF16); b_wu = Buf()
            S.dma("sp", wuq[:], WUQ.rearrange("(c p) n -> p c n", p=128), reads=[b_w], writes=[b_wu])
            S.dma("sp", wuk[:], WUK.rearrange("(c p) n -> p c n", p=128), reads=[b_w], writes=[b_wu])
            S.dma("sp", wuv[:], WUV.rearrange("(c p) n -> p c n", p=128), reads=[b_w], writes=[b_wu])
            tab = sbt(st, "tab", [128, 6, T]); b_tab = Buf()
            ps = [pst(st, f"p1ps{i}", [128, 512]) for i in range(4)]; b_ps = [Buf() for _ in range(4)]
            cqf = sbt(st, "cqf", [128, 4, T]); b_cqf = Buf()
            cqn = sbt(st, "cqn", [128, 4, T], BF16); b_cqn = Buf()
            ckf = sbt(st, "ckf", [128, 2, T]); b_ckf = Buf()
            ckn = sbt(st, "ckn", [128, 2, T], BF16); b_ckn = Buf()
            krf = sbt(st, "krf", [64, T]); b_krf = Buf()
            sqt = sbt(st, "sqt", [128, 4, T], BF16); b_sq = Buf()
            sqkr = sbt(st, "sqkr", [64, T], BF16); b_sqkr = Buf()
            rst = sbt(st, "rst", [128, T]); b_rst = Buf()
            hf = [sbt(st, f"hf{i}", [128, T]) for i in range(2)]; b_hf = [Buf(), Buf()]
            hr = [sbt(st, f"hr{i}", [64, T]) for i in range(2)]; b_hr = [Buf(), Buf()]
            hb = [sbt(st, f"hb{i}", [128, T], BF16) for i in range(2)]; b_hb = [Buf(), Buf()]
            hrb = [sbt(st, f"hrb{i}", [64, T], BF16) for i in range(2)]; b_hrb = [Buf(), Buf()]
            t1 = [sbt(st, f"t1_{i}", [128, T]) for i in range(2)]; b_t1 = [Buf(), Buf()]
            t2 = [sbt(st, f"t2_{i}", [128, T]) for i in range(2)]; b_t2 = [Buf(), Buf()]
            ob = [sbt(st, f"ob{i}", [128, T], BF16) for i in range(3)]; b_ob = [Buf() for _ in range(3)]
            vt = [sbt(st, f"vt{i}", [128, 1024], BF16) for i in range(2)]; b_vt = [Buf(), Buf()]
            cnt = {"ps": 0, "w": 0, "h": 0, "o": 0, "v": 0, "x": 0}

            def nxt(key, n):
                i = cnt[key] % n
                cnt[key] += 1
                return i

            WINv = WIN.rearrange("(c p) n -> p c n", p=128)

            def load_w(c0, w):
                i = nxt("w", 2)
                S.dma("sp", wt[i][:, :, 0:w], WINv[:, :, c0:c0 + w], reads=[b_w], writes=[b_wt[i]])
                return i

            def rope_store(src_f, b_src, rows, ctab, stab, perm, dst, b_dst, scale_ap=None, rstd=None):
                hi = nxt("h", 2)
                xb = hb[hi] if rows == 128 else hrb[hi]
                b_xb = b_hb[hi] if rows == 128 else b_hrb[hi]
                if rstd is not None:
                    S.op("dve", lambda e: e.scalar_tensor_tensor(xb[0:rows, :], src_f, scale_ap, rstd[0:rows, :], ALU.mult, ALU.mult),
                         reads=b_src + [b_rst, b_vec], writes=[b_xb])
                else:
                    S.op("act", lambda e: e.copy(xb[0:rows, :], src_f), reads=b_src, writes=[b_xb])
                pi = nxt("ps", 4)
                S.op("pe", lambda e: e.matmul(ps[pi][0:rows, :], perm[0:rows, 0:rows], xb[0:rows, :], start=True, stop=True),
                     reads=[b_xb, b_cstb], writes=[b_ps[pi]])
                S.op("dve", lambda e: e.tensor_tensor(t1[hi][0:rows, :], xb[0:rows, :], ctab[0:rows, :], ALU.mult),
                     reads=[b_xb, b_tab], writes=[b_t1[hi]])
                S.op("dve", lambda e: e.tensor_tensor(t2[hi][0:rows, :], ps[pi][0:rows, :], stab[0:rows, :], ALU.mult),
                     reads=[b_ps[pi], b_tab], writes=[b_t2[hi]])
                oi = nxt("o", 3)
                S.op("pool", lambda e: e.tensor_tensor(ob[oi][0:rows, :], t1[hi][0:rows, :], t2[hi][0:rows, :], ALU.add),
                     reads=[b_t1[hi], b_t2[hi]], writes=[b_ob[oi]])
                S.dma("sp", dst, ob[oi][0:rows, :], reads=[b_ob[oi]], writes=[b_dst])

            def norm_store(src_f, b_src, scale_ap, rstd, dst, b_dst):
                oi = nxt("o", 3)
                S.op("dve", lambda e: e.scalar_tensor_tensor(ob[oi][:], src_f, scale_ap, rstd[:], ALU.mult, ALU.mult),
                     reads=b_src + [b_rst, b_vec], writes=[b_ob[oi]])
                S.dma("sp", dst, ob[oi][:], reads=[b_ob[oi]], writes=[b_dst])

            def proj_fm(wi, c_in_blk, M, pi, rows0=0):
                def mm(e):
                    for kc in range(KC):
                        ins = e.matmul(ps[pi][rows0:rows0 + M, :], wt[wi][:, kc, c_in_blk:c_in_blk + M], hT[:, kc, :],
                                       start=(kc == 0), stop=(kc == KC - 1))
                    return ins
                S.op("pe", mm, reads=[b_wt[wi], b_hT], writes=[b_ps[pi]])

            tiles = [(t0, si, True) for si, (s0, sl) in enumerate(SEGS) for t0 in range(s0, s0 + sl, T)]
            tiles += [(t0, 0, False) for t0 in range(NOWN, NTOK, T)]
            for (t0, seg, own) in tiles:
                for s in range(4):
                    xi = nxt("x", 2)
                    S.dma("sp", xt[xi][:], xin[t0 + s * 128:t0 + (s + 1) * 128, :], writes=[b_xt[xi]])
                    norm_transpose(st, xt[xi], b_xt[xi], A1, B1, seg, hT, b_hT, s * 128, ntres)
                for ti, tb in enumerate((cosM, sinM, cosR, sinR, cosRk, sinRk)):
                    S.dma("sp", tab[:, ti, :], tb[:, t0:t0 + T], writes=[b_tab])
                if own:
                    wi = load_w(0, 512)
                    for j in range(4):
                        pi = nxt("ps", 4)
                        proj_fm(wi, j * 128, 128, pi)
                        S.op("act", lambda e, j=j, pi=pi: e.copy(cqf[:, j, :], ps[pi][:]), reads=[b_ps[pi]], writes=[b_cqf])
                    pi = nxt("ps", 4)
                    fm_rstd([(cqf[:, j, :], 128) for j in range(4)], [b_cqf], 512, sqt, b_sq, ps[pi], b_ps[pi], rst, b_rst)
                    for j in range(4):
                        S.op("dve", lambda e, j=j: e.scalar_tensor_tensor(cqn[:, j, :], cqf[:, j, :], qanT[:, j:j + 1], rst[:], ALU.mult, ALU.mult),
                             reads=[b_cqf, b_rst, b_vec], writes=[b_cqn])
                    for h in range(H):
                        hi = nxt("h", 2)
                        pn = nxt("ps", 4)

                        def mmq(e, h=h, pn=pn):
                            for k4 in range(4):
                                ins = e.matmul(ps[pn][:], wuq[:, k4, h * 192:h * 192 + 128], cqn[:, k4, :], start=(k4 == 0), stop=(k4 == 3))
                            return ins
                        S.op("pe", mmq, reads=[b_wu, b_cqn], writes=[b_ps[pn]])
                        S.op("act", lambda e, hi=hi, pn=pn: e.copy(hf[hi][:], ps[pn][:]), reads=[b_ps[pn]], writes=[b_hf[hi]])
                        pr = nxt("ps", 4)

                        def mmr(e, h=h, pr=pr):
                            for k4 in range(4):
                                ins = e.matmul(ps[pr][0:64, :], wuq[:, k4, h * 192 + 128:h * 192 + 192], cqn[:, k4, :], start=(k4 == 0), stop=(k4 == 3))
                            return ins
                        S.op("pe", mmr, reads=[b_wu, b_cqn], writes=[b_ps[pr]])
                        S.op("act", lambda e, hi=hi, pr=pr: e.copy(hr[hi][:], ps[pr][0:64, :]), reads=[b_ps[pr]], writes=[b_hr[hi]])
                        pq = nxt("ps", 4)
                        fm_rstd([(hf[hi][:], 128), (hr[hi][:], 64)], [b_hf[hi], b_hr[hi]], 192, sqt, b_sq, ps[pq], b_ps[pq], rst, b_rst)
                        norm_store(hf[hi][:], [b_hf[hi]], qnT[:, 0:1], rst, QmT[h, 0:128, t0:t0 + T], b_QmT)
                        rope_store(hr[hi][:], [b_hr[hi]], 64, tab[:, 0, :], tab[:, 1, :], PMb, QmT[h, 128:192, t0:t0 + T], b_QmT,
                                   scale_ap=qnT[0:64, 1:2], rstd=rst)
                wi = load_w(512, 320)
                for j in range(2):
                    pi = nxt("ps", 4)
                    proj_fm(wi, j * 128, 128, pi)
                    S.op("act", lambda e, j=j, pi=pi: e.copy(ckf[:, j, :], ps[pi][:]), reads=[b_ps[pi]], writes=[b_ckf])
                pi = nxt("ps", 4)
                proj_fm(wi, 256, 64, pi)
                S.op("act", lambda e, pi=pi: e.copy(krf[:], ps[pi][0:64, :]), reads=[b_ps[pi]], writes=[b_krf])
                S.op("act", lambda e: e.activation(sqkr[:], krf[:], AF.Square), reads=[b_krf], writes=[b_sqkr])
                pi = nxt("ps", 4)
                fm_rstd([(ckf[:, j, :], 128) for j in range(2)], [b_ckf], 256, sqt, b_sq, ps[pi], b_ps[pi], rst, b_rst)
                for j in range(2):
                    S.op("dve", lambda e, j=j: e.scalar_tensor_tensor(ckn[:, j, :], ckf[:, j, :], kvanT[:, j:j + 1], rst[:], ALU.mult, ALU.mult),
                         reads=[b_ckf, b_rst, b_vec], writes=[b_ckn])
                for h in range(H):
                    hi = nxt("h", 2)
                    pn = nxt("ps", 4)

                    def mmk(e, h=h, pn=pn):
                        for k2 in range(2):
                            ins = e.matmul(ps[pn][:], wuk[:, k2, h * 128:(h + 1) * 128], ckn[:, k2, :], start=(k2 == 0), stop=(k2 == 1))
                        return ins
                    S.op("pe", mmk, reads=[b_wu, b_ckn], writes=[b_ps[pn]])
                    S.op("act", lambda e, hi=hi, pn=pn: e.copy(hf[hi][:], ps[pn][:]), reads=[b_ps[pn]], writes=[b_hf[hi]])
                    S.op("act", lambda e, hi=hi: e.activation(sqt[:, 0, :], hf[hi][:], AF.Square), reads=[b_hf[hi]], writes=[b_sq])
                    pq = nxt("ps", 4)

                    def mms(e, pq=pq):
                        e.matmul(ps[pq][:], onesb, sqt[:, 0, :], start=True, stop=False)
                        return e.matmul(ps[pq][:], onesb[0:64, :], sqkr[:], start=False, stop=True)
                    S.op("pe", mms, reads=[b_sq, b_sqkr, b_cstb], writes=[b_ps[pq]])
                    S.op("dve", lambda e, pq=pq: e.tensor_scalar(rst[:], ps[pq][:], 1.0 / 192, EPS, ALU.mult, ALU.add), reads=[b_ps[pq]], writes=[b_rst])
                    S.op("act", lambda e: e.sqrt(rst[:], rst[:]), reads=[b_rst], writes=[b_rst])
                    S.op("dve", lambda e: e.reciprocal(rst[:], rst[:]), reads=[b_rst], writes=[b_rst])
                    norm_store(hf[hi][:], [b_hf[hi]], knT[:, 0:1], rst, KmT[h, 0:128, t0:t0 + T], b_KmT)
                    rope_store(krf[:], [b_krf], 64, tab[:, 0, :], tab[:, 1, :], PMb, KmT[h, 128:192, t0:t0 + T], b_KmT,
                               scale_ap=knT[0:64, 1:2], rstd=rst)
                for s in range(4):
                    vi = nxt("v", 2)
                    for cb in range(2):
                        pi = nxt("ps", 4)

                        def mmv(e, s=s, cb=cb, pi=pi):
                            for k2 in range(2):
                                ins = e.matmul(ps[pi][:], ckn[:, k2, s * 128:(s + 1) * 128], wuv[:, k2, cb * 512:(cb + 1) * 512], start=(k2 == 0), stop=(k2 == 1))
                            return ins
                        S.op("pe", mmv, reads=[b_wu, b_ckn], writes=[b_ps[pi]])
                        S.op("act", lambda e, vi=vi, cb=cb, pi=pi: e.copy(vt[vi][:, cb * 512:(cb + 1) * 512], ps[pi][:]), reads=[b_ps[pi]], writes=[b_vt[vi]])
                    S.dma("sp", Vm[t0 + s * 128:t0 + (s + 1) * 128, :], vt[vi][:], reads=[b_vt[vi]], writes=[b_Vm])
                for which in ((0, 1) if own else (1,)):
                    base = 832 if which == 0 else 1856
                    for blk in range(2):
                        wi = load_w(base + blk * 512, 512)
                        for j in range(4):
                            h = blk * 4 + j
                            pi = nxt("ps", 4)
                            proj_fm(wi, j * 128, 128, pi)
                            if which == 0:
                                rope_store(ps[pi][:], [b_ps[pi]], 128, tab[:, 2, :], tab[:, 3, :], PRb, RQT[h, :, t0:t0 + T], b_RQT)
                            else:
                                rope_store(ps[pi][:], [b_ps[pi]], 128, tab[:, 4, :], tab[:, 5, :], PRb, RKT[h, :, t0:t0 + T], b_RKT)
                wis = [load_w(2880, 512), load_w(2880 + 512, 512)]
                for s in range(4):
                    vi = nxt("v", 2)
                    for cb in range(2):
                        pi = nxt("ps", 4)

                        def mmt(e, s=s, cb=cb, pi=pi):
                            for kc in range(KC):
                                ins = e.matmul(ps[pi][:], hT[:, kc, s * 128:(s + 1) * 128], wt[wis[cb]][:, kc, :], start=(kc == 0), stop=(kc == KC - 1))
                            return ins
                        S.op("pe", mmt, reads=[b_hT, b_wt[wis[cb]]], writes=[b_ps[pi]])
                        S.op("act", lambda e, vi=vi, cb=cb, pi=pi: e.copy(vt[vi][:, cb * 512:(cb + 1) * 512], ps[pi][:]),
                             reads=[b_ps[pi]], writes=[b_vt[vi]])
                    S.dma("sp", RV[t0 + s * 128:t0 + (s + 1) * 128, :], vt[vi][:], reads=[b_vt[vi]], writes=[b_RV])
                if own:
                    for blk in range(2):
                        wi = load_w(3904 + blk * 512, 512)
                        for j in range(4):
                            h = blk * 4 + j
                            pi = nxt("ps", 4)
                            proj_fm(wi, j * 128, 128, pi)
                            oi = nxt("o", 3)
                            S.op("act", lambda e, oi=oi, pi=pi: e.activation(ob[oi][:], ps[pi][:], AF.Silu), reads=[b_ps[pi]], writes=[b_ob[oi]])
                            S.dma("sp", RGT[h, :, t0:t0 + T], ob[oi][:], reads=[b_ob[oi]], writes=[b_RG])
            S.barrier()

        if upto <= 1:
            with ExitStack() as st:
                dbg = sbt(st, "dbg", [128, D])
                bd = Buf()
                for r0 in range(0, NOWN, 128):
                    S.dma("sp", dbg[:], xin[r0:r0 + 128, :], writes=[bd])
                    S.dma("sp", yout[r0:r0 + 128, :], dbg[:], reads=[bd])
                S.barrier()
            return nc

        b_MIXT = Buf()
        CTXM = 2 * SA if 2 * SA > SS else SS
        SQM = max(SA, SS)
        with ExitStack() as st:
            ps_s = [pst(st, f"a_s{i}", [128, 512]) for i in range(2)]; b_ps_s = [Buf(), Buf()]
            ps_o = pst(st, "a_o", [128, 512]); b_ps_o = Buf()
            ps_z = pst(st, "a_z", [128, 512]); b_ps_z = Buf()
            kn = [sbt(st, f"a_kn{i}", [128, CTXM], BF16) for i in range(2)]
            kr = [sbt(st, f"a_kr{i}", [64, CTXM], BF16) for i in range(2)]
            vv = [sbt(st, f"a_vv{i}", [128, CTXM // 128, 128], BF16) for i in range(2)]
            qn = [sbt(st, f"a_qn{i}", [128, SQM], BF16) for i in range(2)]
            qr = [sbt(st, f"a_qr{i}", [64, SQM], BF16) for i in range(2)]
            b_ld = [Buf(), Buf()]
            pt = [sbt(st, f"a_pt{i}", [128, 512], BF16) for i in range(3)]; b_pt = [Buf() for _ in range(3)]
            rs = sbt(st, "a_rs", [128, 512]); b_rs = Buf()
            on = [sbt(st, f"a_on{i}", [128, 512], BF16) for i in range(2)]; b_on = [Buf(), Buf()]
            it = 0; lc = 0; oc = 0
            sc_att = float(192 ** -0.5)
            for si, (s0, sl) in enumerate(SEGS):
                ranges = [(s0, sl)] + ([(NOWN, SA)] if si == 0 else [])
                ctx = sum(l for _, l in ranges)
                for h in range(H):
                    i = lc % 2; lc += 1
                    off = 0
                    for (c0, cl) in ranges:
                        S.dma("sp", kn[i][:, off:off + cl], KmT[h, 0:128, c0:c0 + cl], reads=[b_KmT], writes=[b_ld[i]])
                        S.dma("sp", kr[i][:, off:off + cl], KmT[h, 128:192, c0:c0 + cl], reads=[b_KmT], writes=[b_ld[i]])
                        S.dma("sp", vv[i][:, off // 128:(off + cl) // 128, :],
                              Vm[c0:c0 + cl, h * 128:(h + 1) * 128].rearrange("(c p) d -> p c d", p=128), reads=[b_Vm], writes=[b_ld[i]])
                        off += cl
                    S.dma("sp", qn[i][:, 0:sl], QmT[h, 0:128, s0:s0 + sl], reads=[b_QmT], writes=[b_ld[i]])
                    S.dma("sp", qr[i][:, 0:sl], QmT[h, 128:192, s0:s0 + sl], reads=[b_QmT], writes=[b_ld[i]])
                    for qt in range(sl // 512):
                        nk = ctx // 128
                        slots = {}

                        def e_mm1(kc, i=i, qt=qt):
                            nonlocal it
                            sidx = it % 2; pi = it % 3; it += 1
                            slots[kc] = (sidx, pi)

                            def mm1(e):
                                e.matmul(ps_s[sidx][:], kn[i][:, kc * 128:(kc + 1) * 128], qn[i][:, qt * 512:(qt + 1) * 512], start=True, stop=False)
                                return e.matmul(ps_s[sidx][:], kr[i][:, kc * 128:(kc + 1) * 128], qr[i][:, qt * 512:(qt + 1) * 512], start=False, stop=True)
                            S.op("pe", mm1, reads=[b_ld[i]], writes=[b_ps_s[sidx]])

                        def e_rest(kc, i=i, nk=nk):
                            sidx, pi = slots[kc]
                            S.op("act", lambda e: e.activation(pt[pi][:], ps_s[sidx][:], AF.Exp, scale=sc_att),
                                 reads=[b_ps_s[sidx]], writes=[b_pt[pi]])

                            def mm2(e):
                                e.matmul(ps_o[:], vv[i][:, kc, :], pt[pi][:], start=(kc == 0), stop=(kc == nk - 1))
                                return e.matmul(ps_z[:], onesb, pt[pi][:], start=(kc == 0), stop=(kc == nk - 1))
                            S.op("pe", mm2, reads=[b_ld[i], b_pt[pi], b_cstb], writes=[b_ps_o, b_ps_z])
                        e_mm1(0)
                        for kc in range(nk):
                            if kc + 1 < nk:
                                e_mm1(kc + 1)
                            e_rest(kc)
                        oi = oc % 2; oc += 1
                        S.op("dve", lambda e: e.reciprocal(rs[:], ps_z[:]), reads=[b_ps_z], writes=[b_rs])
                        S.op("dve", lambda e, oi=oi: e.tensor_tensor(on[oi][:], ps_o[:], rs[:], ALU.mult), reads=[b_ps_o, b_rs], writes=[b_on[oi]])
                        S.dma("sp", MIXT[h * 128:(h + 1) * 128, s0 + qt * 512:s0 + (qt + 1) * 512], on[oi][:], reads=[b_on[oi]], writes=[b_MIXT])
            S.barrier()
        if upto <= 2:
            return nc

        with ExitStack() as st:
            lg = sbt(st, "r_lg", [128, 16]); b_lg = Buf()
            nlgb = sbt(st, "r_nlgb", [128, 8])
            gnT = sbt(st, "r_gnT", [128, 8])
            pcol = sbt(st, "r_pcol", [128, NTOK // 128])
            S.dma("sp", lg[:], bass.AP(rdl.tensor, 0, [[0, 128], [1, 16]]), writes=[b_lg])
            with nc.allow_non_contiguous_dma(reason="tiny"):
                S.dma("sp", gnT[:], gn_w.rearrange("(h p) -> p h", p=128), writes=[b_lg])
            S.dma("sp", pcol[:], poscol, writes=[b_lg])
            S.op("act", lambda e: e.activation(lg[:], lg[:], AF.Sigmoid), reads=[b_lg], writes=[b_lg])
            S.op("act", lambda e: e.activation(lg[:], lg[:], AF.Ln), reads=[b_lg], writes=[b_lg])
            S.op("dve", lambda e: e.tensor_scalar(nlgb[:], lg[:, 8:16], -1.0, None, ALU.mult), reads=[b_lg], writes=[b_lg])
            ps_s = [pst(st, f"r_s{i}", [128, 512]) for i in range(2)]; b_ps_s = [Buf(), Buf()]
            ps_o = [pst(st, f"r_o{i}", [128, 512]) for i in range(4)]; b_ps_o = [Buf() for _ in range(4)]
            ps_m = [pst(st, f"r_m{i}", [128, 512]) for i in range(2)]; b_ps_m = [Buf(), Buf()]
            qq = [sbt(st, f"r_q{i}", [128, 4, 512], BF16) for i in range(2)]; gg = [sbt(st, f"r_g{i}", [128, 4, 512], BF16) for i in range(2)]
            prow = [sbt(st, f"r_pr{i}", [128, 512]) for i in range(2)]; b_q = [Buf(), Buf()]
            kk = [sbt(st, f"r_k{i}", [128, 4, 128], BF16) for i in range(3)]; vv = [sbt(st, f"r_v{i}", [128, 512], BF16) for i in range(3)]
            b_k = [Buf() for _ in range(3)]
            dd = [sbt(st, f"r_d{i}", [128, 512]) for i in range(2)]; dp = [sbt(st, f"r_dp{i}", [128, 512]) for i in range(2)]
            dn = [sbt(st, f"r_dn{i}", [128, 512]) for i in range(2)]; b_d = [Buf(), Buf()]
            a1 = [sbt(st, f"r_a{i}", [128, 512]) for i in range(2)]; b_a1 = [Buf(), Buf()]
            dm = [sbt(st, f"r_dm{i}", [128, 512]) for i in range(2)]; b_dm = [Buf(), Buf()]
            pt = [sbt(st, f"r_pt{i}", [128, 512], BF16) for i in range(3)]; b_pt = [Buf() for _ in range(3)]
            of = sbt(st, "r_of", [128, 512]); obf = sbt(st, "r_obf", [128, 512], BF16); sq = sbt(st, "r_sq", [128, 512], BF16); b_of = Buf()
            mu = sbt(st, "r_mu", [128, 512]); var = sbt(st, "r_var", [128, 512]); b_mu = Buf()
            ro = [sbt(st, f"r_ro{i}", [128, 512], BF16) for i in range(2)]; b_ro = [Buf(), Buf()]
            qc = 0; kcn = 0; it = 0; rc = 0
            for si, (s0, sl) in enumerate(SEGS):
                ranges = [(s0, sl)] + ([(NOWN, SA)] if si == 0 else [])
                chunks = [c0 + j * 128 for (c0, cl) in ranges for j in range(cl // 128)]
                for hg in range(2):
                    for qt in range(sl // 512):
                        q0 = s0 + qt * 512
                        qi = qc % 2; qc += 1
                        S.dma("sp", qq[qi][:], RQT[hg * 4:(hg + 1) * 4, :, q0:q0 + 512].rearrange("h p t -> p h t"), reads=[b_RQT], writes=[b_q[qi]])
                        S.dma("sp", gg[qi][:], RGT[hg * 4:(hg + 1) * 4, :, q0:q0 + 512].rearrange("h p t -> p h t"), reads=[b_RG], writes=[b_q[qi]])
                        S.dma("sp", prow[qi][:], posrow[:, q0:q0 + 512], writes=[b_q[qi]])
                        seq = [(ci, k0, hh) for ci, k0 in enumerate(chunks) for hh in range(4)]
                        cslot = {}
                        bslot = {}

                        def e_score(n, qi=qi, hg=hg):
                            nonlocal kcn, it
                            ci, k0, hh = seq[n]
                            if hh == 0:
                                ki = kcn % 3; di = kcn % 2; kcn += 1
                                cslot[ci] = (ki, di)
                                S.dma("sp", kk[ki][:], RKT[hg * 4:(hg + 1) * 4, :, k0:k0 + 128].rearrange("h p t -> p h t"), reads=[b_RKT], writes=[b_k[ki]])
                                S.dma("sp", vv[ki][:], RV[k0:k0 + 128, hg * 512:(hg + 1) * 512], reads=[b_RV], writes=[b_k[ki]])
                                S.op("dve", lambda e: e.tensor_scalar(dp[di][:], prow[qi][:], pcol[:, k0 // 128:k0 // 128 + 1], 0.0, ALU.subtract, ALU.max),
                                     reads=[b_q[qi], b_lg], writes=[b_d[di]])
                                S.op("dve", lambda e: e.tensor_scalar(dn[di][:], prow[qi][:], pcol[:, k0 // 128:k0 // 128 + 1], 0.0, ALU.subtract, ALU.min),
                                     reads=[b_q[qi], b_lg], writes=[b_d[di]])
                            ki, di = cslot[ci]
                            sidx = it % 2; pi = it % 3; it += 1
                            bslot[n] = (sidx, pi)
                            S.op("pe", lambda e: e.matmul(ps_s[sidx][:], kk[ki][:, hh, :], qq[qi][:, hh, :], start=True, stop=True),
                                 reads=[b_k[ki], b_q[qi]], writes=[b_ps_s[sidx]])

                        def e_rest(n, hg=hg, nch=len(chunks), q0=q0, s0=s0, sl=sl):
                            ci, k0, hh = seq[n]
                            h = hg * 4 + hh
                            ki, di = cslot[ci]
                            sidx, pi = bslot[n]
                            own_chunk = (s0 <= k0 < s0 + sl)
                            only_f = own_chunk and (k0 + 128 <= q0)
                            only_b = own_chunk and (k0 >= q0 + 512)
                            if only_f or only_b:
                                if only_f:
                                    S.op("act", lambda e: e.activation(a1[sidx][:], dp[di][:], AF.Exp, scale=lg[:, h:h + 1]),
                                         reads=[b_d[di], b_lg], writes=[b_a1[sidx]])
                                else:
                                    S.op("act", lambda e: e.activation(a1[sidx][:], dn[di][:], AF.Exp, scale=nlgb[:, h:h + 1]),
                                         reads=[b_d[di], b_lg], writes=[b_a1[sidx]])
                                S.op("dve", lambda e: e.tensor_tensor(pt[pi][:], a1[sidx][:], ps_s[sidx][:], ALU.mult),
                                     reads=[b_ps_s[sidx], b_a1[sidx]], writes=[b_pt[pi]])
                            else:
                                S.op("act", lambda e: e.activation(a1[sidx][:], dp[di][:], AF.Exp, scale=lg[:, h:h + 1]),
                                     reads=[b_d[di], b_lg], writes=[b_a1[sidx]])
                                S.op("act", lambda e: e.activation(dm[sidx][:], dn[di][:], AF.Exp, scale=nlgb[:, h:h + 1]),
                                     reads=[b_d[di], b_lg], writes=[b_dm[sidx]])
                                S.op("dve", lambda e: e.tensor_tensor(a1[sidx][:], a1[sidx][:], ps_s[sidx][:], ALU.mult),
                                     reads=[b_ps_s[sidx], b_a1[sidx]], writes=[b_a1[sidx]])
                                S.op("dve", lambda e: e.tensor_tensor(pt[pi][:], a1[sidx][:], dm[sidx][:], ALU.mult),
                                     reads=[b_a1[sidx], b_dm[sidx]], writes=[b_pt[pi]])
                            S.op("pe", lambda e: e.matmul(ps_o[hh][:], vv[ki][:, hh * 128:(hh + 1) * 128], pt[pi][:], start=(ci == 0), stop=(ci == nch - 1)),
                                 reads=[b_k[ki], b_pt[pi]], writes=[b_ps_o[hh]])
                        e_score(0)
                        for n in range(len(seq)):
                            if n + 1 < len(seq):
                                e_score(n + 1)
                            e_rest(n)
                        for hh in range(4):
                            h = hg * 4 + hh
                            S.op("act", lambda e, hh=hh: e.copy(of[:], ps_o[hh][:]), reads=[b_ps_o[hh]], writes=[b_of])
                            S.op("act", lambda e, hh=hh: e.activation(sq[:], ps_o[hh][:], AF.Square), reads=[b_ps_o[hh]], writes=[b_of])
                            S.op("dve", lambda e: e.tensor_copy(obf[:], of[:]), reads=[b_of], writes=[b_of])
                            S.op("pe", lambda e: e.matmul(ps_m[0][:], onesb, obf[:], start=True, stop=True), reads=[b_of, b_cstb], writes=[b_ps_m[0]])
                            S.op("pe", lambda e: e.matmul(ps_m[1][:], onesb, sq[:], start=True, stop=True), reads=[b_of, b_cstb], writes=[b_ps_m[1]])
                            S.op("dve", lambda e: e.tensor_scalar(mu[:], ps_m[0][:], 1.0 / 128, None, ALU.mult), reads=[b_ps_m[0]], writes=[b_mu])
                            S.op("dve", lambda e: e.tensor_tensor(var[:], mu[:], mu[:], ALU.mult), reads=[b_mu], writes=[b_mu])
                            S.op("dve", lambda e: e.scalar_tensor_tensor(var[:], ps_m[1][:], 1.0 / 128, var[:], ALU.mult, ALU.subtract), reads=[b_ps_m[1], b_mu], writes=[b_mu])
                            S.op("dve", lambda e: e.tensor_scalar(var[:], var[:], EPS, None, ALU.add), reads=[b_mu], writes=[b_mu])
                            S.op("act", lambda e: e.sqrt(var[:], var[:]), reads=[b_mu], writes=[b_mu])
                            S.op("dve", lambda e: e.reciprocal(var[:], var[:]), reads=[b_mu], writes=[b_mu])
                            S.op("dve", lambda e: e.tensor_tensor(of[:], of[:], mu[:], ALU.subtract), reads=[b_of, b_mu], writes=[b_of])
                            S.op("dve", lambda e: e.tensor_tensor(of[:], of[:], var[:], ALU.mult), reads=[b_of, b_mu], writes=[b_of])
                            ri = rc % 2; rc += 1
                            S.op("dve", lambda e, ri=ri, h=h, hh=hh, qi=qi: e.scalar_tensor_tensor(ro[ri][:], of[:], gnT[:, h:h + 1], gg[qi][:, hh, :], ALU.mult, ALU.mult),
                                 reads=[b_of, b_lg, b_q[qi]], writes=[b_ro[ri]])
                            S.dma("sp", MIXT[1024 + h * 128:1024 + (h + 1) * 128, q0:q0 + 512], ro[ri][:], reads=[b_ro[ri]], writes=[b_MIXT])
            S.barrier()
        if upto <= 3:
            return nc

        b_XMID, b_H2T, b_GT, b_SC = Buf(), Buf(), Buf(), Buf()
        with ExitStack() as st:
            ntres = nt_resources(st)
            ps = [pst(st, f"p4ps{i}", [128, 512]) for i in range(3)]; b_ps = [Buf() for _ in range(3)]
            psb = ntres[6]; b_psb = ntres[7]
            xs = [sbt(st, f"p4x{i}", [128, D]) for i in range(4)]; b_xs = [Buf() for _ in range(4)]
            mixT = sbt(st, "p4mix", [128, KC, T], BF16); b_mix = Buf()
            h2T = mixT; b_h2 = b_mix
            wt = [sbt(st, f"p4w{i}", [128, KC, 512], BF16) for i in range(2)]; b_wt = [Buf(), Buf()]
            g1 = sbt(st, "p4g1", [128, D]); b_g1 = Buf()
            tmp = [sbt(st, f"p4t{i}", [128, 512]) for i in range(2)]; b_tmp = [Buf(), Buf()]
            qp = sbt(st, "p4qp", [128, 16, T], BF16); b_qp = Buf()
            skn = sbt(st, "p4skn", [128, 16, 128], BF16); skT = sbt(st, "p4skT", [128, 16, 128], BF16); b_sk = Buf()
            S.dma("sp", skn[:], SKB.rearrange("(c p) d -> p c d", p=128), reads=[b_w], writes=[b_sk])
            for q4 in range(4):
                def trs(e, q4=q4):
                    for j in range(4):
                        ins = e.transpose(psb[0][:, j * 128:(j + 1) * 128], skn[:, q4 * 4 + j, :], identb)
                    return ins
                S.op("pe", trs, reads=[b_sk, b_cstb], writes=[b_psb[0]])
                S.op("act", lambda e, q4=q4: e.copy(skT[:, q4 * 4:(q4 + 1) * 4, :], psb[0][:, 0:512].rearrange("p (j k) -> p j k", j=4)),
                     reads=[b_psb[0]], writes=[b_sk])
            sc = sbt(st, "p4sc", [128, 16, 128]); b_sc = Buf()
            wc = 0; pc = 0; tc_ = 0; gc = 0
            WOv = WOB.rearrange("(c p) n -> p c n", p=128); WQv = WQB.rearrange("(c p) n -> p c n", p=128)
            own_tiles = [(t0, si) for si, (s0, sl) in enumerate(SEGS) for t0 in range(s0, s0 + sl, T)]
            for (t0, seg) in own_tiles:
                S.dma("sp", mixT[:], MIXT[:, t0:t0 + T].rearrange("(c p) t -> p c t", p=128), reads=[b_MIXT], writes=[b_mix])
                S.dma("sp", g1[:], bass.AP(MOD.tensor, seg * 6 * D + 2 * D, [[0, 128], [1, D]]), reads=[b_MOD], writes=[b_g1])
                for s in range(4):
                    S.dma("sp", xs[s][:], xin[t0 + s * 128:t0 + (s + 1) * 128, :], writes=[b_xs[s]])
                for cb in range(4):
                    wi = wc % 2; wc += 1
                    S.dma("sp", wt[wi][:], WOv[:, :, cb * 512:(cb + 1) * 512], reads=[b_w], writes=[b_wt[wi]])
                    for s in range(4):
                        pi = pc % 3; pc += 1; ti = tc_ % 2; tc_ += 1

                        def mmo(e, s=s, wi=wi, pi=pi):
                            for kc in range(KC):
                                ins = e.matmul(ps[pi][:], mixT[:, kc, s * 128:(s + 1) * 128], wt[wi][:, kc, :], start=(kc == 0), stop=(kc == KC - 1))
                            return ins
                        S.op("pe", mmo, reads=[b_mix, b_wt[wi]], writes=[b_ps[pi]])
                        S.op("dve", lambda e, pi=pi, ti=ti, cb=cb: e.tensor_tensor(tmp[ti][:], ps[pi][:], g1[:, cb * 512:(cb + 1) * 512], ALU.mult),
                             reads=[b_ps[pi], b_g1], writes=[b_tmp[ti]])
                        S.op("pool", lambda e, s=s, ti=ti, cb=cb: e.tensor_tensor(xs[s][:, cb * 512:(cb + 1) * 512], tmp[ti][:], xs[s][:, cb * 512:(cb + 1) * 512], ALU.add),
                             reads=[b_tmp[ti], b_xs[s]], writes=[b_xs[s]])
                for s in range(4):
                    S.dma("sp", XMID[t0 + s * 128:t0 + (s + 1) * 128, :], xs[s][:], reads=[b_xs[s]], writes=[b_XMID])
                    norm_transpose(st, xs[s], b_xs[s], A2, B2, seg, h2T, b_h2, s * 128, ntres)
                S.dma("sp", H2T[:, t0:t0 + T].rearrange("(c p) t -> p c t", p=128), h2T[:], reads=[b_h2], writes=[b_H2T])
                for blk in range(4):
                    wi = wc % 2; wc += 1
                    S.dma("sp", wt[wi][:], WQv[:, :, blk * 512:(blk + 1) * 512], reads=[b_w], writes=[b_wt[wi]])
                    for j in range(4):
                        pi = pc % 3; pc += 1

                        def mmq(e, j=j, wi=wi, pi=pi):
                            for kc in range(KC):
                                ins = e.matmul(ps[pi][:], wt[wi][:, kc, j * 128:(j + 1) * 128], h2T[:, kc, :], start=(kc == 0), stop=(kc == KC - 1))
                            return ins
                        S.op("pe", mmq, reads=[b_h2, b_wt[wi]], writes=[b_ps[pi]])
                        S.op("act", lambda e, pi=pi, c16=blk * 4 + j: e.copy(qp[:, c16, :], ps[pi][:]), reads=[b_ps[pi]], writes=[b_qp])
                for s in range(4):
                    ts0 = t0 + s * 128
                    for g4 in range(4):
                        pi = pc % 3; pc += 1

                        def mms(e, g4=g4, s=s, pi=pi):
                            for j in range(4):
                                c16 = g4 * 4 + j
                                ins = e.matmul(ps[pi][:, j * 128:(j + 1) * 128], qp[:, c16, s * 128:(s + 1) * 128], skT[:, c16, :], start=True, stop=True)
                            return ins
                        S.op("pe", mms, reads=[b_qp, b_sk], writes=[b_ps[pi]])
                        S.op("act", lambda e, g4=g4, pi=pi: e.copy(sc[:, g4 * 4:(g4 + 1) * 4, :], ps[pi][:].rearrange("p (j k) -> p j k", j=4)),
                             reads=[b_ps[pi]], writes=[b_sc])
                    S.dma("sp", SC[ts0:ts0 + 128, :, :], sc[:], reads=[b_sc], writes=[b_SC])
            S.barrier()
        if upto <= 4:
            return nc

        with ExitStack() as st:
            ps_t = pst(st, "g_pt", [128, 512]); b_ps_t = Buf()
            ps_g = [pst(st, f"g_pg{i}", [128, 512]) for i in range(3)]; b_ps_g = [Buf() for _ in range(3)]
            sc = [sbt(st, f"g_sc{i}", [128, 16, 128]) for i in range(2)]; b_sc = [Buf(), Buf()]
            top = sbt(st, "g_top", [128, 16, 16]); mrt = [sbt(st, f"g_mrt{i}", [128, 256]) for i in range(2)]; b_mrt = [Buf(), Buf()]; tops = sbt(st, "g_tops", [128, 16, 16]); b_top = Buf()
            cand = sbt(st, "g_cand", [128, 8, 256]); ce = sbt(st, "g_ce", [128, 8, 256]); b_cand = Buf()
            c8 = [sbt(st, f"g_c8{i}", [128, 16]) for i in range(2)]; b_c8 = [Buf(), Buf()]; th = sbt(st, "g_th", [128, 8]); zz = sbt(st, "g_z", [128, 8]); rz = sbt(st, "g_rz", [128, 8]); b_th = Buf()
            tk = sbt(st, "g_tk", [128, 3, 128]); b_tk = Buf()
            tkT = [sbt(st, f"g_tkT{i}", [128, 3, 128]) for i in range(2)]; b_tkT = [Buf(), Buf()]
            s1r = [sbt(st, f"g_s1r{i}", [128, 32, 128]) for i in range(2)]; s2r = [sbt(st, f"g_s2r{i}", [128, 32, 128]) for i in range(2)]
            b_rep = [Buf(), Buf()]
            e2 = sbt(st, "g_e2", [128, 32, 128], BF16); b_e2 = Buf()
            kapb = [sbt(st, f"g_kapb{i}", [128, 128], BF16) for i in range(2)]
            mk = sbt(st, "g_mk", [128, 32, 128], BF16); b_mk = Buf()
            eq = sbt(st, "g_eq", [128, 32, 128], BF16); b_eq = Buf()
            Lt = [sbt(st, f"g_L{i}", [128, 32, 128], BF16) for i in range(2)]; b_L = [Buf(), Buf()]
            Rt = [sbt(st, f"g_R{i}", [128, 32, 128], BF16) for i in range(2)]; b_R = [Buf(), Buf()]
            gts = sbt(st, "g_gts", [128, 128, 128], BF16); b_gts = Buf()
            b_SCS = [Buf() for _ in range(4)]
            cnts = [0, 0]
            def stageA(sti):
                ts0 = sti * 128
                i = sti % 2
                scc = sc[i]
                S.dma("sp", scc[:], SC[ts0:ts0 + 128, :, :], reads=[b_SC], writes=[b_sc[i]])
                for c0 in range(0, 16, 2):
                    for c16 in (c0, c0 + 1):
                        S.op("dve", lambda e, c16=c16: e.max(top[:, c16, 0:8], scc[:, c16, :]), reads=[b_sc[i]], writes=[b_top])
                    for c16 in (c0, c0 + 1):
                        S.op("dve", lambda e, c16=c16: e.match_replace(mrt[c16 % 2][:, 0:128], top[:, c16, 0:8], scc[:, c16, :], -1e30), reads=[b_sc[i], b_top], writes=[b_mrt[c16 % 2]])
                    for c16 in (c0, c0 + 1):
                        S.op("dve", lambda e, c16=c16: e.max(top[:, c16, 8:16], mrt[c16 % 2][:, 0:128]), reads=[b_mrt[c16 % 2]], writes=[b_top])
                    yield
                S.op("dve", lambda e: e.tensor_tensor(scc[:], scc[:], apx(top[:, 0, 0:1], [[16, 16], [0, 128]]), ALU.subtract), reads=[b_sc[i], b_top], writes=[b_sc[i]])
                S.dma("sp", SCS[:, ts0:ts0 + 128, :].rearrange("c t k -> t c k"), scc[:], reads=[b_sc[i]], writes=[b_SCS[sti % 4]])
                S.op("dve", lambda e: e.tensor_tensor(tops[:], top[:], apx(top[:, 0, 0:1], [[16, 16], [0, 16]]), ALU.subtract), reads=[b_top], writes=[b_top])
                S.op("dve", lambda e: e.tensor_tensor(apx(cand[:, 0, 0:1], [[256, 8], [16, 16], [1, 16]]),
                                                      apx(tops[:, 0, 0:1], [[32, 8], [1, 16], [0, 16]]),
                                                      apx(tops[:, 1, 0:1], [[32, 8], [0, 16], [1, 16]]), ALU.add),
                     reads=[b_top], writes=[b_cand])
                yield
                for h0 in range(0, H, 2):
                    for h in (h0, h0 + 1):
                        S.op("dve", lambda e, h=h: e.max(c8[h % 2][:, 0:8], cand[:, h, :]), reads=[b_cand], writes=[b_c8[h % 2]])
                    for h in (h0, h0 + 1):
                        S.op("dve", lambda e, h=h: e.match_replace(mrt[h % 2][:], c8[h % 2][:, 0:8], cand[:, h, :], -1e30), reads=[b_cand, b_c8[h % 2]], writes=[b_mrt[h % 2]])
                    for h in (h0, h0 + 1):
                        S.op("dve", lambda e, h=h: e.max(c8[h % 2][:, 8:16], mrt[h % 2][:]), reads=[b_mrt[h % 2]], writes=[b_c8[h % 2]])
                    for h in (h0, h0 + 1):
                        S.op("dve", lambda e, h=h: e.tensor_copy(th[:, h:h + 1], c8[h % 2][:, 15:16]), reads=[b_c8[h % 2]], writes=[b_th])
                    yield
                S.op("act", lambda e: e.activation(ce[:], cand[:], AF.Exp), reads=[b_cand], writes=[b_cand])
                S.op("dve", lambda e: e.tensor_tensor(cand[:], cand[:], apx(th[:, 0:1], [[1, 8], [0, 256]]), ALU.is_ge), reads=[b_cand, b_th], writes=[b_cand])
                S.op("dve", lambda e: e.tensor_tensor(ce[:], ce[:], cand[:], ALU.mult), reads=[b_cand], writes=[b_cand])
                S.op("dve", lambda e: e.tensor_reduce(zz[:], ce[:], AX.X, ALU.add), reads=[b_cand, b_th], writes=[b_th])
                S.op("dve", lambda e: e.reciprocal(rz[:], zz[:]), reads=[b_th], writes=[b_th])
                s1tops = apx(tops[:, 0, 0:1], [[32, 8], [1, 16]])
                tk3 = lambda j: tk[:, j, :].rearrange("p (h a) -> p h a", h=8)
                S.op("dve", lambda e: e.tensor_tensor(tk3(0), apx(th[:, 0:1], [[1, 8], [0, 16]]), s1tops, ALU.subtract), reads=[b_th, b_top], writes=[b_tk])
                S.op("dve", lambda e: e.tensor_scalar(tk[:, 0, :], tk[:, 0, :], -1e-5, None, ALU.add), reads=[b_tk], writes=[b_tk])
                S.op("act", lambda e: e.activation(tk3(1), s1tops, AF.Exp), reads=[b_top], writes=[b_tk])
                S.op("dve", lambda e: e.tensor_tensor(tk3(1), tk3(1), apx(rz[:, 0:1], [[1, 8], [0, 16]]), ALU.mult), reads=[b_tk, b_th], writes=[b_tk])
                S.op("dve", lambda e: e.tensor_copy(tk3(2), s1tops), reads=[b_top], writes=[b_tk])

                def trk(e):
                    for j in range(3):
                        ins = e.transpose(ps_t[:, j * 128:(j + 1) * 128], tk[:, j, :], identf)
                    return ins
                S.op("pe", trk, reads=[b_tk, b_cst], writes=[b_ps_t])
                S.op("act", lambda e: e.copy(tkT[i][:], ps_t[:, 0:384].rearrange("p (j t) -> p j t", j=3)), reads=[b_ps_t], writes=[b_tkT[i]])
                S.op("act", lambda e: e.copy(kapb[i][:], ps_t[:, 128:256]), reads=[b_ps_t], writes=[b_tkT[i]])
                yield
            def front(sti, g, genA):
                ts0 = sti * 128
                i = sti % 2
                r = cnts[0] % 2; cnts[0] += 1
                tg0 = ts0 + g * 32
                for h in range(H):
                    S.dma("sp", s1r[r][h * 16:(h + 1) * 16, :, :].rearrange("p t k -> p (t k)"), bass.AP(SCS.tensor, ((2 * h) * NOWN + tg0) * 128, [[0, 16], [1, 4096]]),
                          reads=[b_SCS[sti % 4]], writes=[b_rep[r]])
                    S.dma("sp", s2r[r][h * 16:(h + 1) * 16, :, :].rearrange("p t k -> p (t k)"), bass.AP(SCS.tensor, ((2 * h + 1) * NOWN + tg0) * 128, [[0, 16], [1, 4096]]),
                          reads=[b_SCS[sti % 4]], writes=[b_rep[r]])
                bc = lambda j: apx(tkT[i][:, j, g * 32:g * 32 + 1], [[1, 32], [0, 128]])
                S.op("act", lambda e: e.activation(e2[:], s2r[r][:], AF.Exp), reads=[b_rep[r]], writes=[b_e2])
                S.op("dve", lambda e: e.tensor_tensor(mk[:], s2r[r][:], bc(0), ALU.is_ge), reads=[b_rep[r], b_tkT[i]], writes=[b_mk])
                S.op("pool", lambda e: e.tensor_tensor(Lt[r][:], mk[:], e2[:], ALU.mult), reads=[b_mk, b_e2], writes=[b_L[r]])
                S.op("dve", lambda e: e.tensor_tensor(eq[:], s1r[r][:], bc(2), ALU.is_equal), reads=[b_rep[r], b_tkT[i]], writes=[b_eq])
                S.op("pool", lambda e: e.tensor_tensor(Rt[r][:], eq[:], apx(kapb[i][:, g * 32:g * 32 + 1], [[1, 32], [0, 128]]), ALU.mult),
                     reads=[b_eq, b_tkT[i]], writes=[b_R[r]])
                if genA is not None:
                    for _ in range(5):
                        next(genA, None)
                    if g == 3:
                        for _ in genA:
                            pass
                return r

            def back(sti, g, r):
                ts0 = sti * 128
                for t4 in range(8):
                    k = cnts[1] % 3; cnts[1] += 1

                    def mmg(e, t4=t4, k=k):
                        for j in range(4):
                            tl = t4 * 4 + j
                            ins = e.matmul(ps_g[k][:, j * 128:(j + 1) * 128], Lt[r][:, tl, :], Rt[r][:, tl, :], start=True, stop=True)
                        return ins
                    S.op("pe", mmg, reads=[b_L[r], b_R[r]], writes=[b_ps_g[k]])
                    col = g * 32 + t4 * 4
                    S.op("act", lambda e, k=k, col=col: e.copy(gts[:, :, col:col + 4], apx(ps_g[k][:, 0:1], [[1, 128], [128, 4]])),
                         reads=[b_ps_g[k]], writes=[b_gts])
                if g == 3:
                    S.dma("sp", GT[sti], gts[:], reads=[b_gts], writes=[b_GT])

            nst = NOWN // 128
            for _ in stageA(0):
                pass
            groups = [(sti, g) for sti in range(nst) for g in range(4)]
            gens = {}
            prev = None
            for (sti, g) in groups:
                if g == 0:
                    gens[sti] = stageA(sti + 1) if sti + 1 < nst else None
                r = front(sti, g, gens[sti])
                if prev is not None:
                    back(*prev)
                prev = (sti, g, r)
            back(*prev)
            S.barrier()
        if upto <= 5:
            return nc

        with ExitStack() as st:
            psb = [pst(st, f"p5psb{i}", [128, 1024], BF16) for i in range(2)]; b_psb = [Buf(), Buf()]
            ps_a = [pst(st, f"p5a{i}", [128, 512]) for i in range(2)]; b_ps_a = [Buf(), Buf()]
            ps_o = [pst(st, f"p5o{i}", [128, 512]) for i in range(3)]; b_ps_o = [Buf() for _ in range(3)]
            b_UT = Buf()
            with ExitStack() as st2:
                un = [sbt(st2, f"p5un{i}", [128, 4, D], BF16) for i in range(2)]; b_un = [Buf(), Buf()]
                uts = [sbt(st2, f"p5uts{i}", [128, KC, 512], BF16) for i in range(2)]; b_uts = [Buf(), Buf()]
                tcn = 0
                for g in range(32):
                    i = g % 2
                    S.dma("sp", un[i][:], UB[g * 512:(g + 1) * 512, :].rearrange("(j p) d -> p j d", p=128), reads=[b_w], writes=[b_un[i]])
                    for kc in range(KC):
                        pi = tcn % 2; tcn += 1

                        def tru(e, i=i, kc=kc, pi=pi):
                            for j in range(4):
                                ins = e.transpose(psb[pi][:, j * 128:(j + 1) * 128], un[i][:, j, kc * 128:(kc + 1) * 128], identb)
                            return ins
                        S.op("pe", tru, reads=[b_un[i], b_cstb], writes=[b_psb[pi]])
                        eng = "act" if kc % 2 == 0 else "dve"
                        if eng == "act":
                            S.op("act", lambda e, i=i, kc=kc, pi=pi: e.copy(uts[i][:, kc, :], psb[pi][:, 0:512]), reads=[b_psb[pi]], writes=[b_uts[i]])
                        else:
                            S.op("dve", lambda e, i=i, kc=kc, pi=pi: e.tensor_copy(uts[i][:, kc, :], psb[pi][:, 0:512]), reads=[b_psb[pi]], writes=[b_uts[i]])
                    S.dma("sp", UT[:, g * 512:(g + 1) * 512].rearrange("(c p) e -> p c e", p=128), uts[i][:], reads=[b_uts[i]], writes=[b_UT])
            h2T = sbt(st, "p5h2", [128, KC, T], BF16); b_h2 = Buf()
            acc = sbt(st, "p5acc", [128, 4, D]); b_acc = Buf()
            ug = [sbt(st, f"p5ug{i}", [128, KC, 512], BF16) for i in range(2)]
            vg = [sbt(st, f"p5vg{i}", [128, 4, D], BF16) for i in range(2)]
            gg = [sbt(st, f"p5gg{i}", [128, 4 * T], BF16) for i in range(2)]; b_g = [Buf(), Buf()]
            ga = [sbt(st, f"p5ga{i}", [128, T]) for i in range(2)]; b_ga = [Buf(), Buf()]
            wT = [sbt(st, f"p5wT{i}", [128, 4, T], BF16) for i in range(2)]; b_wT = [Buf(), Buf()]
            xm = [sbt(st, f"p5xm{i}", [128, D]) for i in range(2)]; b_xm = [Buf(), Buf()]
            g2 = sbt(st, "p5g2", [128, D]); b_g2 = Buf()
            ac = 0; oc = 0; xc = 0
            for (t0, seg) in own_tiles:
                S.dma("sp", h2T[:], H2T[:, t0:t0 + T].rearrange("(c p) t -> p c t", p=128), reads=[b_H2T], writes=[b_h2])
                S.dma("sp", g2[:], bass.AP(MOD.tensor, seg * 6 * D + 5 * D, [[0, 128], [1, D]]), reads=[b_MOD], writes=[b_g2])
                for g in range(32):
                    i = g % 2
                    S.dma("sp", ug[i][:], UT[:, g * 512:(g + 1) * 512].rearrange("(c p) e -> p c e", p=128), reads=[b_UT], writes=[b_g[i]])
                    S.dma("sp", vg[i][:], VB[g * 512:(g + 1) * 512, :].rearrange("(j p) d -> p j d", p=128), reads=[b_w], writes=[b_g[i]])
                    for s4 in range(4):
                        S.dma("sp", gg[i][:, s4 * 512:(s4 + 1) * 512], GT[t0 // 128 + s4, :, g * 4:(g + 1) * 4, :].rearrange("p c t -> p (c t)"), reads=[b_GT], writes=[b_g[i]])
                    for j in range(4):
                        ai = ac % 2; ac += 1

                        def mma(e, i=i, j=j, ai=ai):
                            for kc in range(KC):
                                ins = e.matmul(ps_a[ai][:], ug[i][:, kc, j * 128:(j + 1) * 128], h2T[:, kc, :], start=(kc == 0), stop=(kc == KC - 1))
                            return ins
                        S.op("pe", mma, reads=[b_g[i], b_h2], writes=[b_ps_a[ai]])
                        S.op("act", lambda e, ai=ai: e.activation(ga[ai][:], ps_a[ai][:], AF.Gelu_apprx_tanh), reads=[b_ps_a[ai]], writes=[b_ga[ai]])
                        S.op("dve", lambda e, i=i, j=j, ai=ai: e.tensor_tensor(wT[i][:, j, :].rearrange("p (s t) -> p s t", s=4), ga[ai][:].rearrange("p (s t) -> p s t", s=4), apx(gg[i][:, j * 128:j * 128 + 1], [[512, 4], [1, 128]]), ALU.mult),
                             reads=[b_ga[ai], b_g[i]], writes=[b_wT[i]])
                    for s in range(4):
                        for cb in range(4):
                            oi = oc % 3; oc += 1

                            def mmo5(e, i=i, s=s, cb=cb, oi=oi):
                                for j in range(4):
                                    ins = e.matmul(ps_o[oi][:], wT[i][:, j, s * 128:(s + 1) * 128], vg[i][:, j, cb * 512:(cb + 1) * 512], start=(j == 0), stop=(j == 3))
                                return ins
                            S.op("pe", mmo5, reads=[b_wT[i], b_g[i]], writes=[b_ps_o[oi]])
                            eng = "dve" if (s * 4 + cb) % 2 == 0 else "pool"
                            dst = acc[:, s, cb * 512:(cb + 1) * 512]
                            if g == 0:
                                S.op("act", lambda e, dst=dst, oi=oi: e.copy(dst, ps_o[oi][:]), reads=[b_ps_o[oi]], writes=[b_acc])
                            else:
                                S.op("dve", lambda e, dst=dst, oi=oi: e.tensor_tensor(dst, dst, ps_o[oi][:], ALU.add), reads=[b_ps_o[oi], b_acc], writes=[b_acc])
                for s in range(4):
                    xi = xc % 2; xc += 1
                    S.dma("sp", xm[xi][:], XMID[t0 + s * 128:t0 + (s + 1) * 128, :], reads=[b_XMID], writes=[b_xm[xi]])
                    S.op("dve", lambda e, s=s: e.tensor_tensor(acc[:, s, :], acc[:, s, :], g2[:], ALU.mult), reads=[b_acc, b_g2], writes=[b_acc])
                    S.op("pool", lambda e, s=s, xi=xi: e.tensor_tensor(xm[xi][:], xm[xi][:], acc[:, s, :], ALU.add), reads=[b_acc, b_xm[xi]], writes=[b_xm[xi]])
                    S.dma("sp", yout[t0 + s * 128:t0 + (s + 1) * 128, :], xm[xi][:], reads=[b_xm[xi]])
            S.barrier()
    return nc


def _tables(pos):
    pos = pos.astype(np.float32)
    r = np.arange(128)
    invM = (1.0 / (np.float32(10000.0) ** (np.arange(0, 64, 2, dtype=np.float32) / np.float32(64)))).astype(np.float32)
    angM = (pos[None, :] * invM[r % 32][:, None]).astype(np.float32)
    sgnM = np.where((r % 64) < 32, -1.0, 1.0).astype(np.float32)[:, None]
    cosM = np.cos(angM).astype(np.float32); sinM = (np.sin(angM).astype(np.float32) * sgnM).astype(np.float32)
    invR = (1.0 / (np.float32(10000.0) ** (np.arange(0, 128, 2, dtype=np.float32) / np.float32(128)))).astype(np.float32)
    angR = (pos[None, :] * invR[r % 64][:, None]).astype(np.float32)
    sgnR = np.where(r < 64, -1.0, 1.0).astype(np.float32)[:, None]
    cosR = np.cos(angR).astype(np.float32); sinR = (np.sin(angR).astype(np.float32) * sgnR).astype(np.float32)
    ks = np.float32(128.0 ** -0.5)
    return cosM, sinM, cosR, sinR, (cosR * ks).astype(np.float32), (sinR * ks).astype(np.float32)


def _consts():
    cst = np.zeros((128, 10, 128), np.float32)
    m = np.arange(128)
    cst[:, 0, :] = np.eye(128)
    cst[:, 1, :] = 1.0
    cst[m ^ 32, 2, m] = 1.0
    cst[m ^ 64, 3, m] = 1.0
    return cst


_PROG = {}


def kernel(x_prompt, x_sample, c_prompt, c_sample, norm1_w, norm2_w, w_ada, b_ada, w_in, q_a_norm, kv_a_norm,
           w_uq, w_uk, w_uv, q_norm, k_norm, ret_decay_logit, ret_gn_w, w_o, peer_wq, peer_sub_keys, peer_u, peer_v,
           _upto=9, _dbg=(), _trace=False):
    f = lambda a: np.ascontiguousarray(np.asarray(a, dtype=np.float32))
    x_prompt, x_sample, c_prompt, c_sample = f(x_prompt), f(x_sample), f(c_prompt), f(c_sample)
    SEQ = x_prompt.shape[1]; SS = x_sample.shape[1]; SA = SEQ // 2
    NOWN = SA + 2 * SS; NTOK = NOWN + SA
    key = (SA, SS, _upto, tuple(_dbg))
    if key not in _PROG:
        _PROG[key] = build(SA, SS, _upto, _dbg)
    nc = _PROG[key]
    shared = {
        "norm1_w": f(norm1_w).reshape(-1), "norm2_w": f(norm2_w).reshape(-1), "w_ada": f(w_ada)[0], "b_ada": f(b_ada).reshape(-1),
        "w_in": f(w_in)[0], "q_a_norm": f(q_a_norm).reshape(-1), "kv_a_norm": f(kv_a_norm).reshape(-1),
        "w_uq": f(w_uq)[0], "w_uk": f(w_uk)[0], "w_uv": f(w_uv)[0], "q_norm": f(q_norm).reshape(-1), "k_norm": f(k_norm).reshape(-1),
        "ret_decay_logit": f(ret_decay_logit).reshape(-1), "ret_gn_w": f(ret_gn_w).reshape(-1), "w_o": f(w_o)[0],
        "peer_wq": f(peer_wq)[0], "peer_sub_keys": f(peer_sub_keys).reshape(2048, 128),
        "peer_u": f(peer_u)[0], "peer_v": f(peer_v)[0], "cst": _consts(),
    }
    in_maps = []
    for c in range(NCORES):
        pb, par = c // 2, c % 2
        own = x_prompt[pb, par * SA:(par + 1) * SA]; oth = x_prompt[pb, (1 - par) * SA:(2 - par) * SA]
        xin = np.concatenate([own, x_sample[2 * c], x_sample[2 * c + 1], oth], axis=0)
        pos = np.concatenate([par * SA + np.arange(SA), np.arange(SS), np.arange(SS), (1 - par) * SA + np.arange(SA)]).astype(np.float32)
        cosM, sinM, cosR, sinR, cosRk, sinRk = _tables(pos)
        m = dict(shared)
        m.update({"xin": np.ascontiguousarray(xin), "c3": np.stack([c_prompt[pb], c_sample[2 * c], c_sample[2 * c + 1]]),
                  "cosM": cosM, "sinM": sinM, "cosR": cosR, "sinR": sinR, "cosRk": cosRk, "sinRk": sinRk,
                  "posrow": np.ascontiguousarray(np.broadcast_to(pos[None, :], (128, NTOK))),
                  "poscol": np.ascontiguousarray(pos.reshape(NTOK // 128, 128).T),
                  "cvec": np.zeros((128, 4), np.float32)})
        in_maps.append(m)
    res = run_bass_kernel_spmd(nc, in_maps, core_ids=list(range(NCORES)), **({'trace': True} if _trace else {}))
    if _trace:
        print('EXEC_TIME_NS', _upto, res.exec_time_ns)
    yp = np.zeros(x_prompt.shape, np.float32); ys = np.zeros(x_sample.shape, np.float32)
    for c in range(NCORES):
        y = res.results[c]["yout"]
        pb, par = c // 2, c % 2
        yp[pb, par * SA:(par + 1) * SA] = y[0:SA]
        ys[2 * c] = y[SA:SA + SS]; ys[2 * c + 1] = y[SA + SS:SA + 2 * SS]
    if _dbg:
        return (yp, ys), res
    return (yp, ys)
```

```python
from contextlib import ExitStack
import numpy as np
import concourse.bass as bass
import concourse.mybir as mybir
from concourse.bass_utils import run_bass_kernel_spmd

F32 = mybir.dt.float32
BF16 = mybir.dt.bfloat16
ALU = mybir.AluOpType
AF = mybir.ActivationFunctionType
AX = mybir.AxisListType

D = 2048
KC = 16
T = 512
H = 8
EPS = 1e-6
NCORES = 8


class Buf:
    __slots__ = ("name", "w", "r")

    def __init__(self, name=""):
        self.name = name
        self.w = {}
        self.r = {}


class Sched:
    NDMA = 16
    SAME = {"act": True, "dve": True, "pool": True, "pe": False, "sp": True}

    def __init__(self, nc, stack):
        self.nc = nc
        self.stack = stack
        self.eng = {"pe": nc.tensor, "act": nc.scalar, "dve": nc.vector, "pool": nc.gpsimd, "sp": nc.sync}
        self.csem, self.ccnt = {}, {}
        self.known = {k: {} for k in self.eng}
        self.nsem = 0
        for k in self.eng:
            self._new_csem(k)
        self.dsem = {k: [] for k in self.eng}
        self.drr = {k: 0 for k in self.eng}
        self.n_inst = 0

    def _alloc_sem(self, name):
        self.nsem += 1
        return self.stack.enter_context(self.nc.semaphore(f"{name}_{self.nsem}"))

    def _new_csem(self, k):
        self.csem[k] = self._alloc_sem("c" + k)
        self.ccnt[k] = 0

    def _wait(self, k, tok, raw=True, fam=None, force=False):
        sem, val, src = tok
        if not force:
            if src == fam and not raw:
                return False
            if src == k and not self.SAME[k]:
                return False
        kn = self.known[k]
        if kn.get(id(sem), 0) >= val:
            return True
        self.eng[k].wait_ge(sem, val)
        self.n_inst += 1
        kn[id(sem)] = val
        return True

    def _deps(self, k, reads, writes, fam):
        for b in reads:
            for t in b.w.values():
                self._wait(k, t, True, fam)
        for b in writes:
            for t in b.w.values():
                self._wait(k, t, False, fam)
            for t in b.r.values():
                self._wait(k, t, False, fam)

    def _commit(self, tok, reads, writes, fam):
        for b in writes:
            b.w = {i: t for i, t in b.w.items() if t[2] == fam}
            b.w[id(tok[0])] = tok
            b.r = {}
        for b in reads:
            if id(tok[0]) in b.w and b.w[id(tok[0])] is tok:
                continue
            b.r[id(tok[0])] = tok

    def op(self, k, fn, reads=(), writes=()):
        self._deps(k, reads, writes, k)
        ins = fn(self.eng[k])
        self.n_inst += 1
        if self.ccnt[k] >= 30000:
            self._new_csem(k)
        self.ccnt[k] += 1
        ins.then_inc(self.csem[k], 1)
        tok = (self.csem[k], self.ccnt[k], k)
        self._commit(tok, reads, writes, k)
        return tok

    def dma(self, k, out, in_, reads=(), writes=(), **kw):
        fam = "dma:" + k
        self._deps(k, reads, writes, fam)
        pool = self.dsem[k]
        if len(pool) < self.NDMA:
            pool.append([self._alloc_sem("d" + k), 0])
            ent = pool[-1]
        else:
            ent = pool[self.drr[k] % self.NDMA]
            self.drr[k] += 1
            self._wait(k, (ent[0], ent[1], fam), force=True)
        ent[1] += 16
        ins = self.eng[k].dma_start(out=out, in_=in_, **kw)
        ins.then_inc(ent[0], 16)
        self.n_inst += 1
        tok = (ent[0], ent[1], fam)
        self._commit(tok, reads, writes, fam)
        return tok

    def barrier(self, engines=None):
        toks = []
        for k in self.eng:
            if self.ccnt[k] > 0:
                toks.append((self.csem[k], self.ccnt[k], k))
            for ent in self.dsem[k]:
                if ent[1] > 0:
                    toks.append((ent[0], ent[1], "dma:" + k))
        for k in (engines or self.eng):
            for t in toks:
                self._wait(k, t, force=True)


def apx(ap, pattern):
    return bass.AP(ap.tensor, ap.offset, [list(ap.ap[0])] + [list(p) for p in pattern])


def build(SA, SS, upto=9, dbg_out=()):
    NOWN = SA + 2 * SS
    NTOK = NOWN + SA
    SEGS = [(0, SA), (SA, SS), (SA + SS, SS)]
    nc = bass.Bass("TRN2", target_bir_lowering=False)

    def din(name, shape, dt=F32):
        return nc.dram_tensor(name, list(shape), dt, kind="ExternalInput").ap()

    def dscr(name, shape, dt=BF16):
        return nc.dram_tensor(name, list(shape), dt).ap()

    xin = din("xin", [NTOK, D])
    c3 = din("c3", [3, D])
    norm1_w = din("norm1_w", [D]); norm2_w = din("norm2_w", [D])
    w_ada = din("w_ada", [D, 6 * D]); b_ada = din("b_ada", [6 * D])
    w_in = din("w_in", [D, 4928])
    q_a_norm = din("q_a_norm", [512]); kv_a_norm = din("kv_a_norm", [256])
    w_uq = din("w_uq", [512, 1536]); w_uk = din("w_uk", [256, 1024]); w_uv = din("w_uv", [256, 1024])
    q_norm = din("q_norm", [192]); k_norm = din("k_norm", [192])
    rdl = din("ret_decay_logit", [16]); gn_w = din("ret_gn_w", [1024])
    w_o = din("w_o", [D, D]); peer_wq = din("peer_wq", [D, D])
    sub_keys = din("peer_sub_keys", [16 * 128, 128])
    peer_u = din("peer_u", [16384, D]); peer_v = din("peer_v", [16384, D])
    cosM = din("cosM", [128, NTOK]); sinM = din("sinM", [128, NTOK])
    cosR = din("cosR", [128, NTOK]); sinR = din("sinR", [128, NTOK])
    cosRk = din("cosRk", [128, NTOK]); sinRk = din("sinRk", [128, NTOK])
    cst = din("cst", [128, 10, 128])
    cvec = din("cvec", [128, 4])
    posrow = din("posrow", [128, NTOK]); poscol = din("poscol", [128, NTOK // 128])
    yout = nc.dram_tensor("yout", [NOWN, D], F32, kind="ExternalOutput").ap()

    WIN = dscr("WIN", [D, 4928]); WUQ = dscr("WUQ", [512, 1536]); WUK = dscr("WUK", [256, 1024])
    WUV = dscr("WUV", [256, 1024]); WOB = dscr("WOB", [D, D]); WQB = dscr("WQB", [D, D])
    SKB = dscr("SKB", [2048, 128]); UB = dscr("UB", [16384, D]); VB = dscr("VB", [16384, D])
    UT = dscr("UT", [D, 16384])
    MOD = dscr("MOD", [3, 6 * D], F32)
    QmT = dscr("QmT", [H, 192, NOWN]); KmT = dscr("KmT", [H, 192, NTOK]); Vm = dscr("Vm", [NTOK, 1024])
    RQT = dscr("RQT", [H, 128, NOWN]); RKT = dscr("RKT", [H, 128, NTOK]); RV = dscr("RV", [NTOK, 1024])
    RGT = dscr("RGT", [H, 128, NOWN])
    MIXT = dscr("MIXT", [D, NOWN])
    XMID = dscr("XMID", [NOWN, D], F32)
    H2T = dscr("H2T", [D, NOWN])
    SC = dscr("SC", [NOWN, 16, 128], F32)
    SCS = dscr("SCS", [16, NOWN, 128], F32)
    GT = dscr("GT", [NOWN // 128, 128, 128, 128])

    with ExitStack() as top:
        S = Sched(nc, top)

        uid = [0]

        def sbt(st, name, shape, dt=F32):
            uid[0] += 1
            return st.enter_context(nc.sbuf_tensor(f"{name}_{uid[0]}", list(shape), dt))

        def pst(st, name, shape, dt=F32):
            uid[0] += 1
            return st.enter_context(nc.psum_tensor(f"{name}_{uid[0]}", list(shape), dt))

        cstf = sbt(top, "cstf", [128, 10, 128]); b_cst = Buf()
        S.dma("sp", cstf[:], cst, writes=[b_cst])
        cv = sbt(top, "cv", [128, 4]); b_cv = Buf()
        S.dma("sp", cv[:], cvec, writes=[b_cv])
        identf = cstf[:, 0, :]
        cstb = sbt(top, "cstb", [128, 4, 128], BF16); b_cstb = Buf()
        S.op("dve", lambda e: e.tensor_copy(cstb[:], cstf[:, 0:4, :]), reads=[b_cst], writes=[b_cstb])
        identb, onesb, PMb, PRb = cstb[:, 0, :], cstb[:, 1, :], cstb[:, 2, :], cstb[:, 3, :]
        n1T = sbt(top, "n1T", [128, KC]); n2T = sbt(top, "n2T", [128, KC])
        qanT = sbt(top, "qanT", [128, 4]); kvanT = sbt(top, "kvanT", [128, 2])
        qnT = sbt(top, "qnT", [128, 2]); knT = sbt(top, "knT", [128, 2])
        modT = sbt(top, "modT", [128, 6, KC, 3])
        A1 = sbt(top, "A1", [128, KC, 3]); A2 = sbt(top, "A2", [128, KC, 3])
        b_vec = Buf()
        with nc.allow_non_contiguous_dma(reason="tiny one-time parameter loads"):
            S.dma("sp", n1T[:], norm1_w.rearrange("(c p) -> p c", p=128), writes=[b_vec])
            S.dma("sp", n2T[:], norm2_w.rearrange("(c p) -> p c", p=128), writes=[b_vec])
            S.dma("sp", qanT[:], q_a_norm.rearrange("(c p) -> p c", p=128), writes=[b_vec])
            S.dma("sp", kvanT[:], kv_a_norm.rearrange("(c p) -> p c", p=128), writes=[b_vec])
            S.dma("sp", qnT[:, 0:1], q_norm[0:128].rearrange("(p o) -> p o", o=1), writes=[b_vec])
            S.dma("sp", qnT[0:64, 1:2], q_norm[128:192].rearrange("(p o) -> p o", o=1), writes=[b_vec])
            S.dma("sp", knT[:, 0:1], k_norm[0:128].rearrange("(p o) -> p o", o=1), writes=[b_vec])
            S.dma("sp", knT[0:64, 1:2], k_norm[128:192].rearrange("(p o) -> p o", o=1), writes=[b_vec])

        b_w = Buf()

        def cast_rows(dst, src, rows, step):
            for r0 in range(0, rows, step):
                S.dma("pool", dst[r0:r0 + step, :], src[r0:r0 + step, :], writes=[b_w])

        cast_rows(WIN, w_in, D, 256)
        cast_rows(WUQ, w_uq, 512, 256); cast_rows(WUK, w_uk, 256, 256); cast_rows(WUV, w_uv, 256, 256)
        cast_rows(WOB, w_o, D, 512); cast_rows(WQB, peer_wq, D, 512); cast_rows(SKB, sub_keys, 2048, 512)
        cast_rows(UB, peer_u, 16384, 512); cast_rows(VB, peer_v, 16384, 512)

        with ExitStack() as st:
            cT = sbt(st, "cT", [128, KC, 3]); b_cT = Buf()
            with nc.allow_non_contiguous_dma(reason="tiny c transpose"):
                for b in range(3):
                    S.dma("sp", cT[:, :, b], c3[b].rearrange("(c p) -> p c", p=128), writes=[b_cT])
            S.op("act", lambda e: e.activation(cT[:], cT[:], AF.Silu), reads=[b_cT], writes=[b_cT])
            bada = sbt(st, "bada", [3, 6 * D]); b_bada = Buf()
            S.dma("sp", bada[:], bass.AP(b_ada.tensor, 0, [[0, 3], [1, 6 * D]]), writes=[b_bada])
            wa = [sbt(st, f"wa{i}", [128, KC, 512]) for i in range(2)]
            b_wa = [Buf(), Buf()]
            psm = [pst(st, f"psm{i}", [128, 512]) for i in range(2)]
            b_psm = [Buf(), Buf()]
            modrow = sbt(st, "modrow", [3, 6 * D]); b_modrow = Buf()
            wav = w_ada.rearrange("(c p) n -> p c n", p=128)
            for blk in range(24):
                i = blk % 2
                S.dma("sp", wa[i][:], wav[:, :, blk * 512:(blk + 1) * 512], writes=[b_wa[i]])

                def mm(e, i=i):
                    for kc in range(KC):
                        ins = e.matmul(psm[i][0:3, :], cT[:, kc, :], wa[i][:, kc, :], start=(kc == 0), stop=(kc == KC - 1))
                    return ins
                S.op("pe", mm, reads=[b_cT, b_wa[i]], writes=[b_psm[i]])
                S.op("dve", lambda e, i=i, blk=blk: e.tensor_tensor(modrow[:, blk * 512:(blk + 1) * 512], psm[i][0:3, :],
                                                                    bada[:, blk * 512:(blk + 1) * 512], ALU.add),
                     reads=[b_psm[i], b_bada], writes=[b_modrow])
            b_MOD = Buf()
            S.dma("sp", MOD, modrow[:], reads=[b_modrow], writes=[b_MOD])
            b_modT = Buf()
            with nc.allow_non_contiguous_dma(reason="one-time mod transpose"):
                for g in range(6):
                    for b in range(3):
                        S.dma("sp", modT[:, g, :, b], MOD[b, g * D:(g + 1) * D].rearrange("(c p) -> p c", p=128),
                              reads=[b_MOD], writes=[b_modT])
            b_A = Buf()
            for (A, nT, g) in ((A1, n1T, 1), (A2, n2T, 4)):
                S.op("dve", lambda e, A=A, g=g: e.tensor_scalar(A[:], modT[:, g, :, :], 1.0, None, ALU.add),
                     reads=[b_modT], writes=[b_A])
                S.op("dve", lambda e, A=A, nT=nT: e.tensor_tensor(A[:], A[:], apx(nT[:, 0:1], [[1, KC], [0, 3]]), ALU.mult),
                     reads=[b_A, b_vec], writes=[b_A])
            S.barrier()
        B1 = modT[:, 0, :, :]
        B2 = modT[:, 3, :, :]

        def norm_transpose(st_, xt, b_xt, A, B, seg, hT, b_hT, col0, res):
            junk, b_junk, ss, b_ss, xn, b_xn, psb, b_psb = res
            S.op("act", lambda e: e.activation(junk[:], xt[:], AF.Square, accum_out=ss[:, 0:1]),
                 reads=[b_xt], writes=[b_junk, b_ss])
            S.op("dve", lambda e: e.tensor_scalar(ss[:, 1:2], ss[:, 0:1], 1.0 / D, EPS, ALU.mult, ALU.add),
                 reads=[b_ss], writes=[b_ss])
            S.op("act", lambda e: e.sqrt(ss[:, 1:2], ss[:, 1:2]), reads=[b_ss], writes=[b_ss])
            S.op("dve", lambda e: e.reciprocal(ss[:, 2:3], ss[:, 1:2]), reads=[b_ss], writes=[b_ss])
            S.op("dve", lambda e: e.tensor_scalar(xn[:], xt[:], ss[:, 2:3], None, ALU.mult),
                 reads=[b_xt, b_ss], writes=[b_xn])
            for q4 in range(4):
                pi = q4 % 2

                def tr(e, q4=q4, pi=pi):
                    for j in range(4):
                        kc = q4 * 4 + j
                        ins = e.transpose(psb[pi][:, j * 128:(j + 1) * 128], xn[:, kc * 128:(kc + 1) * 128], identb)
                    return ins
                S.op("pe", tr, reads=[b_xn, b_cstb], writes=[b_psb[pi]])
                for j in range(4):
                    kc = q4 * 4 + j
                    S.op("act", lambda e, kc=kc, j=j, pi=pi: e.activation(
                        hT[:, kc, col0:col0 + 128], psb[pi][:, j * 128:(j + 1) * 128], AF.Identity,
                        bias=B[:, kc, seg:seg + 1], scale=A[:, kc, seg:seg + 1]),
                        reads=[b_psb[pi], b_A, b_modT], writes=[b_hT])

        def nt_resources(st_):
            junk = sbt(st_, "nt_junk", [128, D], BF16); ss = sbt(st_, "nt_ss", [128, 4])
            xn = sbt(st_, "nt_xn", [128, D], BF16)
            psb = [pst(st_, f"nt_psb{i}", [128, 1024], BF16) for i in range(2)]
            return (junk, Buf(), ss, Buf(), xn, Buf(), psb, [Buf(), Buf()])

        def fm_rstd(chunks, b_in, nfeat, sqt, b_sq, ps, b_ps, rst, b_rst):
            for ci, (ap, rows) in enumerate(chunks):
                S.op("act", lambda e, ap=ap, rows=rows, ci=ci: e.activation(sqt[0:rows, ci, :], ap, AF.Square),
                     reads=b_in, writes=[b_sq])

            def mm(e):
                for ci, (ap, rows) in enumerate(chunks):
                    ins = e.matmul(ps[:], onesb[0:rows, :], sqt[0:rows, ci, :], start=(ci == 0), stop=(ci == len(chunks) - 1))
                return ins
            S.op("pe", mm, reads=[b_sq, b_cstb], writes=[b_ps])
            S.op("dve", lambda e: e.tensor_scalar(rst[:], ps[:], 1.0 / nfeat, EPS, ALU.mult, ALU.add), reads=[b_ps], writes=[b_rst])
            S.op("act", lambda e: e.sqrt(rst[:], rst[:]), reads=[b_rst], writes=[b_rst])
            S.op("dve", lambda e: e.reciprocal(rst[:], rst[:]), reads=[b_rst], writes=[b_rst])

        b_QmT, b_KmT, b_Vm, b_RQT, b_RKT, b_RV, b_RG = (Buf() for _ in range(7))
        with ExitStack() as st:
            ntres = nt_resources(st)
            xt = [sbt(st, f"xt{i}", [128, D]) for i in range(2)]; b_xt = [Buf(), Buf()]
            hT = sbt(st, "hT", [128, KC, T], BF16); b_hT = Buf()
            wt = [sbt(st, f"wt{i}", [128, KC, 512], BF16) for i in range(2)]; b_wt = [Buf(), Buf()]
            wuq = sbt(st, "wuq", [128, 4, 1536], BF16); wuk = sbt(st, "wuk", [128, 2, 1024], BF16)
            wuv = sbt(st, "wuv", [128, 2, 1024], BF16); b_wu = Buf()
            S.dma("sp", wuq[:], WUQ.rearrange("(c p) n -> p c n", p=128), reads=[b_w], writes=[b_wu])
            S.dma("sp", wuk[:], WUK.rearrange("(c p) n -> p c n", p=128), reads=[b_w], writes=[b_wu])
            S.dma("sp", wuv[:], WUV.rearrange("(c p) n -> p c n", p=128), reads=[b_w], writes=[b_wu])
            tab = sbt(st, "tab", [128, 6, T]); b_tab = Buf()
            ps = [pst(st, f"p1ps{i}", [128, 512]) for i in range(6)]; b_ps = [Buf() for _ in range(6)]
            cqf = sbt(st, "cqf", [128, 4, T]); b_cqf = Buf()
            cqn = sbt(st, "cqn", [128, 4, T], BF16); b_cqn = Buf()
            ckf = sbt(st, "ckf", [128, 2, T]); b_ckf = Buf()
            ckn = sbt(st, "ckn", [128, 2, T], BF16); b_ckn = Buf()
            krf = sbt(st, "krf", [64, T]); b_krf = Buf()
            sqt = sbt(st, "sqt", [128, 4, T], BF16); b_sq = Buf()
            sqkr = sbt(st, "sqkr", [64, T], BF16); b_sqkr = Buf()
            rst = sbt(st, "rst", [128, T]); b_rst = Buf()
            hf = [sbt(st, f"hf{i}", [128, T]) for i in range(2)]; b_hf = [Buf(), Buf()]
            hr = [sbt(st, f"hr{i}", [64, T]) for i in range(2)]; b_hr = [Buf(), Buf()]
            hb = [sbt(st, f"hb{i}", [128, T], BF16) for i in range(2)]; b_hb = [Buf(), Buf()]
            hrb = [sbt(st, f"hrb{i}", [64, T], BF16) for i in range(2)]; b_hrb = [Buf(), Buf()]
            t1 = [sbt(st, f"t1_{i}", [128, T]) for i in range(2)]; b_t1 = [Buf(), Buf()]
            t2 = [sbt(st, f"t2_{i}", [128, T]) for i in range(2)]; b_t2 = [Buf(), Buf()]
            ob = [sbt(st, f"ob{i}", [128, T], BF16) for i in range(3)]; b_ob = [Buf() for _ in range(3)]
            vt = [sbt(st, f"vt{i}", [128, 1024], BF16) for i in range(2)]; b_vt = [Buf(), Buf()]
            cnt = {"ps": 0, "w": 0, "h": 0, "o": 0, "v": 0, "x": 0}

            def nxt(key, n):
                i = cnt[key] % n
                cnt[key] += 1
                return i

            WINv = WIN.rearrange("(c p) n -> p c n", p=128)

            def load_w(c0, w):
                i = nxt("w", 2)
                S.dma("sp", wt[i][:, :, 0:w], WINv[:, :, c0:c0 + w], reads=[b_w], writes=[b_wt[i]])
                return i

            def rope_store(src_f, b_src, rows, ctab, stab, perm, dst, b_dst, scale_ap=None, rstd=None):
                hi = nxt("h", 2)
                xb = hb[hi] if rows == 128 else hrb[hi]
                b_xb = b_hb[hi] if rows == 128 else b_hrb[hi]
                if rstd is not None:
                    S.op("dve", lambda e: e.scalar_tensor_tensor(xb[0:rows, :], src_f, scale_ap, rstd[0:rows, :], ALU.mult, ALU.mult),
                         reads=b_src + [b_rst, b_vec], writes=[b_xb])
                else:
                    S.op("act", lambda e: e.copy(xb[0:rows, :], src_f), reads=b_src, writes=[b_xb])
                pi = nxt("ps", len(ps))
                S.op("pe", lambda e: e.matmul(ps[pi][0:rows, :], perm[0:rows, 0:rows], xb[0:rows, :], start=True, stop=True),
                     reads=[b_xb, b_cstb], writes=[b_ps[pi]])
                S.op("dve", lambda e: e.tensor_tensor(t1[hi][0:rows, :], xb[0:rows, :], ctab[0:rows, :], ALU.mult),
                     reads=[b_xb, b_tab], writes=[b_t1[hi]])
                S.op("dve", lambda e: e.tensor_tensor(t2[hi][0:rows, :], ps[pi][0:rows, :], stab[0:rows, :], ALU.mult),
                     reads=[b_ps[pi], b_tab], writes=[b_t2[hi]])
                oi = nxt("o", 3)
                S.op("pool", lambda e: e.tensor_tensor(ob[oi][0:rows, :], t1[hi][0:rows, :], t2[hi][0:rows, :], ALU.add),
                     reads=[b_t1[hi], b_t2[hi]], writes=[b_ob[oi]])
                S.dma("pool", dst, ob[oi][0:rows, :], reads=[b_ob[oi]], writes=[b_dst])

            def norm_store(src_f, b_src, scale_ap, rstd, dst, b_dst):
                oi = nxt("o", 3)
                S.op("dve", lambda e: e.scalar_tensor_tensor(ob[oi][:], src_f, scale_ap, rstd[:], ALU.mult, ALU.mult),
                     reads=b_src + [b_rst, b_vec], writes=[b_ob[oi]])
                S.dma("pool", dst, ob[oi][:], reads=[b_ob[oi]], writes=[b_dst])

            def proj_fm(wi, c_in_blk, M, pi, rows0=0):
                def mm(e):
                    for kc in range(KC):
                        ins = e.matmul(ps[pi][rows0:rows0 + M, :], wt[wi][:, kc, c_in_blk:c_in_blk + M], hT[:, kc, :],
                                       start=(kc == 0), stop=(kc == KC - 1))
                    return ins
                S.op("pe", mm, reads=[b_wt[wi], b_hT], writes=[b_ps[pi]])

            tiles = [(t0, si, True) for si, (s0, sl) in enumerate(SEGS) for t0 in range(s0, s0 + sl, T)]
            tiles += [(t0, 0, False) for t0 in range(NOWN, NTOK, T)]
            class Pipe:
                def __init__(self, lag=1):
                    self.q = []; self.lag = lag

                def unit(self, head, tail=None):
                    head()
                    self.q.append(tail)
                    while len(self.q) > self.lag:
                        t_ = self.q.pop(0)
                        if t_ is not None:
                            t_()

                def flush(self):
                    while self.q:
                        t_ = self.q.pop(0)
                        if t_ is not None:
                            t_()
            pipe = Pipe(1)
            NPS = len(ps)
            cnt["hf"] = 0
            for (t0, seg, own) in tiles:
                for s in range(4):
                    xi = nxt("x", 2)
                    S.dma("sp", xt[xi][:], xin[t0 + s * 128:t0 + (s + 1) * 128, :], writes=[b_xt[xi]])
                    norm_transpose(st, xt[xi], b_xt[xi], A1, B1, seg, hT, b_hT, s * 128, ntres)
                for ti, tb in enumerate((cosM, sinM, cosR, sinR, cosRk, sinRk)):
                    S.dma("sp", tab[:, ti, :], tb[:, t0:t0 + T], writes=[b_tab])

                def cq_head():
                    wi = load_w(0, 512)
                    for j in range(4):
                        pi = nxt("ps", NPS)
                        proj_fm(wi, j * 128, 128, pi)
                        S.op("act", lambda e, j=j, pi=pi: e.copy(cqf[:, j, :], ps[pi][:]), reads=[b_ps[pi]], writes=[b_cqf])

                def cq_tail():
                    pi = nxt("ps", NPS)
                    fm_rstd([(cqf[:, j, :], 128) for j in range(4)], [b_cqf], 512, sqt, b_sq, ps[pi], b_ps[pi], rst, b_rst)
                    for j in range(4):
                        S.op("dve", lambda e, j=j: e.scalar_tensor_tensor(cqn[:, j, :], cqf[:, j, :], qanT[:, j:j + 1], rst[:], ALU.mult, ALU.mult),
                             reads=[b_cqf, b_rst, b_vec], writes=[b_cqn])

                def ckv_head():
                    wi = load_w(512, 320)
                    for j in range(2):
                        pi = nxt("ps", NPS)
                        proj_fm(wi, j * 128, 128, pi)
                        S.op("act", lambda e, j=j, pi=pi: e.copy(ckf[:, j, :], ps[pi][:]), reads=[b_ps[pi]], writes=[b_ckf])
                    pi = nxt("ps", NPS)
                    proj_fm(wi, 256, 64, pi)
                    S.op("act", lambda e, pi=pi: e.copy(krf[:], ps[pi][0:64, :]), reads=[b_ps[pi]], writes=[b_krf])
                    S.op("act", lambda e: e.activation(sqkr[:], krf[:], AF.Square), reads=[b_krf], writes=[b_sqkr])

                def ckv_tail():
                    pi = nxt("ps", NPS)
                    fm_rstd([(ckf[:, j, :], 128) for j in range(2)], [b_ckf], 256, sqt, b_sq, ps[pi], b_ps[pi], rst, b_rst)
                    for j in range(2):
                        S.op("dve", lambda e, j=j: e.scalar_tensor_tensor(ckn[:, j, :], ckf[:, j, :], kvanT[:, j:j + 1], rst[:], ALU.mult, ALU.mult),
                             reads=[b_ckf, b_rst, b_vec], writes=[b_ckn])

                if own:
                    pipe.unit(cq_head, cq_tail)
                pipe.unit(ckv_head, ckv_tail)
                if not own:
                    pipe.flush()

                def q_unit(h):
                    slot = {}

                    def head():
                        hi = nxt("hf", 2); slot["hi"] = hi
                        pn = nxt("ps", NPS)

                        def mmq(e):
                            for k4 in range(4):
                                ins = e.matmul(ps[pn][:], wuq[:, k4, h * 192:h * 192 + 128], cqn[:, k4, :], start=(k4 == 0), stop=(k4 == 3))
                            return ins
                        S.op("pe", mmq, reads=[b_wu, b_cqn], writes=[b_ps[pn]])
                        S.op("act", lambda e: e.copy(hf[hi][:], ps[pn][:]), reads=[b_ps[pn]], writes=[b_hf[hi]])
                        pr = nxt("ps", NPS)

                        def mmr(e):
                            for k4 in range(4):
                                ins = e.matmul(ps[pr][0:64, :], wuq[:, k4, h * 192 + 128:h * 192 + 192], cqn[:, k4, :], start=(k4 == 0), stop=(k4 == 3))
                            return ins
                        S.op("pe", mmr, reads=[b_wu, b_cqn], writes=[b_ps[pr]])
                        S.op("act", lambda e: e.copy(hr[hi][:], ps[pr][0:64, :]), reads=[b_ps[pr]], writes=[b_hr[hi]])

                    def tail():
                        hi = slot["hi"]
                        pq = nxt("ps", NPS)
                        fm_rstd([(hf[hi][:], 128), (hr[hi][:], 64)], [b_hf[hi], b_hr[hi]], 192, sqt, b_sq, ps[pq], b_ps[pq], rst, b_rst)
                        norm_store(hf[hi][:], [b_hf[hi]], qnT[:, 0:1], rst, QmT[h, 0:128, t0:t0 + T], b_QmT)
                        rope_store(hr[hi][:], [b_hr[hi]], 64, tab[:, 0, :], tab[:, 1, :], PMb, QmT[h, 128:192, t0:t0 + T], b_QmT,
                                   scale_ap=qnT[0:64, 1:2], rstd=rst)
                    pipe.unit(head, tail)

                def k_unit(h):
                    slot = {}

                    def head():
                        hi = nxt("hf", 2); slot["hi"] = hi
                        pn = nxt("ps", NPS)

                        def mmk(e):
                            for k2 in range(2):
                                ins = e.matmul(ps[pn][:], wuk[:, k2, h * 128:(h + 1) * 128], ckn[:, k2, :], start=(k2 == 0), stop=(k2 == 1))
                            return ins
                        S.op("pe", mmk, reads=[b_wu, b_ckn], writes=[b_ps[pn]])
                        S.op("act", lambda e: e.copy(hf[hi][:], ps[pn][:]), reads=[b_ps[pn]], writes=[b_hf[hi]])

                    def tail():
                        hi = slot["hi"]
                        S.op("act", lambda e: e.activation(sqt[:, 0, :], hf[hi][:], AF.Square), reads=[b_hf[hi]], writes=[b_sq])
                        pq = nxt("ps", NPS)

                        def mms(e):
                            e.matmul(ps[pq][:], onesb, sqt[:, 0, :], start=True, stop=False)
                            return e.matmul(ps[pq][:], onesb[0:64, :], sqkr[:], start=False, stop=True)
                        S.op("pe", mms, reads=[b_sq, b_sqkr, b_cstb], writes=[b_ps[pq]])
                        S.op("dve", lambda e: e.tensor_scalar(rst[:], ps[pq][:], 1.0 / 192, EPS, ALU.mult, ALU.add), reads=[b_ps[pq]], writes=[b_rst])
                        S.op("act", lambda e: e.sqrt(rst[:], rst[:]), reads=[b_rst], writes=[b_rst])
                        S.op("dve", lambda e: e.reciprocal(rst[:], rst[:]), reads=[b_rst], writes=[b_rst])
                        norm_store(hf[hi][:], [b_hf[hi]], knT[:, 0:1], rst, KmT[h, 0:128, t0:t0 + T], b_KmT)
                        rope_store(krf[:], [b_krf], 64, tab[:, 0, :], tab[:, 1, :], PMb, KmT[h, 128:192, t0:t0 + T], b_KmT,
                                   scale_ap=knT[0:64, 1:2], rstd=rst)
                    pipe.unit(head, tail)

                if own:
                    for h in range(H):
                        q_unit(h)
                for h in range(H):
                    k_unit(h)

                def v_unit(s):
                    def head():
                        vi = nxt("v", 2)
                        for cb in range(2):
                            pi = nxt("ps", NPS)

                            def mmv(e, cb=cb, pi=pi):
                                for k2 in range(2):
                                    ins = e.matmul(ps[pi][:], ckn[:, k2, s * 128:(s + 1) * 128], wuv[:, k2, cb * 512:(cb + 1) * 512], start=(k2 == 0), stop=(k2 == 1))
                                return ins
                            S.op("pe", mmv, reads=[b_wu, b_ckn], writes=[b_ps[pi]])
                            S.op("act", lambda e, cb=cb, pi=pi: e.copy(vt[vi][:, cb * 512:(cb + 1) * 512], ps[pi][:]), reads=[b_ps[pi]], writes=[b_vt[vi]])
                        S.dma("pool", Vm[t0 + s * 128:t0 + (s + 1) * 128, :], vt[vi][:], reads=[b_vt[vi]], writes=[b_Vm])
                    pipe.unit(head, None)
                for s in range(4):
                    v_unit(s)

                def r_unit(which, wi, j, h):
                    slot = {}

                    def head():
                        pi = nxt("ps", NPS); slot["pi"] = pi
                        proj_fm(wi, j * 128, 128, pi)

                    def tail():
                        pi = slot["pi"]
                        if which == 0:
                            rope_store(ps[pi][:], [b_ps[pi]], 128, tab[:, 2, :], tab[:, 3, :], PRb, RQT[h, :, t0:t0 + T], b_RQT)
                        else:
                            rope_store(ps[pi][:], [b_ps[pi]], 128, tab[:, 4, :], tab[:, 5, :], PRb, RKT[h, :, t0:t0 + T], b_RKT)
                    pipe.unit(head, tail)
                for which in ((0, 1) if own else (1,)):
                    base = 832 if which == 0 else 1856
                    for blk in range(2):
                        wi = load_w(base + blk * 512, 512)
                        for j in range(4):
                            r_unit(which, wi, j, blk * 4 + j)

                wis = [load_w(2880, 512), load_w(2880 + 512, 512)]

                def rv_unit(s, wis=wis):
                    def head():
                        vi = nxt("v", 2)
                        for cb in range(2):
                            pi = nxt("ps", NPS)

                            def mmt(e, cb=cb, pi=pi):
                                for kc in range(KC):
                                    ins = e.matmul(ps[pi][:], hT[:, kc, s * 128:(s + 1) * 128], wt[wis[cb]][:, kc, :], start=(kc == 0), stop=(kc == KC - 1))
                                return ins
                            S.op("pe", mmt, reads=[b_hT, b_wt[wis[cb]]], writes=[b_ps[pi]])
                            S.op("act", lambda e, cb=cb, pi=pi: e.copy(vt[vi][:, cb * 512:(cb + 1) * 512], ps[pi][:]),
                                 reads=[b_ps[pi]], writes=[b_vt[vi]])
                        S.dma("pool", RV[t0 + s * 128:t0 + (s + 1) * 128, :], vt[vi][:], reads=[b_vt[vi]], writes=[b_RV])
                    pipe.unit(head, None)
                for s in range(4):
                    rv_unit(s)

                def rg_unit(wi, j, h):
                    def head():
                        pi = nxt("ps", NPS)
                        proj_fm(wi, j * 128, 128, pi)
                        oi = nxt("o", 3)
                        S.op("act", lambda e: e.activation(ob[oi][:], ps[pi][:], AF.Silu), reads=[b_ps[pi]], writes=[b_ob[oi]])
                        S.dma("pool", RGT[h, :, t0:t0 + T], ob[oi][:], reads=[b_ob[oi]], writes=[b_RG])
                    pipe.unit(head, None)
                if own:
                    for blk in range(2):
                        wi = load_w(3904 + blk * 512, 512)
                        for j in range(4):
                            rg_unit(wi, j, blk * 4 + j)
                pipe.flush()
            S.barrier()

        if upto <= 1:
            with ExitStack() as st:
                dbg = sbt(st, "dbg", [128, D])
                bd = Buf()
                for r0 in range(0, NOWN, 128):
                    S.dma("sp", dbg[:], xin[r0:r0 + 128, :], writes=[bd])
                    S.dma("sp", yout[r0:r0 + 128, :], dbg[:], reads=[bd])
                S.barrier()
            return nc

        b_MIXT = Buf()
        CTXM = 2 * SA if 2 * SA > SS else SS
        SQM = max(SA, SS)
        with ExitStack() as st:
            ps_s = [pst(st, f"a_s{i}", [128, 512]) for i in range(2)]; b_ps_s = [Buf(), Buf()]
            ps_o = pst(st, "a_o", [128, 512]); b_ps_o = Buf()
            ps_z = pst(st, "a_z", [128, 512]); b_ps_z = Buf()
            kn = [sbt(st, f"a_kn{i}", [128, CTXM], BF16) for i in range(2)]
            kr = [sbt(st, f"a_kr{i}", [64, CTXM], BF16) for i in range(2)]
            vv = [sbt(st, f"a_vv{i}", [128, CTXM // 128, 128], BF16) for i in range(2)]
            qn = [sbt(st, f"a_qn{i}", [128, SQM], BF16) for i in range(2)]
            qr = [sbt(st, f"a_qr{i}", [64, SQM], BF16) for i in range(2)]
            b_ld = [Buf(), Buf()]
            pt = [sbt(st, f"a_pt{i}", [128, 512], BF16) for i in range(3)]; b_pt = [Buf() for _ in range(3)]
            rs = sbt(st, "a_rs", [128, 512]); b_rs = Buf()
            on = [sbt(st, f"a_on{i}", [128, 512], BF16) for i in range(2)]; b_on = [Buf(), Buf()]
            it = 0; lc = 0; oc = 0
            sc_att = float(192 ** -0.5)
            for si, (s0, sl) in enumerate(SEGS):
                ranges = [(s0, sl)] + ([(NOWN, SA)] if si == 0 else [])
                ctx = sum(l for _, l in ranges)
                for h in range(H):
                    i = lc % 2; lc += 1
                    off = 0
                    for (c0, cl) in ranges:
                        S.dma("sp", kn[i][:, off:off + cl], KmT[h, 0:128, c0:c0 + cl], reads=[b_KmT], writes=[b_ld[i]])
                        S.dma("sp", kr[i][:, off:off + cl], KmT[h, 128:192, c0:c0 + cl], reads=[b_KmT], writes=[b_ld[i]])
                        S.dma("sp", vv[i][:, off // 128:(off + cl) // 128, :],
                              Vm[c0:c0 + cl, h * 128:(h + 1) * 128].rearrange("(c p) d -> p c d", p=128), reads=[b_Vm], writes=[b_ld[i]])
                        off += cl
                    S.dma("sp", qn[i][:, 0:sl], QmT[h, 0:128, s0:s0 + sl], reads=[b_QmT], writes=[b_ld[i]])
                    S.dma("sp", qr[i][:, 0:sl], QmT[h, 128:192, s0:s0 + sl], reads=[b_QmT], writes=[b_ld[i]])
                    for qt in range(sl // 512):
                        nk = ctx // 128
                        slots = {}

                        def e_mm1(kc, i=i, qt=qt):
                            nonlocal it
                            sidx = it % 2; pi = it % 3; it += 1
                            slots[kc] = (sidx, pi)

                            def mm1(e):
                                e.matmul(ps_s[sidx][:], kn[i][:, kc * 128:(kc + 1) * 128], qn[i][:, qt * 512:(qt + 1) * 512], start=True, stop=False)
                                return e.matmul(ps_s[sidx][:], kr[i][:, kc * 128:(kc + 1) * 128], qr[i][:, qt * 512:(qt + 1) * 512], start=False, stop=True)
                            S.op("pe", mm1, reads=[b_ld[i]], writes=[b_ps_s[sidx]])

                        def e_rest(kc, i=i, nk=nk):
                            sidx, pi = slots[kc]
                            S.op("act", lambda e: e.activation(pt[pi][:], ps_s[sidx][:], AF.Exp, scale=sc_att),
                                 reads=[b_ps_s[sidx]], writes=[b_pt[pi]])

                            def mm2(e):
                                e.matmul(ps_o[:], vv[i][:, kc, :], pt[pi][:], start=(kc == 0), stop=(kc == nk - 1))
                                return e.matmul(ps_z[:], onesb, pt[pi][:], start=(kc == 0), stop=(kc == nk - 1))
                            S.op("pe", mm2, reads=[b_ld[i], b_pt[pi], b_cstb], writes=[b_ps_o, b_ps_z])
                        e_mm1(0)
                        for kc in range(nk):
                            if kc + 1 < nk:
                                e_mm1(kc + 1)
                            e_rest(kc)
                        oi = oc % 2; oc += 1
                        S.op("dve", lambda e: e.reciprocal(rs[:], ps_z[:]), reads=[b_ps_z], writes=[b_rs])
                        S.op("dve", lambda e, oi=oi: e.tensor_tensor(on[oi][:], ps_o[:], rs[:], ALU.mult), reads=[b_ps_o, b_rs], writes=[b_on[oi]])
                        S.dma("pool", MIXT[h * 128:(h + 1) * 128, s0 + qt * 512:s0 + (qt + 1) * 512], on[oi][:], reads=[b_on[oi]], writes=[b_MIXT])
            S.barrier()
        if upto <= 2:
            return nc

        with ExitStack() as st:
            lg = sbt(st, "r_lg", [128, 16]); b_lg = Buf()
            nlgb = sbt(st, "r_nlgb", [128, 8])
            gnT = sbt(st, "r_gnT", [128, 8])
            pcol = sbt(st, "r_pcol", [128, NTOK // 128])
            S.dma("sp", lg[:], bass.AP(rdl.tensor, 0, [[0, 128], [1, 16]]), writes=[b_lg])
            with nc.allow_non_contiguous_dma(reason="tiny"):
                S.dma("sp", gnT[:], gn_w.rearrange("(h p) -> p h", p=128), writes=[b_lg])
            S.dma("sp", pcol[:], poscol, writes=[b_lg])
            S.op("act", lambda e: e.activation(lg[:], lg[:], AF.Sigmoid), reads=[b_lg], writes=[b_lg])
            S.op("act", lambda e: e.activation(lg[:], lg[:], AF.Ln), reads=[b_lg], writes=[b_lg])
            S.op("dve", lambda e: e.tensor_scalar(nlgb[:], lg[:, 8:16], -1.0, None, ALU.mult), reads=[b_lg], writes=[b_lg])
            ps_s = [pst(st, f"r_s{i}", [128, 512]) for i in range(2)]; b_ps_s = [Buf(), Buf()]
            ps_o = [pst(st, f"r_o{i}", [128, 512]) for i in range(4)]; b_ps_o = [Buf() for _ in range(4)]
            ps_m = [pst(st, f"r_m{i}", [128, 512]) for i in range(2)]; b_ps_m = [Buf(), Buf()]
            qq = [sbt(st, f"r_q{i}", [128, 4, 512], BF16) for i in range(2)]; gg = [sbt(st, f"r_g{i}", [128, 4, 512], BF16) for i in range(2)]
            prow = [sbt(st, f"r_pr{i}", [128, 512]) for i in range(2)]; b_q = [Buf(), Buf()]
            kk = [sbt(st, f"r_k{i}", [128, 4, 128], BF16) for i in range(3)]; vv = [sbt(st, f"r_v{i}", [128, 512], BF16) for i in range(3)]
            b_k = [Buf() for _ in range(3)]
            dd = [sbt(st, f"r_d{i}", [128, 512]) for i in range(2)]; dp = [sbt(st, f"r_dp{i}", [128, 512]) for i in range(2)]
            dn = [sbt(st, f"r_dn{i}", [128, 512]) for i in range(2)]; b_d = [Buf(), Buf()]
            a1 = [sbt(st, f"r_a{i}", [128, 512]) for i in range(2)]; b_a1 = [Buf(), Buf()]
            dm = [sbt(st, f"r_dm{i}", [128, 512]) for i in range(2)]; b_dm = [Buf(), Buf()]
            pt = [sbt(st, f"r_pt{i}", [128, 512], BF16) for i in range(3)]; b_pt = [Buf() for _ in range(3)]
            of = sbt(st, "r_of", [128, 512]); obf = sbt(st, "r_obf", [128, 512], BF16); sq = sbt(st, "r_sq", [128, 512], BF16); b_of = Buf()
            mu = sbt(st, "r_mu", [128, 512]); var = sbt(st, "r_var", [128, 512]); b_mu = Buf()
            ro = [sbt(st, f"r_ro{i}", [128, 512], BF16) for i in range(2)]; b_ro = [Buf(), Buf()]
            qc = 0; kcn = 0; it = 0; rc = 0
            for si, (s0, sl) in enumerate(SEGS):
                ranges = [(s0, sl)] + ([(NOWN, SA)] if si == 0 else [])
                chunks = [c0 + j * 128 for (c0, cl) in ranges for j in range(cl // 128)]
                for hg in range(2):
                    for qt in range(sl // 512):
                        q0 = s0 + qt * 512
                        qi = qc % 2; qc += 1
                        S.dma("sp", qq[qi][:], RQT[hg * 4:(hg + 1) * 4, :, q0:q0 + 512].rearrange("h p t -> p h t"), reads=[b_RQT], writes=[b_q[qi]])
                        S.dma("sp", gg[qi][:], RGT[hg * 4:(hg + 1) * 4, :, q0:q0 + 512].rearrange("h p t -> p h t"), reads=[b_RG], writes=[b_q[qi]])
                        S.dma("sp", prow[qi][:], posrow[:, q0:q0 + 512], writes=[b_q[qi]])
                        seq = [(ci, k0, hh) for ci, k0 in enumerate(chunks) for hh in range(4)]
                        cslot = {}
                        bslot = {}

                        def e_score(n, qi=qi, hg=hg):
                            nonlocal kcn, it
                            ci, k0, hh = seq[n]
                            if hh == 0:
                                ki = kcn % 3; di = kcn % 2; kcn += 1
                                cslot[ci] = (ki, di)
                                S.dma("sp", kk[ki][:], RKT[hg * 4:(hg + 1) * 4, :, k0:k0 + 128].rearrange("h p t -> p h t"), reads=[b_RKT], writes=[b_k[ki]])
                                S.dma("sp", vv[ki][:], RV[k0:k0 + 128, hg * 512:(hg + 1) * 512], reads=[b_RV], writes=[b_k[ki]])
                                S.op("dve", lambda e: e.tensor_scalar(dp[di][:], prow[qi][:], pcol[:, k0 // 128:k0 // 128 + 1], 0.0, ALU.subtract, ALU.max),
                                     reads=[b_q[qi], b_lg], writes=[b_d[di]])
                                S.op("dve", lambda e: e.tensor_scalar(dn[di][:], prow[qi][:], pcol[:, k0 // 128:k0 // 128 + 1], 0.0, ALU.subtract, ALU.min),
                                     reads=[b_q[qi], b_lg], writes=[b_d[di]])
                            ki, di = cslot[ci]
                            sidx = it % 2; pi = it % 3; it += 1
                            bslot[n] = (sidx, pi)
                            S.op("pe", lambda e: e.matmul(ps_s[sidx][:], kk[ki][:, hh, :], qq[qi][:, hh, :], start=True, stop=True),
                                 reads=[b_k[ki], b_q[qi]], writes=[b_ps_s[sidx]])

                        def e_rest(n, hg=hg, nch=len(chunks), q0=q0, s0=s0, sl=sl):
                            ci, k0, hh = seq[n]
                            h = hg * 4 + hh
                            ki, di = cslot[ci]
                            sidx, pi = bslot[n]
                            own_chunk = (s0 <= k0 < s0 + sl)
                            only_f = own_chunk and (k0 + 128 <= q0)
                            only_b = own_chunk and (k0 >= q0 + 512)
                            if only_f or only_b:
                                if only_f:
                                    S.op("act", lambda e: e.activation(a1[sidx][:], dp[di][:], AF.Exp, scale=lg[:, h:h + 1]),
                                         reads=[b_d[di], b_lg], writes=[b_a1[sidx]])
                                else:
                                    S.op("act", lambda e: e.activation(a1[sidx][:], dn[di][:], AF.Exp, scale=nlgb[:, h:h + 1]),
                                         reads=[b_d[di], b_lg], writes=[b_a1[sidx]])
                                S.op("dve", lambda e: e.tensor_tensor(pt[pi][:], a1[sidx][:], ps_s[sidx][:], ALU.mult),
                                     reads=[b_ps_s[sidx], b_a1[sidx]], writes=[b_pt[pi]])
                            else:
                                S.op("act", lambda e: e.activation(a1[sidx][:], dp[di][:], AF.Exp, scale=lg[:, h:h + 1]),
                                     reads=[b_d[di], b_lg], writes=[b_a1[sidx]])
                                S.op("act", lambda e: e.activation(dm[sidx][:], dn[di][:], AF.Exp, scale=nlgb[:, h:h + 1]),
                                     reads=[b_d[di], b_lg], writes=[b_dm[sidx]])
                                S.op("dve", lambda e: e.tensor_tensor(a1[sidx][:], a1[sidx][:], ps_s[sidx][:], ALU.mult),
                                     reads=[b_ps_s[sidx], b_a1[sidx]], writes=[b_a1[sidx]])
                                S.op("dve", lambda e: e.tensor_tensor(pt[pi][:], a1[sidx][:], dm[sidx][:], ALU.mult),
                                     reads=[b_a1[sidx], b_dm[sidx]], writes=[b_pt[pi]])
                            S.op("pe", lambda e: e.matmul(ps_o[hh][:], vv[ki][:, hh * 128:(hh + 1) * 128], pt[pi][:], start=(ci == 0), stop=(ci == nch - 1)),
                                 reads=[b_k[ki], b_pt[pi]], writes=[b_ps_o[hh]])
                        e_score(0)
                        for n in range(len(seq)):
                            if n + 1 < len(seq):
                                e_score(n + 1)
                            e_rest(n)
                        for hh in range(4):
                            h = hg * 4 + hh
                            S.op("act", lambda e, hh=hh: e.copy(of[:], ps_o[hh][:]), reads=[b_ps_o[hh]], writes=[b_of])
                            S.op("act", lambda e, hh=hh: e.activation(sq[:], ps_o[hh][:], AF.Square), reads=[b_ps_o[hh]], writes=[b_of])
                            S.op("dve", lambda e: e.tensor_copy(obf[:], of[:]), reads=[b_of], writes=[b_of])
                            S.op("pe", lambda e: e.matmul(ps_m[0][:], onesb, obf[:], start=True, stop=True), reads=[b_of, b_cstb], writes=[b_ps_m[0]])
                            S.op("pe", lambda e: e.matmul(ps_m[1][:], onesb, sq[:], start=True, stop=True), reads=[b_of, b_cstb], writes=[b_ps_m[1]])
                            S.op("dve", lambda e: e.tensor_scalar(mu[:], ps_m[0][:], 1.0 / 128, None, ALU.mult), reads=[b_ps_m[0]], writes=[b_mu])
                            S.op("dve", lambda e: e.tensor_tensor(var[:], mu[:], mu[:], ALU.mult), reads=[b_mu], writes=[b_mu])
                            S.op("dve", lambda e: e.scalar_tensor_tensor(var[:], ps_m[1][:], 1.0 / 128, var[:], ALU.mult, ALU.subtract), reads=[b_ps_m[1], b_mu], writes=[b_mu])
                            S.op("dve", lambda e: e.tensor_scalar(var[:], var[:], EPS, None, ALU.add), reads=[b_mu], writes=[b_mu])
                            S.op("act", lambda e: e.sqrt(var[:], var[:]), reads=[b_mu], writes=[b_mu])
                            S.op("dve", lambda e: e.reciprocal(var[:], var[:]), reads=[b_mu], writes=[b_mu])
                            S.op("dve", lambda e: e.tensor_tensor(of[:], of[:], mu[:], ALU.subtract), reads=[b_of, b_mu], writes=[b_of])
                            S.op("dve", lambda e: e.tensor_tensor(of[:], of[:], var[:], ALU.mult), reads=[b_of, b_mu], writes=[b_of])
                            ri = rc % 2; rc += 1
                            S.op("dve", lambda e, ri=ri, h=h, hh=hh, qi=qi: e.scalar_tensor_tensor(ro[ri][:], of[:], gnT[:, h:h + 1], gg[qi][:, hh, :], ALU.mult, ALU.mult),
                                 reads=[b_of, b_lg, b_q[qi]], writes=[b_ro[ri]])
                            S.dma("pool", MIXT[1024 + h * 128:1024 + (h + 1) * 128, q0:q0 + 512], ro[ri][:], reads=[b_ro[ri]], writes=[b_MIXT])
            S.barrier()
        if upto <= 3:
            return nc

        b_XMID, b_H2T, b_GT, b_SC = Buf(), Buf(), Buf(), Buf()
        with ExitStack() as st:
            ntres = nt_resources(st)
            ps = [pst(st, f"p4ps{i}", [128, 512]) for i in range(3)]; b_ps = [Buf() for _ in range(3)]
            psb = ntres[6]; b_psb = ntres[7]
            xs = [sbt(st, f"p4x{i}", [128, D]) for i in range(4)]; b_xs = [Buf() for _ in range(4)]
            mixT = sbt(st, "p4mix", [128, KC, T], BF16); b_mix = Buf()
            h2T = mixT; b_h2 = b_mix
            wt = [sbt(st, f"p4w{i}", [128, KC, 512], BF16) for i in range(2)]; b_wt = [Buf(), Buf()]
            g1 = sbt(st, "p4g1", [128, D]); b_g1 = Buf()
            tmp = [sbt(st, f"p4t{i}", [128, 512]) for i in range(2)]; b_tmp = [Buf(), Buf()]
            qp = sbt(st, "p4qp", [128, 16, T], BF16); b_qp = Buf()
            skn = sbt(st, "p4skn", [128, 16, 128], BF16); skT = sbt(st, "p4skT", [128, 16, 128], BF16); b_sk = Buf()
            S.dma("sp", skn[:], SKB.rearrange("(c p) d -> p c d", p=128), reads=[b_w], writes=[b_sk])
            for q4 in range(4):
                def trs(e, q4=q4):
                    for j in range(4):
                        ins = e.transpose(psb[0][:, j * 128:(j + 1) * 128], skn[:, q4 * 4 + j, :], identb)
                    return ins
                S.op("pe", trs, reads=[b_sk, b_cstb], writes=[b_psb[0]])
                S.op("act", lambda e, q4=q4: e.copy(skT[:, q4 * 4:(q4 + 1) * 4, :], psb[0][:, 0:512].rearrange("p (j k) -> p j k", j=4)),
                     reads=[b_psb[0]], writes=[b_sk])
            sc = sbt(st, "p4sc", [128, 16, 128]); b_sc = Buf()
            wc = 0; pc = 0; tc_ = 0; gc = 0
            WOv = WOB.rearrange("(c p) n -> p c n", p=128); WQv = WQB.rearrange("(c p) n -> p c n", p=128)
            own_tiles = [(t0, si) for si, (s0, sl) in enumerate(SEGS) for t0 in range(s0, s0 + sl, T)]
            for (t0, seg) in own_tiles:
                S.dma("sp", mixT[:], MIXT[:, t0:t0 + T].rearrange("(c p) t -> p c t", p=128), reads=[b_MIXT], writes=[b_mix])
                S.dma("sp", g1[:], bass.AP(MOD.tensor, seg * 6 * D + 2 * D, [[0, 128], [1, D]]), reads=[b_MOD], writes=[b_g1])
                for s in range(4):
                    S.dma("sp", xs[s][:], xin[t0 + s * 128:t0 + (s + 1) * 128, :], writes=[b_xs[s]])
                for cb in range(4):
                    wi = wc % 2; wc += 1
                    S.dma("sp", wt[wi][:], WOv[:, :, cb * 512:(cb + 1) * 512], reads=[b_w], writes=[b_wt[wi]])
                    for s in range(4):
                        pi = pc % 3; pc += 1; ti = tc_ % 2; tc_ += 1

                        def mmo(e, s=s, wi=wi, pi=pi):
                            for kc in range(KC):
                                ins = e.matmul(ps[pi][:], mixT[:, kc, s * 128:(s + 1) * 128], wt[wi][:, kc, :], start=(kc == 0), stop=(kc == KC - 1))
                            return ins
                        S.op("pe", mmo, reads=[b_mix, b_wt[wi]], writes=[b_ps[pi]])
                        S.op("dve", lambda e, pi=pi, ti=ti, cb=cb: e.tensor_tensor(tmp[ti][:], ps[pi][:], g1[:, cb * 512:(cb + 1) * 512], ALU.mult),
                             reads=[b_ps[pi], b_g1], writes=[b_tmp[ti]])
                        S.op("pool", lambda e, s=s, ti=ti, cb=cb: e.tensor_tensor(xs[s][:, cb * 512:(cb + 1) * 512], tmp[ti][:], xs[s][:, cb * 512:(cb + 1) * 512], ALU.add),
                             reads=[b_tmp[ti], b_xs[s]], writes=[b_xs[s]])
                for s in range(4):
                    S.dma("pool", XMID[t0 + s * 128:t0 + (s + 1) * 128, :], xs[s][:], reads=[b_xs[s]], writes=[b_XMID])
                    norm_transpose(st, xs[s], b_xs[s], A2, B2, seg, h2T, b_h2, s * 128, ntres)
                S.dma("pool", H2T[:, t0:t0 + T].rearrange("(c p) t -> p c t", p=128), h2T[:], reads=[b_h2], writes=[b_H2T])
                for blk in range(4):
                    wi = wc % 2; wc += 1
                    S.dma("sp", wt[wi][:], WQv[:, :, blk * 512:(blk + 1) * 512], reads=[b_w], writes=[b_wt[wi]])
                    for j in range(4):
                        pi = pc % 3; pc += 1

                        def mmq(e, j=j, wi=wi, pi=pi):
                            for kc in range(KC):
                                ins = e.matmul(ps[pi][:], wt[wi][:, kc, j * 128:(j + 1) * 128], h2T[:, kc, :], start=(kc == 0), stop=(kc == KC - 1))
                            return ins
                        S.op("pe", mmq, reads=[b_h2, b_wt[wi]], writes=[b_ps[pi]])
                        S.op("act", lambda e, pi=pi, c16=blk * 4 + j: e.copy(qp[:, c16, :], ps[pi][:]), reads=[b_ps[pi]], writes=[b_qp])
                for s in range(4):
                    ts0 = t0 + s * 128
                    for g4 in range(4):
                        pi = pc % 3; pc += 1

                        def mms(e, g4=g4, s=s, pi=pi):
                            for j in range(4):
                                c16 = g4 * 4 + j
                                ins = e.matmul(ps[pi][:, j * 128:(j + 1) * 128], qp[:, c16, s * 128:(s + 1) * 128], skT[:, c16, :], start=True, stop=True)
                            return ins
                        S.op("pe", mms, reads=[b_qp, b_sk], writes=[b_ps[pi]])
                        S.op("act", lambda e, g4=g4, pi=pi: e.copy(sc[:, g4 * 4:(g4 + 1) * 4, :], ps[pi][:].rearrange("p (j k) -> p j k", j=4)),
                             reads=[b_ps[pi]], writes=[b_sc])
                    S.dma("pool", SC[ts0:ts0 + 128, :, :], sc[:], reads=[b_sc], writes=[b_SC])
            S.barrier()
        if upto <= 4:
            return nc

        with ExitStack() as st:
            ps_t = pst(st, "g_pt", [128, 512]); b_ps_t = Buf()
            ps_g = [pst(st, f"g_pg{i}", [128, 512]) for i in range(3)]; b_ps_g = [Buf() for _ in range(3)]
            sc = [sbt(st, f"g_sc{i}", [128, 16, 128]) for i in range(2)]; b_sc = [Buf(), Buf()]
            top = sbt(st, "g_top", [128, 16, 16]); mrt = [sbt(st, f"g_mrt{i}", [128, 256]) for i in range(2)]; b_mrt = [Buf(), Buf()]; tops = sbt(st, "g_tops", [128, 16, 16]); b_top = Buf()
            cand = sbt(st, "g_cand", [128, 8, 256]); ce = sbt(st, "g_ce", [128, 8, 256]); b_cand = Buf()
            c8 = [sbt(st, f"g_c8{i}", [128, 16]) for i in range(2)]; b_c8 = [Buf(), Buf()]; th = sbt(st, "g_th", [128, 8]); zz = sbt(st, "g_z", [128, 8]); rz = sbt(st, "g_rz", [128, 8]); b_th = Buf()
            tk = sbt(st, "g_tk", [128, 3, 128]); b_tk = Buf()
            tkT = [sbt(st, f"g_tkT{i}", [128, 3, 128]) for i in range(2)]; b_tkT = [Buf(), Buf()]
            s1r = [sbt(st, f"g_s1r{i}", [128, 32, 128]) for i in range(2)]; s2r = [sbt(st, f"g_s2r{i}", [128, 32, 128]) for i in range(2)]
            b_rep = [Buf(), Buf()]
            e2 = sbt(st, "g_e2", [128, 32, 128], BF16); b_e2 = Buf()
            kapb = [sbt(st, f"g_kapb{i}", [128, 128], BF16) for i in range(2)]
            mk = sbt(st, "g_mk", [128, 32, 128], BF16); b_mk = Buf()
            eq = sbt(st, "g_eq", [128, 32, 128], BF16); b_eq = Buf()
            Lt = [sbt(st, f"g_L{i}", [128, 32, 128], BF16) for i in range(2)]; b_L = [Buf(), Buf()]
            Rt = [sbt(st, f"g_R{i}", [128, 32, 128], BF16) for i in range(2)]; b_R = [Buf(), Buf()]
            gts = sbt(st, "g_gts", [128, 128, 128], BF16); b_gts = Buf()
            b_SCS = [Buf() for _ in range(4)]
            cnts = [0, 0]
            def stageA(sti):
                ts0 = sti * 128
                i = sti % 2
                scc = sc[i]
                S.dma("sp", scc[:], SC[ts0:ts0 + 128, :, :], reads=[b_SC], writes=[b_sc[i]])
                for c0 in range(0, 16, 2):
                    for c16 in (c0, c0 + 1):
                        S.op("dve", lambda e, c16=c16: e.max(top[:, c16, 0:8], scc[:, c16, :]), reads=[b_sc[i]], writes=[b_top])
                    for c16 in (c0, c0 + 1):
                        S.op("dve", lambda e, c16=c16: e.match_replace(mrt[c16 % 2][:, 0:128], top[:, c16, 0:8], scc[:, c16, :], -1e30), reads=[b_sc[i], b_top], writes=[b_mrt[c16 % 2]])
                    for c16 in (c0, c0 + 1):
                        S.op("dve", lambda e, c16=c16: e.max(top[:, c16, 8:16], mrt[c16 % 2][:, 0:128]), reads=[b_mrt[c16 % 2]], writes=[b_top])
                    yield
                S.op("dve", lambda e: e.tensor_tensor(scc[:], scc[:], apx(top[:, 0, 0:1], [[16, 16], [0, 128]]), ALU.subtract), reads=[b_sc[i], b_top], writes=[b_sc[i]])
                S.dma("act", SCS[:, ts0:ts0 + 128, :].rearrange("c t k -> t c k"), scc[:], reads=[b_sc[i]], writes=[b_SCS[sti % 4]])
                S.op("dve", lambda e: e.tensor_tensor(tops[:], top[:], apx(top[:, 0, 0:1], [[16, 16], [0, 16]]), ALU.subtract), reads=[b_top], writes=[b_top])
                S.op("dve", lambda e: e.tensor_tensor(apx(cand[:, 0, 0:1], [[256, 8], [16, 16], [1, 16]]),
                                                      apx(tops[:, 0, 0:1], [[32, 8], [1, 16], [0, 16]]),
                                                      apx(tops[:, 1, 0:1], [[32, 8], [0, 16], [1, 16]]), ALU.add),
                     reads=[b_top], writes=[b_cand])
                yield
                for h0 in range(0, H, 2):
                    for h in (h0, h0 + 1):
                        S.op("dve", lambda e, h=h: e.max(c8[h % 2][:, 0:8], cand[:, h, :]), reads=[b_cand], writes=[b_c8[h % 2]])
                    for h in (h0, h0 + 1):
                        S.op("dve", lambda e, h=h: e.match_replace(mrt[h % 2][:], c8[h % 2][:, 0:8], cand[:, h, :], -1e30), reads=[b_cand, b_c8[h % 2]], writes=[b_mrt[h % 2]])
                    for h in (h0, h0 + 1):
                        S.op("dve", lambda e, h=h: e.max(c8[h % 2][:, 8:16], mrt[h % 2][:]), reads=[b_mrt[h % 2]], writes=[b_c8[h % 2]])
                    for h in (h0, h0 + 1):
                        S.op("dve", lambda e, h=h: e.tensor_copy(th[:, h:h + 1], c8[h % 2][:, 15:16]), reads=[b_c8[h % 2]], writes=[b_th])
                    yield
                S.op("act", lambda e: e.activation(ce[:], cand[:], AF.Exp), reads=[b_cand], writes=[b_cand])
                S.op("dve", lambda e: e.tensor_tensor(cand[:], cand[:], apx(th[:, 0:1], [[1, 8], [0, 256]]), ALU.is_ge), reads=[b_cand, b_th], writes=[b_cand])
                S.op("dve", lambda e: e.tensor_tensor(ce[:], ce[:], cand[:], ALU.mult), reads=[b_cand], writes=[b_cand])
                S.op("dve", lambda e: e.tensor_reduce(zz[:], ce[:], AX.X, ALU.add), reads=[b_cand, b_th], writes=[b_th])
                S.op("dve", lambda e: e.reciprocal(rz[:], zz[:]), reads=[b_th], writes=[b_th])
                s1tops = apx(tops[:, 0, 0:1], [[32, 8], [1, 16]])
                tk3 = lambda j: tk[:, j, :].rearrange("p (h a) -> p h a", h=8)
                S.op("dve", lambda e: e.tensor_tensor(tk3(0), apx(th[:, 0:1], [[1, 8], [0, 16]]), s1tops, ALU.subtract), reads=[b_th, b_top], writes=[b_tk])
                S.op("dve", lambda e: e.tensor_scalar(tk[:, 0, :], tk[:, 0, :], -1e-5, None, ALU.add), reads=[b_tk], writes=[b_tk])
                S.op("act", lambda e: e.activation(tk3(1), s1tops, AF.Exp), reads=[b_top], writes=[b_tk])
                S.op("dve", lambda e: e.tensor_tensor(tk3(1), tk3(1), apx(rz[:, 0:1], [[1, 8], [0, 16]]), ALU.mult), reads=[b_tk, b_th], writes=[b_tk])
                S.op("dve", lambda e: e.tensor_copy(tk3(2), s1tops), reads=[b_top], writes=[b_tk])

                def trk(e):
                    for j in range(3):
                        ins = e.transpose(ps_t[:, j * 128:(j + 1) * 128], tk[:, j, :], identf)
                    return ins
                S.op("pe", trk, reads=[b_tk, b_cst], writes=[b_ps_t])
                S.op("act", lambda e: e.copy(tkT[i][:], ps_t[:, 0:384].rearrange("p (j t) -> p j t", j=3)), reads=[b_ps_t], writes=[b_tkT[i]])
                S.op("act", lambda e: e.copy(kapb[i][:], ps_t[:, 128:256]), reads=[b_ps_t], writes=[b_tkT[i]])
                yield
            def front(sti, g, genA):
                ts0 = sti * 128
                i = sti % 2
                r = cnts[0] % 2; cnts[0] += 1
                tg0 = ts0 + g * 32
                for h in range(H):
                    S.dma("sp", s1r[r][h * 16:(h + 1) * 16, :, :].rearrange("p t k -> p (t k)"), bass.AP(SCS.tensor, ((2 * h) * NOWN + tg0) * 128, [[0, 16], [1, 4096]]),
                          reads=[b_SCS[sti % 4]], writes=[b_rep[r]])
                    S.dma("sp", s2r[r][h * 16:(h + 1) * 16, :, :].rearrange("p t k -> p (t k)"), bass.AP(SCS.tensor, ((2 * h + 1) * NOWN + tg0) * 128, [[0, 16], [1, 4096]]),
                          reads=[b_SCS[sti % 4]], writes=[b_rep[r]])
                bc = lambda j: apx(tkT[i][:, j, g * 32:g * 32 + 1], [[1, 32], [0, 128]])
                S.op("act", lambda e: e.activation(e2[:], s2r[r][:], AF.Exp), reads=[b_rep[r]], writes=[b_e2])
                S.op("dve", lambda e: e.tensor_tensor(mk[:], s2r[r][:], bc(0), ALU.is_ge), reads=[b_rep[r], b_tkT[i]], writes=[b_mk])
                S.op("pool", lambda e: e.tensor_tensor(Lt[r][:], mk[:], e2[:], ALU.mult), reads=[b_mk, b_e2], writes=[b_L[r]])
                S.op("dve", lambda e: e.tensor_tensor(eq[:], s1r[r][:], bc(2), ALU.is_equal), reads=[b_rep[r], b_tkT[i]], writes=[b_eq])
                S.op("pool", lambda e: e.tensor_tensor(Rt[r][:], eq[:], apx(kapb[i][:, g * 32:g * 32 + 1], [[1, 32], [0, 128]]), ALU.mult),
                     reads=[b_eq, b_tkT[i]], writes=[b_R[r]])
                if genA is not None:
                    for _ in range(5):
                        next(genA, None)
                    if g == 3:
                        for _ in genA:
                            pass
                return r

            def back(sti, g, r):
                ts0 = sti * 128
                for t4 in range(8):
                    k = cnts[1] % 3; cnts[1] += 1

                    def mmg(e, t4=t4, k=k):
                        for j in range(4):
                            tl = t4 * 4 + j
                            ins = e.matmul(ps_g[k][:, j * 128:(j + 1) * 128], Lt[r][:, tl, :], Rt[r][:, tl, :], start=True, stop=True)
                        return ins
                    S.op("pe", mmg, reads=[b_L[r], b_R[r]], writes=[b_ps_g[k]])
                    col = g * 32 + t4 * 4
                    S.op("act", lambda e, k=k, col=col: e.copy(gts[:, :, col:col + 4], apx(ps_g[k][:, 0:1], [[1, 128], [128, 4]])),
                         reads=[b_ps_g[k]], writes=[b_gts])
                if g == 3:
                    S.dma("act", GT[sti], gts[:], reads=[b_gts], writes=[b_GT])

            nst = NOWN // 128
            for _ in stageA(0):
                pass
            groups = [(sti, g) for sti in range(nst) for g in range(4)]
            gens = {}
            prev = None
            for (sti, g) in groups:
                if g == 0:
                    gens[sti] = stageA(sti + 1) if sti + 1 < nst else None
                r = front(sti, g, gens[sti])
                if prev is not None:
                    back(*prev)
                prev = (sti, g, r)
            back(*prev)
            S.barrier()
        if upto <= 5:
            return nc

        with ExitStack() as st:
            psb = [pst(st, f"p5psb{i}", [128, 1024], BF16) for i in range(2)]; b_psb = [Buf(), Buf()]
            ps_a = [pst(st, f"p5a{i}", [128, 512]) for i in range(2)]; b_ps_a = [Buf(), Buf()]
            ps_o = [pst(st, f"p5o{i}", [128, 512]) for i in range(3)]; b_ps_o = [Buf() for _ in range(3)]
            b_UT = Buf()
            with ExitStack() as st2:
                un = [sbt(st2, f"p5un{i}", [128, 4, D], BF16) for i in range(2)]; b_un = [Buf(), Buf()]
                uts = [sbt(st2, f"p5uts{i}", [128, KC, 512], BF16) for i in range(2)]; b_uts = [Buf(), Buf()]
                tcn = 0
                for g in range(32):
                    i = g % 2
                    S.dma("sp", un[i][:], UB[g * 512:(g + 1) * 512, :].rearrange("(j p) d -> p j d", p=128), reads=[b_w], writes=[b_un[i]])
                    for kc in range(KC):
                        pi = tcn % 2; tcn += 1

                        def tru(e, i=i, kc=kc, pi=pi):
                            for j in range(4):
                                ins = e.transpose(psb[pi][:, j * 128:(j + 1) * 128], un[i][:, j, kc * 128:(kc + 1) * 128], identb)
                            return ins
                        S.op("pe", tru, reads=[b_un[i], b_cstb], writes=[b_psb[pi]])
                        eng = "act" if kc % 2 == 0 else "dve"
                        if eng == "act":
                            S.op("act", lambda e, i=i, kc=kc, pi=pi: e.copy(uts[i][:, kc, :], psb[pi][:, 0:512]), reads=[b_psb[pi]], writes=[b_uts[i]])
                        else:
                            S.op("dve", lambda e, i=i, kc=kc, pi=pi: e.tensor_copy(uts[i][:, kc, :], psb[pi][:, 0:512]), reads=[b_psb[pi]], writes=[b_uts[i]])
                    S.dma("sp", UT[:, g * 512:(g + 1) * 512].rearrange("(c p) e -> p c e", p=128), uts[i][:], reads=[b_uts[i]], writes=[b_UT])
            h2T = sbt(st, "p5h2", [128, KC, T], BF16); b_h2 = Buf()
            acc = sbt(st, "p5acc", [128, 4, D]); b_acc = Buf()
            ug = [sbt(st, f"p5ug{i}", [128, KC, 512], BF16) for i in range(2)]
            vg = [sbt(st, f"p5vg{i}", [128, 4, D], BF16) for i in range(2)]
            gg = [sbt(st, f"p5gg{i}", [128, 4 * T], BF16) for i in range(2)]; b_g = [Buf(), Buf()]
            ga = [sbt(st, f"p5ga{i}", [128, T]) for i in range(2)]; b_ga = [Buf(), Buf()]
            wT = [sbt(st, f"p5wT{i}", [128, 4, T], BF16) for i in range(2)]; b_wT = [Buf(), Buf()]
            xm = [sbt(st, f"p5xm{i}", [128, D]) for i in range(2)]; b_xm = [Buf(), Buf()]
            g2 = sbt(st, "p5g2", [128, D]); b_g2 = Buf()
            ac = 0; oc = 0; xc = 0
            for (t0, seg) in own_tiles:
                S.dma("sp", h2T[:], H2T[:, t0:t0 + T].rearrange("(c p) t -> p c t", p=128), reads=[b_H2T], writes=[b_h2])
                S.dma("sp", g2[:], bass.AP(MOD.tensor, seg * 6 * D + 5 * D, [[0, 128], [1, D]]), reads=[b_MOD], writes=[b_g2])
                for g in range(32):
                    i = g % 2
                    S.dma("sp", ug[i][:], UT[:, g * 512:(g + 1) * 512].rearrange("(c p) e -> p c e", p=128), reads=[b_UT], writes=[b_g[i]])
                    S.dma("sp", vg[i][:], VB[g * 512:(g + 1) * 512, :].rearrange("(j p) d -> p j d", p=128), reads=[b_w], writes=[b_g[i]])
                    for s4 in range(4):
                        S.dma("sp", gg[i][:, s4 * 512:(s4 + 1) * 512], GT[t0 // 128 + s4, :, g * 4:(g + 1) * 4, :].rearrange("p c t -> p (c t)"), reads=[b_GT], writes=[b_g[i]])
                    for j in range(4):
                        ai = ac % 2; ac += 1

                        def mma(e, i=i, j=j, ai=ai):
                            for kc in range(KC):
                                ins = e.matmul(ps_a[ai][:], ug[i][:, kc, j * 128:(j + 1) * 128], h2T[:, kc, :], start=(kc == 0), stop=(kc == KC - 1))
                            return ins
                        S.op("pe", mma, reads=[b_g[i], b_h2], writes=[b_ps_a[ai]])
                        S.op("act", lambda e, ai=ai: e.activation(ga[ai][:], ps_a[ai][:], AF.Gelu_apprx_tanh), reads=[b_ps_a[ai]], writes=[b_ga[ai]])
                        S.op("dve", lambda e, i=i, j=j, ai=ai: e.tensor_tensor(wT[i][:, j, :].rearrange("p (s t) -> p s t", s=4), ga[ai][:].rearrange("p (s t) -> p s t", s=4), apx(gg[i][:, j * 128:j * 128 + 1], [[512, 4], [1, 128]]), ALU.mult),
                             reads=[b_ga[ai], b_g[i]], writes=[b_wT[i]])
                    for s in range(4):
                        for cb in range(4):
                            oi = oc % 3; oc += 1

                            def mmo5(e, i=i, s=s, cb=cb, oi=oi):
                                for j in range(4):
                                    ins = e.matmul(ps_o[oi][:], wT[i][:, j, s * 128:(s + 1) * 128], vg[i][:, j, cb * 512:(cb + 1) * 512], start=(j == 0), stop=(j == 3))
                                return ins
                            S.op("pe", mmo5, reads=[b_wT[i], b_g[i]], writes=[b_ps_o[oi]])
                            eng = "dve" if (s * 4 + cb) % 2 == 0 else "pool"
                            dst = acc[:, s, cb * 512:(cb + 1) * 512]
                            if g == 0:
                                S.op("act", lambda e, dst=dst, oi=oi: e.copy(dst, ps_o[oi][:]), reads=[b_ps_o[oi]], writes=[b_acc])
                            else:
                                S.op("dve", lambda e, dst=dst, oi=oi: e.tensor_tensor(dst, dst, ps_o[oi][:], ALU.add), reads=[b_ps_o[oi], b_acc], writes=[b_acc])
                for s in range(4):
                    xi = xc % 2; xc += 1
                    S.dma("sp", xm[xi][:], XMID[t0 + s * 128:t0 + (s + 1) * 128, :], reads=[b_XMID], writes=[b_xm[xi]])
                    S.op("dve", lambda e, s=s: e.tensor_tensor(acc[:, s, :], acc[:, s, :], g2[:], ALU.mult), reads=[b_acc, b_g2], writes=[b_acc])
                    S.op("pool", lambda e, s=s, xi=xi: e.tensor_tensor(xm[xi][:], xm[xi][:], acc[:, s, :], ALU.add), reads=[b_acc, b_xm[xi]], writes=[b_xm[xi]])
                    S.dma("sp", yout[t0 + s * 128:t0 + (s + 1) * 128, :], xm[xi][:], reads=[b_xm[xi]])
            S.barrier()
    return nc


def _tables(pos):
    pos = pos.astype(np.float32)
    r = np.arange(128)
    invM = (1.0 / (np.float32(10000.0) ** (np.arange(0, 64, 2, dtype=np.float32) / np.float32(64)))).astype(np.float32)
    angM = (pos[None, :] * invM[r % 32][:, None]).astype(np.float32)
    sgnM = np.where((r % 64) < 32, -1.0, 1.0).astype(np.float32)[:, None]
    cosM = np.cos(angM).astype(np.float32); sinM = (np.sin(angM).astype(np.float32) * sgnM).astype(np.float32)
    invR = (1.0 / (np.float32(10000.0) ** (np.arange(0, 128, 2, dtype=np.float32) / np.float32(128)))).astype(np.float32)
    angR = (pos[None, :] * invR[r % 64][:, None]).astype(np.float32)
    sgnR = np.where(r < 64, -1.0, 1.0).astype(np.float32)[:, None]
    cosR = np.cos(angR).astype(np.float32); sinR = (np.sin(angR).astype(np.float32) * sgnR).astype(np.float32)
    ks = np.float32(128.0 ** -0.5)
    return cosM, sinM, cosR, sinR, (cosR * ks).astype(np.float32), (sinR * ks).astype(np.float32)


def _consts():
    cst = np.zeros((128, 10, 128), np.float32)
    m = np.arange(128)
    cst[:, 0, :] = np.eye(128)
    cst[:, 1, :] = 1.0
    cst[m ^ 32, 2, m] = 1.0
    cst[m ^ 64, 3, m] = 1.0
    return cst


_PROG = {}


def kernel(x_prompt, x_sample, c_prompt, c_sample, norm1_w, norm2_w, w_ada, b_ada, w_in, q_a_norm, kv_a_norm,
           w_uq, w_uk, w_uv, q_norm, k_norm, ret_decay_logit, ret_gn_w, w_o, peer_wq, peer_sub_keys, peer_u, peer_v,
           _upto=9, _dbg=(), _trace=False):
    f = lambda a: np.ascontiguousarray(np.asarray(a, dtype=np.float32))
    x_prompt, x_sample, c_prompt, c_sample = f(x_prompt), f(x_sample), f(c_prompt), f(c_sample)
    SEQ = x_prompt.shape[1]; SS = x_sample.shape[1]; SA = SEQ // 2
    NOWN = SA + 2 * SS; NTOK = NOWN + SA
    key = (SA, SS, _upto, tuple(_dbg))
    if key not in _PROG:
        _PROG[key] = build(SA, SS, _upto, _dbg)
    nc = _PROG[key]
    shared = {
        "norm1_w": f(norm1_w).reshape(-1), "norm2_w": f(norm2_w).reshape(-1), "w_ada": f(w_ada)[0], "b_ada": f(b_ada).reshape(-1),
        "w_in": f(w_in)[0], "q_a_norm": f(q_a_norm).reshape(-1), "kv_a_norm": f(kv_a_norm).reshape(-1),
        "w_uq": f(w_uq)[0], "w_uk": f(w_uk)[0], "w_uv": f(w_uv)[0], "q_norm": f(q_norm).reshape(-1), "k_norm": f(k_norm).reshape(-1),
        "ret_decay_logit": f(ret_decay_logit).reshape(-1), "ret_gn_w": f(ret_gn_w).reshape(-1), "w_o": f(w_o)[0],
        "peer_wq": f(peer_wq)[0], "peer_sub_keys": f(peer_sub_keys).reshape(2048, 128),
        "peer_u": f(peer_u)[0], "peer_v": f(peer_v)[0], "cst": _consts(),
    }
    in_maps = []
    for c in range(NCORES):
        pb, par = c // 2, c % 2
        own = x_prompt[pb, par * SA:(par + 1) * SA]; oth = x_prompt[pb, (1 - par) * SA:(2 - par) * SA]
        xin = np.concatenate([own, x_sample[2 * c], x_sample[2 * c + 1], oth], axis=0)
        pos = np.concatenate([par * SA + np.arange(SA), np.arange(SS), np.arange(SS), (1 - par) * SA + np.arange(SA)]).astype(np.float32)
        cosM, sinM, cosR, sinR, cosRk, sinRk = _tables(pos)
        m = dict(shared)
        m.update({"xin": np.ascontiguousarray(xin), "c3": np.stack([c_prompt[pb], c_sample[2 * c], c_sample[2 * c + 1]]),
                  "cosM": cosM, "sinM": sinM, "cosR": cosR, "sinR": sinR, "cosRk": cosRk, "sinRk": sinRk,
                  "posrow": np.ascontiguousarray(np.broadcast_to(pos[None, :], (128, NTOK))),
                  "poscol": np.ascontiguousarray(pos.reshape(NTOK // 128, 128).T),
                  "cvec": np.zeros((128, 4), np.float32)})
        in_maps.append(m)
    res = run_bass_kernel_spmd(nc, in_maps, core_ids=list(range(NCORES)), **({'trace': True} if _trace else {}))
    if _trace:
        print('EXEC_TIME_NS', _upto, res.exec_time_ns)
    yp = np.zeros(x_prompt.shape, np.float32); ys = np.zeros(x_sample.shape, np.float32)
    for c in range(NCORES):
        y = res.results[c]["yout"]
        pb, par = c // 2, c % 2
        yp[pb, par * SA:(par + 1) * SA] = y[0:SA]
        ys[2 * c] = y[SA:SA + SS]; ys[2 * c + 1] = y[SA + SS:SA + 2 * SS]
    if _dbg:
        return (yp, ys), res
    return (yp, ys)
```

```python
from contextlib import ExitStack
import numpy as np
import concourse.bass as bass
import concourse.mybir as mybir
from concourse.bass_utils import run_bass_kernel_spmd

F32 = mybir.dt.float32
BF16 = mybir.dt.bfloat16
ALU = mybir.AluOpType
AF = mybir.ActivationFunctionType
AX = mybir.AxisListType

D = 2048
KC = 16
T = 512
H = 8
EPS = 1e-6
NCORES = 8


class Buf:
    __slots__ = ("name", "w", "r")

    def __init__(self, name=""):
        self.name = name
        self.w = {}
        self.r = {}


class Sched:
    NDMA = 16
    SAME = {"act": True, "dve": True, "pool": True, "pe": False, "sp": True}

    def __init__(self, nc, stack):
        self.nc = nc
        self.stack = stack
        self.eng = {"pe": nc.tensor, "act": nc.scalar, "dve": nc.vector, "pool": nc.gpsimd, "sp": nc.sync}
        self.csem, self.ccnt = {}, {}
        self.known = {k: {} for k in self.eng}
        self.nsem = 0
        for k in self.eng:
            self._new_csem(k)
        self.dsem = {k: [] for k in self.eng}
        self.drr = {k: 0 for k in self.eng}
        self.n_inst = 0

    def _alloc_sem(self, name):
        self.nsem += 1
        return self.stack.enter_context(self.nc.semaphore(f"{name}_{self.nsem}"))

    def _new_csem(self, k):
        self.csem[k] = self._alloc_sem("c" + k)
        self.ccnt[k] = 0

    def _wait(self, k, tok, raw=True, fam=None, force=False):
        sem, val, src = tok
        if not force:
            if src == fam and not raw:
                return False
            if src == k and not self.SAME[k]:
                return False
        kn = self.known[k]
        if kn.get(id(sem), 0) >= val:
            return True
        self.eng[k].wait_ge(sem, val)
        self.n_inst += 1
        kn[id(sem)] = val
        return True

    def _deps(self, k, reads, writes, fam):
        for b in reads:
            for t in b.w.values():
                self._wait(k, t, True, fam)
        for b in writes:
            for t in b.w.values():
                self._wait(k, t, False, fam)
            for t in b.r.values():
                self._wait(k, t, False, fam)

    def _commit(self, tok, reads, writes, fam):
        for b in writes:
            b.w = {i: t for i, t in b.w.items() if t[2] == fam}
            b.w[id(tok[0])] = tok
            b.r = {}
        for b in reads:
            if id(tok[0]) in b.w and b.w[id(tok[0])] is tok:
                continue
            b.r[id(tok[0])] = tok

    def op(self, k, fn, reads=(), writes=()):
        self._deps(k, reads, writes, k)
        ins = fn(self.eng[k])
        self.n_inst += 1
        if self.ccnt[k] >= 30000:
            self._new_csem(k)
        self.ccnt[k] += 1
        ins.then_inc(self.csem[k], 1)
        tok = (self.csem[k], self.ccnt[k], k)
        self._commit(tok, reads, writes, k)
        return tok

    def dma(self, k, out, in_, reads=(), writes=(), **kw):
        fam = "dma:" + k
        self._deps(k, reads, writes, fam)
        pool = self.dsem[k]
        if len(pool) < self.NDMA:
            pool.append([self._alloc_sem("d" + k), 0])
            ent = pool[-1]
        else:
            ent = pool[self.drr[k] % self.NDMA]
            self.drr[k] += 1
            self._wait(k, (ent[0], ent[1], fam), force=True)
        ent[1] += 16
        ins = self.eng[k].dma_start(out=out, in_=in_, **kw)
        ins.then_inc(ent[0], 16)
        self.n_inst += 1
        tok = (ent[0], ent[1], fam)
        self._commit(tok, reads, writes, fam)
        return tok

    def barrier(self, engines=None):
        toks = []
        for k in self.eng:
            if self.ccnt[k] > 0:
                toks.append((self.csem[k], self.ccnt[k], k))
            for ent in self.dsem[k]:
                if ent[1] > 0:
                    toks.append((ent[0], ent[1], "dma:" + k))
        for k in (engines or self.eng):
            for t in toks:
                self._wait(k, t, force=True)


def apx(ap, pattern):
    return bass.AP(ap.tensor, ap.offset, [list(ap.ap[0])] + [list(p) for p in pattern])


def build(SA, SS, upto=9, dbg_out=()):
    NOWN = SA + 2 * SS
    NTOK = NOWN + SA
    SEGS = [(0, SA), (SA, SS), (SA + SS, SS)]
    nc = bass.Bass("TRN2", target_bir_lowering=False)

    def din(name, shape, dt=F32):
        return nc.dram_tensor(name, list(shape), dt, kind="ExternalInput").ap()

    def dscr(name, shape, dt=BF16):
        return nc.dram_tensor(name, list(shape), dt).ap()

    xin = din("xin", [NTOK, D])
    c3 = din("c3", [3, D])
    norm1_w = din("norm1_w", [D]); norm2_w = din("norm2_w", [D])
    w_ada = din("w_ada", [D, 6 * D]); b_ada = din("b_ada", [6 * D])
    w_in = din("w_in", [D, 4928])
    q_a_norm = din("q_a_norm", [512]); kv_a_norm = din("kv_a_norm", [256])
    w_uq = din("w_uq", [512, 1536]); w_uk = din("w_uk", [256, 1024]); w_uv = din("w_uv", [256, 1024])
    q_norm = din("q_norm", [192]); k_norm = din("k_norm", [192])
    rdl = din("ret_decay_logit", [16]); gn_w = din("ret_gn_w", [1024])
    w_o = din("w_o", [D, D]); peer_wq = din("peer_wq", [D, D])
    sub_keys = din("peer_sub_keys", [16 * 128, 128])
    peer_u = din("peer_u", [16384, D]); peer_v = din("peer_v", [16384, D])
    cosM = din("cosM", [128, NTOK]); sinM = din("sinM", [128, NTOK])
    cosR = din("cosR", [128, NTOK]); sinR = din("sinR", [128, NTOK])
    cosRk = din("cosRk", [128, NTOK]); sinRk = din("sinRk", [128, NTOK])
    cst = din("cst", [128, 10, 128])
    cvec = din("cvec", [128, 4])
    posrow = din("posrow", [128, NTOK]); poscol = din("poscol", [128, NTOK // 128])
    yout = nc.dram_tensor("yout", [NOWN, D], F32, kind="ExternalOutput").ap()

    WIN = dscr("WIN", [D, 4928]); WUQ = dscr("WUQ", [512, 1536]); WUK = dscr("WUK", [256, 1024])
    WUV = dscr("WUV", [256, 1024]); WOB = dscr("WOB", [D, D]); WQB = dscr("WQB", [D, D])
    SKB = dscr("SKB", [2048, 128]); UB = dscr("UB", [16384, D]); VB = dscr("VB", [16384, D])
    UT = dscr("UT", [D, 16384])
    MOD = dscr("MOD", [3, 6 * D], F32)
    QmT = dscr("QmT", [H, 192, NOWN]); KmT = dscr("KmT", [H, 192, NTOK]); Vm = dscr("Vm", [NTOK, 1024])
    RQT = dscr("RQT", [H, 128, NOWN]); RKT = dscr("RKT", [H, 128, NTOK]); RV = dscr("RV", [NTOK, 1024])
    RGT = dscr("RGT", [H, 128, NOWN])
    MIXT = dscr("MIXT", [D, NOWN])
    XMID = dscr("XMID", [NOWN, D], F32)
    H2T = dscr("H2T", [D, NOWN])
    SC = dscr("SC", [NOWN, 16, 128], F32)
    SCS = dscr("SCS", [16, NOWN, 128], F32)
    GT = dscr("GT", [NOWN // 128, 128, 128, 128])

    with ExitStack() as top:
        S = Sched(nc, top)

        uid = [0]

        def sbt(st, name, shape, dt=F32):
            uid[0] += 1
            return st.enter_context(nc.sbuf_tensor(f"{name}_{uid[0]}", list(shape), dt))

        def pst(st, name, shape, dt=F32):
            uid[0] += 1
            return st.enter_context(nc.psum_tensor(f"{name}_{uid[0]}", list(shape), dt))

        cstf = sbt(top, "cstf", [128, 10, 128]); b_cst = Buf()
        S.dma("sp", cstf[:], cst, writes=[b_cst])
        cv = sbt(top, "cv", [128, 4]); b_cv = Buf()
        S.dma("sp", cv[:], cvec, writes=[b_cv])
        identf = cstf[:, 0, :]
        cstb = sbt(top, "cstb", [128, 4, 128], BF16); b_cstb = Buf()
        S.op("dve", lambda e: e.tensor_copy(cstb[:], cstf[:, 0:4, :]), reads=[b_cst], writes=[b_cstb])
        identb, onesb, PMb, PRb = cstb[:, 0, :], cstb[:, 1, :], cstb[:, 2, :], cstb[:, 3, :]
        n1T = sbt(top, "n1T", [128, KC]); n2T = sbt(top, "n2T", [128, KC])
        qanT = sbt(top, "qanT", [128, 4]); kvanT = sbt(top, "kvanT", [128, 2])
        qnT = sbt(top, "qnT", [128, 2]); knT = sbt(top, "knT", [128, 2])
        modT = sbt(top, "modT", [128, 6, KC, 3])
        A1 = sbt(top, "A1", [128, KC, 3]); A2 = sbt(top, "A2", [128, KC, 3])
        b_vec = Buf()
        with nc.allow_non_contiguous_dma(reason="tiny one-time parameter loads"):
            S.dma("sp", n1T[:], norm1_w.rearrange("(c p) -> p c", p=128), writes=[b_vec])
            S.dma("sp", n2T[:], norm2_w.rearrange("(c p) -> p c", p=128), writes=[b_vec])
            S.dma("sp", qanT[:], q_a_norm.rearrange("(c p) -> p c", p=128), writes=[b_vec])
            S.dma("sp", kvanT[:], kv_a_norm.rearrange("(c p) -> p c", p=128), writes=[b_vec])
            S.dma("sp", qnT[:, 0:1], q_norm[0:128].rearrange("(p o) -> p o", o=1), writes=[b_vec])
            S.dma("sp", qnT[0:64, 1:2], q_norm[128:192].rearrange("(p o) -> p o", o=1), writes=[b_vec])
            S.dma("sp", knT[:, 0:1], k_norm[0:128].rearrange("(p o) -> p o", o=1), writes=[b_vec])
            S.dma("sp", knT[0:64, 1:2], k_norm[128:192].rearrange("(p o) -> p o", o=1), writes=[b_vec])

        b_w = Buf()

        def cast_rows(dst, src, rows, step):
            for r0 in range(0, rows, step):
                S.dma("pool", dst[r0:r0 + step, :], src[r0:r0 + step, :], writes=[b_w])

        cast_rows(WIN, w_in, D, 256)
        cast_rows(WUQ, w_uq, 512, 256); cast_rows(WUK, w_uk, 256, 256); cast_rows(WUV, w_uv, 256, 256)
        cast_rows(WOB, w_o, D, 512); cast_rows(WQB, peer_wq, D, 512); cast_rows(SKB, sub_keys, 2048, 512)
        cast_rows(UB, peer_u, 16384, 512); cast_rows(VB, peer_v, 16384, 512)

        with ExitStack() as st:
            cT = sbt(st, "cT", [128, KC, 3]); b_cT = Buf()
            with nc.allow_non_contiguous_dma(reason="tiny c transpose"):
                for b in range(3):
                    S.dma("sp", cT[:, :, b], c3[b].rearrange("(c p) -> p c", p=128), writes=[b_cT])
            S.op("act", lambda e: e.activation(cT[:], cT[:], AF.Silu), reads=[b_cT], writes=[b_cT])
            bada = sbt(st, "bada", [3, 6 * D]); b_bada = Buf()
            S.dma("sp", bada[:], bass.AP(b_ada.tensor, 0, [[0, 3], [1, 6 * D]]), writes=[b_bada])
            wa = [sbt(st, f"wa{i}", [128, KC, 512]) for i in range(2)]
            b_wa = [Buf(), Buf()]
            psm = [pst(st, f"psm{i}", [128, 512]) for i in range(2)]
            b_psm = [Buf(), Buf()]
            modrow = sbt(st, "modrow", [3, 6 * D]); b_modrow = Buf()
            wav = w_ada.rearrange("(c p) n -> p c n", p=128)
            for blk in range(24):
                i = blk % 2
                S.dma("sp", wa[i][:], wav[:, :, blk * 512:(blk + 1) * 512], writes=[b_wa[i]])

                def mm(e, i=i):
                    for kc in range(KC):
                        ins = e.matmul(psm[i][0:3, :], cT[:, kc, :], wa[i][:, kc, :], start=(kc == 0), stop=(kc == KC - 1))
                    return ins
                S.op("pe", mm, reads=[b_cT, b_wa[i]], writes=[b_psm[i]])
                S.op("dve", lambda e, i=i, blk=blk: e.tensor_tensor(modrow[:, blk * 512:(blk + 1) * 512], psm[i][0:3, :],
                                                                    bada[:, blk * 512:(blk + 1) * 512], ALU.add),
                     reads=[b_psm[i], b_bada], writes=[b_modrow])
            b_MOD = Buf()
            S.dma("sp", MOD, modrow[:], reads=[b_modrow], writes=[b_MOD])
            b_modT = Buf()
            with nc.allow_non_contiguous_dma(reason="one-time mod transpose"):
                for g in range(6):
                    for b in range(3):
                        S.dma("sp", modT[:, g, :, b], MOD[b, g * D:(g + 1) * D].rearrange("(c p) -> p c", p=128),
                              reads=[b_MOD], writes=[b_modT])
            b_A = Buf()
            for (A, nT, g) in ((A1, n1T, 1), (A2, n2T, 4)):
                S.op("dve", lambda e, A=A, g=g: e.tensor_scalar(A[:], modT[:, g, :, :], 1.0, None, ALU.add),
                     reads=[b_modT], writes=[b_A])
                S.op("dve", lambda e, A=A, nT=nT: e.tensor_tensor(A[:], A[:], apx(nT[:, 0:1], [[1, KC], [0, 3]]), ALU.mult),
                     reads=[b_A, b_vec], writes=[b_A])
            S.barrier()
        B1 = modT[:, 0, :, :]
        B2 = modT[:, 3, :, :]

        def norm_transpose(st_, xt, b_xt, A, B, seg, hT, b_hT, col0, res):
            junk, b_junk, ss, b_ss, xn, b_xn, psb, b_psb = res
            S.op("act", lambda e: e.activation(junk[:], xt[:], AF.Square, accum_out=ss[:, 0:1]),
                 reads=[b_xt], writes=[b_junk, b_ss])
            S.op("dve", lambda e: e.tensor_scalar(ss[:, 1:2], ss[:, 0:1], 1.0 / D, EPS, ALU.mult, ALU.add),
                 reads=[b_ss], writes=[b_ss])
            S.op("act", lambda e: e.sqrt(ss[:, 1:2], ss[:, 1:2]), reads=[b_ss], writes=[b_ss])
            S.op("dve", lambda e: e.reciprocal(ss[:, 2:3], ss[:, 1:2]), reads=[b_ss], writes=[b_ss])
            S.op("dve", lambda e: e.tensor_scalar(xn[:], xt[:], ss[:, 2:3], None, ALU.mult),
                 reads=[b_xt, b_ss], writes=[b_xn])
            for q4 in range(4):
                pi = q4 % 2

                def tr(e, q4=q4, pi=pi):
                    for j in range(4):
                        kc = q4 * 4 + j
                        ins = e.transpose(psb[pi][:, j * 128:(j + 1) * 128], xn[:, kc * 128:(kc + 1) * 128], identb)
                    return ins
                S.op("pe", tr, reads=[b_xn, b_cstb], writes=[b_psb[pi]])
                for j in range(4):
                    kc = q4 * 4 + j
                    S.op("act", lambda e, kc=kc, j=j, pi=pi: e.activation(
                        hT[:, kc, col0:col0 + 128], psb[pi][:, j * 128:(j + 1) * 128], AF.Identity,
                        bias=B[:, kc, seg:seg + 1], scale=A[:, kc, seg:seg + 1]),
                        reads=[b_psb[pi], b_A, b_modT], writes=[b_hT])

        def nt_resources(st_):
            junk = sbt(st_, "nt_junk", [128, D], BF16); ss = sbt(st_, "nt_ss", [128, 4])
            xn = sbt(st_, "nt_xn", [128, D], BF16)
            psb = [pst(st_, f"nt_psb{i}", [128, 1024], BF16) for i in range(2)]
            return (junk, Buf(), ss, Buf(), xn, Buf(), psb, [Buf(), Buf()])

        def fm_rstd(chunks, b_in, nfeat, sqt, b_sq, ps, b_ps, rst, b_rst):
            for ci, (ap, rows) in enumerate(chunks):
                S.op("act", lambda e, ap=ap, rows=rows, ci=ci: e.activation(sqt[0:rows, ci, :], ap, AF.Square),
                     reads=b_in, writes=[b_sq])

            def mm(e):
                for ci, (ap, rows) in enumerate(chunks):
                    ins = e.matmul(ps[:], onesb[0:rows, :], sqt[0:rows, ci, :], start=(ci == 0), stop=(ci == len(chunks) - 1))
                return ins
            S.op("pe", mm, reads=[b_sq, b_cstb], writes=[b_ps])
            S.op("dve", lambda e: e.tensor_scalar(rst[:], ps[:], 1.0 / nfeat, EPS, ALU.mult, ALU.add), reads=[b_ps], writes=[b_rst])
            S.op("act", lambda e: e.sqrt(rst[:], rst[:]), reads=[b_rst], writes=[b_rst])
            S.op("dve", lambda e: e.reciprocal(rst[:], rst[:]), reads=[b_rst], writes=[b_rst])

        b_QmT, b_KmT, b_Vm, b_RQT, b_RKT, b_RV, b_RG = (Buf() for _ in range(7))
        with ExitStack() as st:
            ntres = nt_resources(st)
            xt = [sbt(st, f"xt{i}", [128, D]) for i in range(2)]; b_xt = [Buf(), Buf()]
            hT = sbt(st, "hT", [128, KC, T], BF16); b_hT = Buf()
            wt = [sbt(st, f"wt{i}", [128, KC, 512], BF16) for i in range(2)]; b_wt = [Buf(), Buf()]
            wuq = sbt(st, "wuq", [128, 4, 1536], BF16); wuk = sbt(st, "wuk", [128, 2, 1024], BF16)
            wuv = sbt(st, "wuv", [128, 2, 1024], BF16); b_wu = Buf()
            S.dma("sp", wuq[:], WUQ.rearrange("(c p) n -> p c n", p=128), reads=[b_w], writes=[b_wu])
            S.dma("sp", wuk[:], WUK.rearrange("(c p) n -> p c n", p=128), reads=[b_w], writes=[b_wu])
            S.dma("sp", wuv[:], WUV.rearrange("(c p) n -> p c n", p=128), reads=[b_w], writes=[b_wu])
            tab = sbt(st, "tab", [128, 6, T]); b_tab = Buf()
            ps = [pst(st, f"p1ps{i}", [128, 512]) for i in range(6)]; b_ps = [Buf() for _ in range(6)]
            cqf = sbt(st, "cqf", [128, 4, T]); b_cqf = Buf()
            cqn = sbt(st, "cqn", [128, 4, T], BF16); b_cqn = Buf()
            ckf = sbt(st, "ckf", [128, 2, T]); b_ckf = Buf()
            ckn = sbt(st, "ckn", [128, 2, T], BF16); b_ckn = Buf()
            krf = sbt(st, "krf", [64, T]); b_krf = Buf()
            sqt = sbt(st, "sqt", [128, 4, T], BF16); b_sq = Buf()
            sqkr = sbt(st, "sqkr", [64, T], BF16); b_sqkr = Buf()
            rst = sbt(st, "rst", [128, T]); b_rst = Buf()
            hf = [sbt(st, f"hf{i}", [128, T]) for i in range(2)]; b_hf = [Buf(), Buf()]
            hr = [sbt(st, f"hr{i}", [64, T]) for i in range(2)]; b_hr = [Buf(), Buf()]
            hb = [sbt(st, f"hb{i}", [128, T], BF16) for i in range(2)]; b_hb = [Buf(), Buf()]
            hrb = [sbt(st, f"hrb{i}", [64, T], BF16) for i in range(2)]; b_hrb = [Buf(), Buf()]
            t1 = [sbt(st, f"t1_{i}", [128, T]) for i in range(2)]; b_t1 = [Buf(), Buf()]
            t2 = [sbt(st, f"t2_{i}", [128, T]) for i in range(2)]; b_t2 = [Buf(), Buf()]
            ob = [sbt(st, f"ob{i}", [128, T], BF16) for i in range(3)]; b_ob = [Buf() for _ in range(3)]
            vt = [sbt(st, f"vt{i}", [128, 1024], BF16) for i in range(2)]; b_vt = [Buf(), Buf()]
            cnt = {"ps": 0, "w": 0, "h": 0, "o": 0, "v": 0, "x": 0}

            def nxt(key, n):
                i = cnt[key] % n
                cnt[key] += 1
                return i

            WINv = WIN.rearrange("(c p) n -> p c n", p=128)

            def load_w(c0, w):
                i = nxt("w", 2)
                S.dma("sp", wt[i][:, :, 0:w], WINv[:, :, c0:c0 + w], reads=[b_w], writes=[b_wt[i]])
                return i

            def rope_store(src_f, b_src, rows, ctab, stab, perm, dst, b_dst, scale_ap=None, rstd=None):
                hi = nxt("h", 2)
                xb = hb[hi] if rows == 128 else hrb[hi]
                b_xb = b_hb[hi] if rows == 128 else b_hrb[hi]
                if rstd is not None:
                    S.op("dve", lambda e: e.scalar_tensor_tensor(xb[0:rows, :], src_f, scale_ap, rstd[0:rows, :], ALU.mult, ALU.mult),
                         reads=b_src + [b_rst, b_vec], writes=[b_xb])
                else:
                    S.op("act", lambda e: e.copy(xb[0:rows, :], src_f), reads=b_src, writes=[b_xb])
                pi = nxt("ps", len(ps))
                S.op("pe", lambda e: e.matmul(ps[pi][0:rows, :], perm[0:rows, 0:rows], xb[0:rows, :], start=True, stop=True),
                     reads=[b_xb, b_cstb], writes=[b_ps[pi]])
                S.op("dve", lambda e: e.tensor_tensor(t1[hi][0:rows, :], xb[0:rows, :], ctab[0:rows, :], ALU.mult),
                     reads=[b_xb, b_tab], writes=[b_t1[hi]])
                S.op("dve", lambda e: e.tensor_tensor(t2[hi][0:rows, :], ps[pi][0:rows, :], stab[0:rows, :], ALU.mult),
                     reads=[b_ps[pi], b_tab], writes=[b_t2[hi]])
                oi = nxt("o", 3)
                S.op("pool", lambda e: e.tensor_tensor(ob[oi][0:rows, :], t1[hi][0:rows, :], t2[hi][0:rows, :], ALU.add),
                     reads=[b_t1[hi], b_t2[hi]], writes=[b_ob[oi]])
                S.dma("pool", dst, ob[oi][0:rows, :], reads=[b_ob[oi]], writes=[b_dst])

            def norm_store(src_f, b_src, scale_ap, rstd, dst, b_dst):
                oi = nxt("o", 3)
                S.op("dve", lambda e: e.scalar_tensor_tensor(ob[oi][:], src_f, scale_ap, rstd[:], ALU.mult, ALU.mult),
                     reads=b_src + [b_rst, b_vec], writes=[b_ob[oi]])
                S.dma("pool", dst, ob[oi][:], reads=[b_ob[oi]], writes=[b_dst])

            def proj_fm(wi, c_in_blk, M, pi, rows0=0):
                def mm(e):
                    for kc in range(KC):
                        ins = e.matmul(ps[pi][rows0:rows0 + M, :], wt[wi][:, kc, c_in_blk:c_in_blk + M], hT[:, kc, :],
                                       start=(kc == 0), stop=(kc == KC - 1))
                    return ins
                S.op("pe", mm, reads=[b_wt[wi], b_hT], writes=[b_ps[pi]])

            tiles = [(t0, si, True) for si, (s0, sl) in enumerate(SEGS) for t0 in range(s0, s0 + sl, T)]
            tiles += [(t0, 0, False) for t0 in range(NOWN, NTOK, T)]
            class Pipe:
                def __init__(self, lag=1):
                    self.q = []; self.lag = lag

                def unit(self, head, tail=None):
                    head()
                    self.q.append(tail)
                    while len(self.q) > self.lag:
                        t_ = self.q.pop(0)
                        if t_ is not None:
                            t_()

                def flush(self):
                    while self.q:
                        t_ = self.q.pop(0)
                        if t_ is not None:
                            t_()
            pipe = Pipe(1)
            NPS = len(ps)
            cnt["hf"] = 0
            for (t0, seg, own) in tiles:
                for s in range(4):
                    xi = nxt("x", 2)
                    S.dma("sp", xt[xi][:], xin[t0 + s * 128:t0 + (s + 1) * 128, :], writes=[b_xt[xi]])
                    norm_transpose(st, xt[xi], b_xt[xi], A1, B1, seg, hT, b_hT, s * 128, ntres)
                for ti, tb in enumerate((cosM, sinM, cosR, sinR, cosRk, sinRk)):
                    S.dma("sp", tab[:, ti, :], tb[:, t0:t0 + T], writes=[b_tab])

                def cq_head():
                    wi = load_w(0, 512)
                    for j in range(4):
                        pi = nxt("ps", NPS)
                        proj_fm(wi, j * 128, 128, pi)
                        S.op("act", lambda e, j=j, pi=pi: e.copy(cqf[:, j, :], ps[pi][:]), reads=[b_ps[pi]], writes=[b_cqf])

                def cq_tail():
                    pi = nxt("ps", NPS)
                    fm_rstd([(cqf[:, j, :], 128) for j in range(4)], [b_cqf], 512, sqt, b_sq, ps[pi], b_ps[pi], rst, b_rst)
                    for j in range(4):
                        S.op("dve", lambda e, j=j: e.scalar_tensor_tensor(cqn[:, j, :], cqf[:, j, :], qanT[:, j:j + 1], rst[:], ALU.mult, ALU.mult),
                             reads=[b_cqf, b_rst, b_vec], writes=[b_cqn])

                def ckv_head():
                    wi = load_w(512, 320)
                    for j in range(2):
                        pi = nxt("ps", NPS)
                        proj_fm(wi, j * 128, 128, pi)
                        S.op("act", lambda e, j=j, pi=pi: e.copy(ckf[:, j, :], ps[pi][:]), reads=[b_ps[pi]], writes=[b_ckf])
                    pi = nxt("ps", NPS)
                    proj_fm(wi, 256, 64, pi)
                    S.op("act", lambda e, pi=pi: e.copy(krf[:], ps[pi][0:64, :]), reads=[b_ps[pi]], writes=[b_krf])
                    S.op("act", lambda e: e.activation(sqkr[:], krf[:], AF.Square), reads=[b_krf], writes=[b_sqkr])

                def ckv_tail():
                    pi = nxt("ps", NPS)
                    fm_rstd([(ckf[:, j, :], 128) for j in range(2)], [b_ckf], 256, sqt, b_sq, ps[pi], b_ps[pi], rst, b_rst)
                    for j in range(2):
                        S.op("dve", lambda e, j=j: e.scalar_tensor_tensor(ckn[:, j, :], ckf[:, j, :], kvanT[:, j:j + 1], rst[:], ALU.mult, ALU.mult),
                             reads=[b_ckf, b_rst, b_vec], writes=[b_ckn])

                if own:
                    pipe.unit(cq_head, cq_tail)
                pipe.unit(ckv_head, ckv_tail)
                if not own:
                    pipe.flush()

                def q_unit(h):
                    slot = {}

                    def head():
                        hi = nxt("hf", 2); slot["hi"] = hi
                        pn = nxt("ps", NPS)

                        def mmq(e):
                            for k4 in range(4):
                                ins = e.matmul(ps[pn][:], wuq[:, k4, h * 192:h * 192 + 128], cqn[:, k4, :], start=(k4 == 0), stop=(k4 == 3))
                            return ins
                        S.op("pe", mmq, reads=[b_wu, b_cqn], writes=[b_ps[pn]])
                        S.op("act", lambda e: e.copy(hf[hi][:], ps[pn][:]), reads=[b_ps[pn]], writes=[b_hf[hi]])
                        pr = nxt("ps", NPS)

                        def mmr(e):
                            for k4 in range(4):
                                ins = e.matmul(ps[pr][0:64, :], wuq[:, k4, h * 192 + 128:h * 192 + 192], cqn[:, k4, :], start=(k4 == 0), stop=(k4 == 3))
                            return ins
                        S.op("pe", mmr, reads=[b_wu, b_cqn], writes=[b_ps[pr]])
                        S.op("act", lambda e: e.copy(hr[hi][:], ps[pr][0:64, :]), reads=[b_ps[pr]], writes=[b_hr[hi]])

                    def tail():
                        hi = slot["hi"]
                        pq = nxt("ps", NPS)
                        fm_rstd([(hf[hi][:], 128), (hr[hi][:], 64)], [b_hf[hi], b_hr[hi]], 192, sqt, b_sq, ps[pq], b_ps[pq], rst, b_rst)
                        norm_store(hf[hi][:], [b_hf[hi]], qnT[:, 0:1], rst, QmT[h, 0:128, t0:t0 + T], b_QmT)
                        rope_store(hr[hi][:], [b_hr[hi]], 64, tab[:, 0, :], tab[:, 1, :], PMb, QmT[h, 128:192, t0:t0 + T], b_QmT,
                                   scale_ap=qnT[0:64, 1:2], rstd=rst)
                    pipe.unit(head, tail)

                def k_unit(h):
                    slot = {}

                    def head():
                        hi = nxt("hf", 2); slot["hi"] = hi
                        pn = nxt("ps", NPS)

                        def mmk(e):
                            for k2 in range(2):
                                ins = e.matmul(ps[pn][:], wuk[:, k2, h * 128:(h + 1) * 128], ckn[:, k2, :], start=(k2 == 0), stop=(k2 == 1))
                            return ins
                        S.op("pe", mmk, reads=[b_wu, b_ckn], writes=[b_ps[pn]])
                        S.op("act", lambda e: e.copy(hf[hi][:], ps[pn][:]), reads=[b_ps[pn]], writes=[b_hf[hi]])

                    def tail():
                        hi = slot["hi"]
                        S.op("act", lambda e: e.activation(sqt[:, 0, :], hf[hi][:], AF.Square), reads=[b_hf[hi]], writes=[b_sq])
                        pq = nxt("ps", NPS)

                        def mms(e):
                            e.matmul(ps[pq][:], onesb, sqt[:, 0, :], start=True, stop=False)
                            return e.matmul(ps[pq][:], onesb[0:64, :], sqkr[:], start=False, stop=True)
                        S.op("pe", mms, reads=[b_sq, b_sqkr, b_cstb], writes=[b_ps[pq]])
                        S.op("dve", lambda e: e.tensor_scalar(rst[:], ps[pq][:], 1.0 / 192, EPS, ALU.mult, ALU.add), reads=[b_ps[pq]], writes=[b_rst])
                        S.op("act", lambda e: e.sqrt(rst[:], rst[:]), reads=[b_rst], writes=[b_rst])
                        S.op("dve", lambda e: e.reciprocal(rst[:], rst[:]), reads=[b_rst], writes=[b_rst])
                        norm_store(hf[hi][:], [b_hf[hi]], knT[:, 0:1], rst, KmT[h, 0:128, t0:t0 + T], b_KmT)
                        rope_store(krf[:], [b_krf], 64, tab[:, 0, :], tab[:, 1, :], PMb, KmT[h, 128:192, t0:t0 + T], b_KmT,
                                   scale_ap=knT[0:64, 1:2], rstd=rst)
                    pipe.unit(head, tail)

                if own:
                    for h in range(H):
                        q_unit(h)
                for h in range(H):
                    k_unit(h)

                def v_unit(s):
                    def head():
                        vi = nxt("v", 2)
                        for cb in range(2):
                            pi = nxt("ps", NPS)

                            def mmv(e, cb=cb, pi=pi):
                                for k2 in range(2):
                                    ins = e.matmul(ps[pi][:], ckn[:, k2, s * 128:(s + 1) * 128], wuv[:, k2, cb * 512:(cb + 1) * 512], start=(k2 == 0), stop=(k2 == 1))
                                return ins
                            S.op("pe", mmv, reads=[b_wu, b_ckn], writes=[b_ps[pi]])
                            S.op("act", lambda e, cb=cb, pi=pi: e.copy(vt[vi][:, cb * 512:(cb + 1) * 512], ps[pi][:]), reads=[b_ps[pi]], writes=[b_vt[vi]])
                        S.dma("pool", Vm[t0 + s * 128:t0 + (s + 1) * 128, :], vt[vi][:], reads=[b_vt[vi]], writes=[b_Vm])
                    pipe.unit(head, None)
                for s in range(4):
                    v_unit(s)

                def r_unit(which, wi, j, h):
                    slot = {}

                    def head():
                        pi = nxt("ps", NPS); slot["pi"] = pi
                        proj_fm(wi, j * 128, 128, pi)

                    def tail():
                        pi = slot["pi"]
                        if which == 0:
                            rope_store(ps[pi][:], [b_ps[pi]], 128, tab[:, 2, :], tab[:, 3, :], PRb, RQT[h, :, t0:t0 + T], b_RQT)
                        else:
                            rope_store(ps[pi][:], [b_ps[pi]], 128, tab[:, 4, :], tab[:, 5, :], PRb, RKT[h, :, t0:t0 + T], b_RKT)
                    pipe.unit(head, tail)
                for which in ((0, 1) if own else (1,)):
                    base = 832 if which == 0 else 1856
                    for blk in range(2):
                        wi = load_w(base + blk * 512, 512)
                        for j in range(4):
                            r_unit(which, wi, j, blk * 4 + j)

                wis = [load_w(2880, 512), load_w(2880 + 512, 512)]

                def rv_unit(s, wis=wis):
                    def head():
                        vi = nxt("v", 2)
                        for cb in range(2):
                            pi = nxt("ps", NPS)

                            def mmt(e, cb=cb, pi=pi):
                                for kc in range(KC):
                                    ins = e.matmul(ps[pi][:], hT[:, kc, s * 128:(s + 1) * 128], wt[wis[cb]][:, kc, :], start=(kc == 0), stop=(kc == KC - 1))
                                return ins
                            S.op("pe", mmt, reads=[b_hT, b_wt[wis[cb]]], writes=[b_ps[pi]])
                            S.op("act", lambda e, cb=cb, pi=pi: e.copy(vt[vi][:, cb * 512:(cb + 1) * 512], ps[pi][:]),
                                 reads=[b_ps[pi]], writes=[b_vt[vi]])
                        S.dma("pool", RV[t0 + s * 128:t0 + (s + 1) * 128, :], vt[vi][:], reads=[b_vt[vi]], writes=[b_RV])
                    pipe.unit(head, None)
                for s in range(4):
                    rv_unit(s)

                def rg_unit(wi, j, h):
                    def head():
                        pi = nxt("ps", NPS)
                        proj_fm(wi, j * 128, 128, pi)
                        oi = nxt("o", 3)
                        S.op("act", lambda e: e.activation(ob[oi][:], ps[pi][:], AF.Silu), reads=[b_ps[pi]], writes=[b_ob[oi]])
                        S.dma("pool", RGT[h, :, t0:t0 + T], ob[oi][:], reads=[b_ob[oi]], writes=[b_RG])
                    pipe.unit(head, None)
                if own:
                    for blk in range(2):
                        wi = load_w(3904 + blk * 512, 512)
                        for j in range(4):
                            rg_unit(wi, j, blk * 4 + j)
                pipe.flush()
            S.barrier()

        if upto <= 1:
            with ExitStack() as st:
                dbg = sbt(st, "dbg", [128, D])
                bd = Buf()
                for r0 in range(0, NOWN, 128):
                    S.dma("sp", dbg[:], xin[r0:r0 + 128, :], writes=[bd])
                    S.dma("sp", yout[r0:r0 + 128, :], dbg[:], reads=[bd])
                S.barrier()
            return nc

        b_MIXT = Buf()
        CTXM = 2 * SA if 2 * SA > SS else SS
        SQM = max(SA, SS)
        with ExitStack() as st:
            ps_s = [pst(st, f"a_s{i}", [128, 512]) for i in range(2)]; b_ps_s = [Buf(), Buf()]
            ps_o = pst(st, "a_o", [128, 512]); b_ps_o = Buf()
            ps_z = pst(st, "a_z", [128, 512]); b_ps_z = Buf()
            kn = [sbt(st, f"a_kn{i}", [128, CTXM], BF16) for i in range(2)]
            kr = [sbt(st, f"a_kr{i}", [64, CTXM], BF16) for i in range(2)]
            vv = [sbt(st, f"a_vv{i}", [128, CTXM // 128, 128], BF16) for i in range(2)]
            qn = [sbt(st, f"a_qn{i}", [128, SQM], BF16) for i in range(2)]
            qr = [sbt(st, f"a_qr{i}", [64, SQM], BF16) for i in range(2)]
            b_ld = [Buf(), Buf()]
            pt = [sbt(st, f"a_pt{i}", [128, 512], BF16) for i in range(3)]; b_pt = [Buf() for _ in range(3)]
            rs = sbt(st, "a_rs", [128, 512]); b_rs = Buf()
            on = [sbt(st, f"a_on{i}", [128, 512], BF16) for i in range(2)]; b_on = [Buf(), Buf()]
            it = 0; lc = 0; oc = 0
            sc_att = float(192 ** -0.5)
            for si, (s0, sl) in enumerate(SEGS):
                ranges = [(s0, sl)] + ([(NOWN, SA)] if si == 0 else [])
                ctx = sum(l for _, l in ranges)
                for h in range(H):
                    i = lc % 2; lc += 1
                    off = 0
                    for (c0, cl) in ranges:
                        S.dma("sp", kn[i][:, off:off + cl], KmT[h, 0:128, c0:c0 + cl], reads=[b_KmT], writes=[b_ld[i]])
                        S.dma("sp", kr[i][:, off:off + cl], KmT[h, 128:192, c0:c0 + cl], reads=[b_KmT], writes=[b_ld[i]])
                        S.dma("sp", vv[i][:, off // 128:(off + cl) // 128, :],
                              Vm[c0:c0 + cl, h * 128:(h + 1) * 128].rearrange("(c p) d -> p c d", p=128), reads=[b_Vm], writes=[b_ld[i]])
                        off += cl
                    S.dma("sp", qn[i][:, 0:sl], QmT[h, 0:128, s0:s0 + sl], reads=[b_QmT], writes=[b_ld[i]])
                    S.dma("sp", qr[i][:, 0:sl], QmT[h, 128:192, s0:s0 + sl], reads=[b_QmT], writes=[b_ld[i]])
                    for qt in range(sl // 512):
                        nk = ctx // 128
                        slots = {}

                        def e_mm1(kc, i=i, qt=qt):
                            nonlocal it
                            sidx = it % 2; pi = it % 3; it += 1
                            slots[kc] = (sidx, pi)

                            def mm1(e):
                                e.matmul(ps_s[sidx][:], kn[i][:, kc * 128:(kc + 1) * 128], qn[i][:, qt * 512:(qt + 1) * 512], start=True, stop=False)
                                return e.matmul(ps_s[sidx][:], kr[i][:, kc * 128:(kc + 1) * 128], qr[i][:, qt * 512:(qt + 1) * 512], start=False, stop=True)
                            S.op("pe", mm1, reads=[b_ld[i]], writes=[b_ps_s[sidx]])

                        def e_rest(kc, i=i, nk=nk):
                            sidx, pi = slots[kc]
                            S.op("act", lambda e: e.activation(pt[pi][:], ps_s[sidx][:], AF.Exp, scale=sc_att),
                                 reads=[b_ps_s[sidx]], writes=[b_pt[pi]])

                            def mm2(e):
                                e.matmul(ps_o[:], vv[i][:, kc, :], pt[pi][:], start=(kc == 0), stop=(kc == nk - 1))
                                return e.matmul(ps_z[:], onesb, pt[pi][:], start=(kc == 0), stop=(kc == nk - 1))
                            S.op("pe", mm2, reads=[b_ld[i], b_pt[pi], b_cstb], writes=[b_ps_o, b_ps_z])
                        e_mm1(0)
                        for kc in range(nk):
                            if kc + 1 < nk:
                                e_mm1(kc + 1)
                            e_rest(kc)
                        oi = oc % 2; oc += 1
                        S.op("dve", lambda e: e.reciprocal(rs[:], ps_z[:]), reads=[b_ps_z], writes=[b_rs])
                        S.op("dve", lambda e, oi=oi: e.tensor_tensor(on[oi][:], ps_o[:], rs[:], ALU.mult), reads=[b_ps_o, b_rs], writes=[b_on[oi]])
                        S.dma("pool", MIXT[h * 128:(h + 1) * 128, s0 + qt * 512:s0 + (qt + 1) * 512], on[oi][:], reads=[b_on[oi]], writes=[b_MIXT])
            S.barrier()
        if upto <= 2:
            return nc

        with ExitStack() as st:
            lg = sbt(st, "r_lg", [128, 16]); b_lg = Buf()
            nlgb = sbt(st, "r_nlgb", [128, 8])
            gnT = sbt(st, "r_gnT", [128, 8])
            pcol = sbt(st, "r_pcol", [128, NTOK // 128])
            S.dma("sp", lg[:], bass.AP(rdl.tensor, 0, [[0, 128], [1, 16]]), writes=[b_lg])
            with nc.allow_non_contiguous_dma(reason="tiny"):
                S.dma("sp", gnT[:], gn_w.rearrange("(h p) -> p h", p=128), writes=[b_lg])
            S.dma("sp", pcol[:], poscol, writes=[b_lg])
            npcol = sbt(st, "r_npcol", [128, NTOK // 128])
            S.op("dve", lambda e: e.tensor_scalar(npcol[:], pcol[:], -1.0, None, ALU.mult), reads=[b_lg], writes=[b_lg])
            S.op("act", lambda e: e.activation(lg[:], lg[:], AF.Sigmoid), reads=[b_lg], writes=[b_lg])
            S.op("act", lambda e: e.activation(lg[:], lg[:], AF.Ln), reads=[b_lg], writes=[b_lg])
            S.op("dve", lambda e: e.tensor_scalar(nlgb[:], lg[:, 8:16], -1.0, None, ALU.mult), reads=[b_lg], writes=[b_lg])
            ps_s = [pst(st, f"r_s{i}", [128, 512]) for i in range(2)]; b_ps_s = [Buf(), Buf()]
            ps_o = [pst(st, f"r_o{i}", [128, 512]) for i in range(4)]; b_ps_o = [Buf() for _ in range(4)]
            ps_m = [pst(st, f"r_m{i}", [128, 512]) for i in range(2)]; b_ps_m = [Buf(), Buf()]
            qq = [sbt(st, f"r_q{i}", [128, 4, 512], BF16) for i in range(2)]; gg = [sbt(st, f"r_g{i}", [128, 4, 512], BF16) for i in range(2)]
            prow = [sbt(st, f"r_pr{i}", [128, 512]) for i in range(2)]; b_q = [Buf(), Buf()]
            kk = [sbt(st, f"r_k{i}", [128, 4, 128], BF16) for i in range(3)]; vv = [sbt(st, f"r_v{i}", [128, 512], BF16) for i in range(3)]
            b_k = [Buf() for _ in range(3)]
            dd = [sbt(st, f"r_d{i}", [128, 512]) for i in range(2)]; dp = [sbt(st, f"r_dp{i}", [128, 512]) for i in range(2)]
            dn = [sbt(st, f"r_dn{i}", [128, 512]) for i in range(2)]; b_d = [Buf(), Buf()]
            a1 = [sbt(st, f"r_a{i}", [128, 512]) for i in range(2)]; b_a1 = [Buf(), Buf()]
            dm = [sbt(st, f"r_dm{i}", [128, 512]) for i in range(2)]; b_dm = [Buf(), Buf()]
            pt = [sbt(st, f"r_pt{i}", [128, 512], BF16) for i in range(3)]; b_pt = [Buf() for _ in range(3)]
            of = sbt(st, "r_of", [128, 512]); obf = sbt(st, "r_obf", [128, 512], BF16); sq = sbt(st, "r_sq", [128, 512], BF16); b_of = Buf()
            mu = sbt(st, "r_mu", [128, 512]); var = sbt(st, "r_var", [128, 512]); b_mu = Buf()
            ro = [sbt(st, f"r_ro{i}", [128, 512], BF16) for i in range(2)]; b_ro = [Buf(), Buf()]
            qc = 0; kcn = 0; it = 0; rc = 0
            for si, (s0, sl) in enumerate(SEGS):
                ranges = [(s0, sl)] + ([(NOWN, SA)] if si == 0 else [])
                chunks = [c0 + j * 128 for (c0, cl) in ranges for j in range(cl // 128)]
                for hg in range(2):
                    for qt in range(sl // 512):
                        q0 = s0 + qt * 512
                        qi = qc % 2; qc += 1
                        S.dma("sp", qq[qi][:], RQT[hg * 4:(hg + 1) * 4, :, q0:q0 + 512].rearrange("h p t -> p h t"), reads=[b_RQT], writes=[b_q[qi]])
                        S.dma("sp", gg[qi][:], RGT[hg * 4:(hg + 1) * 4, :, q0:q0 + 512].rearrange("h p t -> p h t"), reads=[b_RG], writes=[b_q[qi]])
                        S.dma("sp", prow[qi][:], posrow[:, q0:q0 + 512], writes=[b_q[qi]])
                        seq = [(ci, k0, hh) for ci, k0 in enumerate(chunks) for hh in range(4)]
                        cslot = {}
                        bslot = {}

                        def e_score(n, qi=qi, hg=hg):
                            nonlocal kcn, it
                            ci, k0, hh = seq[n]
                            if hh == 0:
                                ki = kcn % 3; di = kcn % 2; kcn += 1
                                cslot[ci] = (ki, di)
                                S.dma("sp", kk[ki][:], RKT[hg * 4:(hg + 1) * 4, :, k0:k0 + 128].rearrange("h p t -> p h t"), reads=[b_RKT], writes=[b_k[ki]])
                                S.dma("sp", vv[ki][:], RV[k0:k0 + 128, hg * 512:(hg + 1) * 512], reads=[b_RV], writes=[b_k[ki]])
                                S.op("act", lambda e: e.activation(dp[di][:], prow[qi][:], AF.Relu, bias=npcol[:, k0 // 128:k0 // 128 + 1], scale=1.0),
                                     reads=[b_q[qi], b_lg], writes=[b_d[di]])
                                S.op("act", lambda e: e.activation(dn[di][:], prow[qi][:], AF.Relu, bias=pcol[:, k0 // 128:k0 // 128 + 1], scale=-1.0),
                                     reads=[b_q[qi], b_lg], writes=[b_d[di]])
                            ki, di = cslot[ci]
                            sidx = it % 2; pi = it % 3; it += 1
                            bslot[n] = (sidx, pi)
                            S.op("pe", lambda e: e.matmul(ps_s[sidx][:], kk[ki][:, hh, :], qq[qi][:, hh, :], start=True, stop=True),
                                 reads=[b_k[ki], b_q[qi]], writes=[b_ps_s[sidx]])

                        def e_rest(n, hg=hg, nch=len(chunks), q0=q0, s0=s0, sl=sl):
                            ci, k0, hh = seq[n]
                            h = hg * 4 + hh
                            ki, di = cslot[ci]
                            sidx, pi = bslot[n]
                            own_chunk = (s0 <= k0 < s0 + sl)
                            only_f = own_chunk and (k0 + 128 <= q0)
                            only_b = own_chunk and (k0 >= q0 + 512)
                            if only_f or only_b:
                                if only_f:
                                    S.op("act", lambda e: e.activation(a1[sidx][:], dp[di][:], AF.Exp, scale=lg[:, h:h + 1]),
                                         reads=[b_d[di], b_lg], writes=[b_a1[sidx]])
                                else:
                                    S.op("act", lambda e: e.activation(a1[sidx][:], dn[di][:], AF.Exp, scale=lg[:, 8 + h:9 + h]),
                                         reads=[b_d[di], b_lg], writes=[b_a1[sidx]])
                                S.op("dve", lambda e: e.tensor_tensor(pt[pi][:], a1[sidx][:], ps_s[sidx][:], ALU.mult),
                                     reads=[b_ps_s[sidx], b_a1[sidx]], writes=[b_pt[pi]])
                            else:
                                S.op("act", lambda e: e.activation(a1[sidx][:], dp[di][:], AF.Exp, scale=lg[:, h:h + 1]),
                                     reads=[b_d[di], b_lg], writes=[b_a1[sidx]])
                                S.op("act", lambda e: e.activation(dm[sidx][:], dn[di][:], AF.Exp, scale=lg[:, 8 + h:9 + h]),
                                     reads=[b_d[di], b_lg], writes=[b_dm[sidx]])
                                S.op("dve", lambda e: e.tensor_tensor(a1[sidx][:], a1[sidx][:], ps_s[sidx][:], ALU.mult),
                                     reads=[b_ps_s[sidx], b_a1[sidx]], writes=[b_a1[sidx]])
                                S.op("dve", lambda e: e.tensor_tensor(pt[pi][:], a1[sidx][:], dm[sidx][:], ALU.mult),
                                     reads=[b_a1[sidx], b_dm[sidx]], writes=[b_pt[pi]])
                            S.op("pe", lambda e: e.matmul(ps_o[hh][:], vv[ki][:, hh * 128:(hh + 1) * 128], pt[pi][:], start=(ci == 0), stop=(ci == nch - 1)),
                                 reads=[b_k[ki], b_pt[pi]], writes=[b_ps_o[hh]])
                        e_score(0)
                        for n in range(len(seq)):
                            if n + 1 < len(seq):
                                e_score(n + 1)
                            e_rest(n)
                        for hh in range(4):
                            h = hg * 4 + hh
                            S.op("act", lambda e, hh=hh: e.copy(of[:], ps_o[hh][:]), reads=[b_ps_o[hh]], writes=[b_of])
                            S.op("act", lambda e, hh=hh: e.activation(sq[:], ps_o[hh][:], AF.Square), reads=[b_ps_o[hh]], writes=[b_of])
                            S.op("dve", lambda e: e.tensor_copy(obf[:], of[:]), reads=[b_of], writes=[b_of])
                            S.op("pe", lambda e: e.matmul(ps_m[0][:], onesb, obf[:], start=True, stop=True), reads=[b_of, b_cstb], writes=[b_ps_m[0]])
                            S.op("pe", lambda e: e.matmul(ps_m[1][:], onesb, sq[:], start=True, stop=True), reads=[b_of, b_cstb], writes=[b_ps_m[1]])
                            S.op("dve", lambda e: e.tensor_scalar(mu[:], ps_m[0][:], 1.0 / 128, None, ALU.mult), reads=[b_ps_m[0]], writes=[b_mu])
                            S.op("dve", lambda e: e.tensor_tensor(var[:], mu[:], mu[:], ALU.mult), reads=[b_mu], writes=[b_mu])
                            S.op("dve", lambda e: e.scalar_tensor_tensor(var[:], ps_m[1][:], 1.0 / 128, var[:], ALU.mult, ALU.subtract), reads=[b_ps_m[1], b_mu], writes=[b_mu])
                            S.op("dve", lambda e: e.tensor_scalar(var[:], var[:], EPS, None, ALU.add), reads=[b_mu], writes=[b_mu])
                            S.op("act", lambda e: e.sqrt(var[:], var[:]), reads=[b_mu], writes=[b_mu])
                            S.op("dve", lambda e: e.reciprocal(var[:], var[:]), reads=[b_mu], writes=[b_mu])
                            S.op("dve", lambda e: e.tensor_tensor(of[:], of[:], mu[:], ALU.subtract), reads=[b_of, b_mu], writes=[b_of])
                            S.op("dve", lambda e: e.tensor_tensor(of[:], of[:], var[:], ALU.mult), reads=[b_of, b_mu], writes=[b_of])
                            ri = rc % 2; rc += 1
                            S.op("dve", lambda e, ri=ri, h=h, hh=hh, qi=qi: e.scalar_tensor_tensor(ro[ri][:], of[:], gnT[:, h:h + 1], gg[qi][:, hh, :], ALU.mult, ALU.mult),
                                 reads=[b_of, b_lg, b_q[qi]], writes=[b_ro[ri]])
                            S.dma("pool", MIXT[1024 + h * 128:1024 + (h + 1) * 128, q0:q0 + 512], ro[ri][:], reads=[b_ro[ri]], writes=[b_MIXT])
            S.barrier()
        if upto <= 3:
            return nc

        b_XMID, b_H2T, b_GT, b_SC = Buf(), Buf(), Buf(), Buf()
        with ExitStack() as st:
            ntres = nt_resources(st)
            ps = [pst(st, f"p4ps{i}", [128, 512]) for i in range(3)]; b_ps = [Buf() for _ in range(3)]
            psb = ntres[6]; b_psb = ntres[7]
            xs = [sbt(st, f"p4x{i}", [128, D]) for i in range(4)]; b_xs = [Buf() for _ in range(4)]
            mixT = sbt(st, "p4mix", [128, KC, T], BF16); b_mix = Buf()
            h2T = mixT; b_h2 = b_mix
            wt = [sbt(st, f"p4w{i}", [128, KC, 512], BF16) for i in range(2)]; b_wt = [Buf(), Buf()]
            g1 = sbt(st, "p4g1", [128, D]); b_g1 = Buf()
            tmp = [sbt(st, f"p4t{i}", [128, 512]) for i in range(2)]; b_tmp = [Buf(), Buf()]
            qp = sbt(st, "p4qp", [128, 16, T], BF16); b_qp = Buf()
            skn = sbt(st, "p4skn", [128, 16, 128], BF16); skT = sbt(st, "p4skT", [128, 16, 128], BF16); b_sk = Buf()
            S.dma("sp", skn[:], SKB.rearrange("(c p) d -> p c d", p=128), reads=[b_w], writes=[b_sk])
            for q4 in range(4):
                def trs(e, q4=q4):
                    for j in range(4):
                        ins = e.transpose(psb[0][:, j * 128:(j + 1) * 128], skn[:, q4 * 4 + j, :], identb)
                    return ins
                S.op("pe", trs, reads=[b_sk, b_cstb], writes=[b_psb[0]])
                S.op("act", lambda e, q4=q4: e.copy(skT[:, q4 * 4:(q4 + 1) * 4, :], psb[0][:, 0:512].rearrange("p (j k) -> p j k", j=4)),
                     reads=[b_psb[0]], writes=[b_sk])
            sc = sbt(st, "p4sc", [128, 16, 128]); b_sc = Buf()
            wc = 0; pc = 0; tc_ = 0; gc = 0
            WOv = WOB.rearrange("(c p) n -> p c n", p=128); WQv = WQB.rearrange("(c p) n -> p c n", p=128)
            own_tiles = [(t0, si) for si, (s0, sl) in enumerate(SEGS) for t0 in range(s0, s0 + sl, T)]
            for (t0, seg) in own_tiles:
                S.dma("sp", mixT[:], MIXT[:, t0:t0 + T].rearrange("(c p) t -> p c t", p=128), reads=[b_MIXT], writes=[b_mix])
                S.dma("sp", g1[:], bass.AP(MOD.tensor, seg * 6 * D + 2 * D, [[0, 128], [1, D]]), reads=[b_MOD], writes=[b_g1])
                for s in range(4):
                    S.dma("sp", xs[s][:], xin[t0 + s * 128:t0 + (s + 1) * 128, :], writes=[b_xs[s]])
                for cb in range(4):
                    wi = wc % 2; wc += 1
                    S.dma("sp", wt[wi][:], WOv[:, :, cb * 512:(cb + 1) * 512], reads=[b_w], writes=[b_wt[wi]])
                    for s in range(4):
                        pi = pc % 3; pc += 1; ti = tc_ % 2; tc_ += 1

                        def mmo(e, s=s, wi=wi, pi=pi):
                            for kc in range(KC):
                                ins = e.matmul(ps[pi][:], mixT[:, kc, s * 128:(s + 1) * 128], wt[wi][:, kc, :], start=(kc == 0), stop=(kc == KC - 1))
                            return ins
                        S.op("pe", mmo, reads=[b_mix, b_wt[wi]], writes=[b_ps[pi]])
                        S.op("dve", lambda e, pi=pi, ti=ti, cb=cb: e.tensor_tensor(tmp[ti][:], ps[pi][:], g1[:, cb * 512:(cb + 1) * 512], ALU.mult),
                             reads=[b_ps[pi], b_g1], writes=[b_tmp[ti]])
                        S.op("pool", lambda e, s=s, ti=ti, cb=cb: e.tensor_tensor(xs[s][:, cb * 512:(cb + 1) * 512], tmp[ti][:], xs[s][:, cb * 512:(cb + 1) * 512], ALU.add),
                             reads=[b_tmp[ti], b_xs[s]], writes=[b_xs[s]])
                for s in range(4):
                    S.dma("pool", XMID[t0 + s * 128:t0 + (s + 1) * 128, :], xs[s][:], reads=[b_xs[s]], writes=[b_XMID])
                    norm_transpose(st, xs[s], b_xs[s], A2, B2, seg, h2T, b_h2, s * 128, ntres)
                S.dma("pool", H2T[:, t0:t0 + T].rearrange("(c p) t -> p c t", p=128), h2T[:], reads=[b_h2], writes=[b_H2T])
                for blk in range(4):
                    wi = wc % 2; wc += 1
                    S.dma("sp", wt[wi][:], WQv[:, :, blk * 512:(blk + 1) * 512], reads=[b_w], writes=[b_wt[wi]])
                    for j in range(4):
                        pi = pc % 3; pc += 1

                        def mmq(e, j=j, wi=wi, pi=pi):
                            for kc in range(KC):
                                ins = e.matmul(ps[pi][:], wt[wi][:, kc, j * 128:(j + 1) * 128], h2T[:, kc, :], start=(kc == 0), stop=(kc == KC - 1))
                            return ins
                        S.op("pe", mmq, reads=[b_h2, b_wt[wi]], writes=[b_ps[pi]])
                        S.op("act", lambda e, pi=pi, c16=blk * 4 + j: e.copy(qp[:, c16, :], ps[pi][:]), reads=[b_ps[pi]], writes=[b_qp])
                for s in range(4):
                    ts0 = t0 + s * 128
                    for g4 in range(4):
                        pi = pc % 3; pc += 1

                        def mms(e, g4=g4, s=s, pi=pi):
                            for j in range(4):
                                c16 = g4 * 4 + j
                                ins = e.matmul(ps[pi][:, j * 128:(j + 1) * 128], qp[:, c16, s * 128:(s + 1) * 128], skT[:, c16, :], start=True, stop=True)
                            return ins
                        S.op("pe", mms, reads=[b_qp, b_sk], writes=[b_ps[pi]])
                        S.op("act", lambda e, g4=g4, pi=pi: e.copy(sc[:, g4 * 4:(g4 + 1) * 4, :], ps[pi][:].rearrange("p (j k) -> p j k", j=4)),
                             reads=[b_ps[pi]], writes=[b_sc])
                    S.dma("pool", SC[ts0:ts0 + 128, :, :], sc[:], reads=[b_sc], writes=[b_SC])
            S.barrier()
        if upto <= 4:
            return nc

        with ExitStack() as st:
            ps_t = pst(st, "g_pt", [128, 512]); b_ps_t = Buf()
            ps_g = [pst(st, f"g_pg{i}", [128, 512]) for i in range(3)]; b_ps_g = [Buf() for _ in range(3)]
            sc = [sbt(st, f"g_sc{i}", [128, 16, 128]) for i in range(2)]; b_sc = [Buf(), Buf()]
            top = sbt(st, "g_top", [128, 16, 16]); mrt = [sbt(st, f"g_mrt{i}", [128, 256]) for i in range(2)]; b_mrt = [Buf(), Buf()]; tops = sbt(st, "g_tops", [128, 16, 16]); b_top = Buf()
            cand = sbt(st, "g_cand", [128, 8, 256]); ce = sbt(st, "g_ce", [128, 8, 256]); b_cand = Buf()
            c8 = [sbt(st, f"g_c8{i}", [128, 16]) for i in range(2)]; b_c8 = [Buf(), Buf()]; th = sbt(st, "g_th", [128, 8]); zz = sbt(st, "g_z", [128, 8]); rz = sbt(st, "g_rz", [128, 8]); b_th = Buf()
            tk = sbt(st, "g_tk", [128, 3, 128]); b_tk = Buf()
            tkT = [sbt(st, f"g_tkT{i}", [128, 3, 128]) for i in range(2)]; b_tkT = [Buf(), Buf()]
            s1r = [sbt(st, f"g_s1r{i}", [128, 32, 128]) for i in range(2)]; s2r = [sbt(st, f"g_s2r{i}", [128, 32, 128]) for i in range(2)]
            b_rep = [Buf(), Buf()]
            e2 = sbt(st, "g_e2", [128, 32, 128], BF16); b_e2 = Buf()
            kapb = [sbt(st, f"g_kapb{i}", [128, 128], BF16) for i in range(2)]
            mk = sbt(st, "g_mk", [128, 32, 128], BF16); b_mk = Buf()
            eq = sbt(st, "g_eq", [128, 32, 128], BF16); b_eq = Buf()
            Lt = [sbt(st, f"g_L{i}", [128, 32, 128], BF16) for i in range(2)]; b_L = [Buf(), Buf()]
            Rt = [sbt(st, f"g_R{i}", [128, 32, 128], BF16) for i in range(2)]; b_R = [Buf(), Buf()]
            gts = sbt(st, "g_gts", [128, 128, 128], BF16); b_gts = Buf()
            b_SCS = [Buf() for _ in range(4)]
            cnts = [0, 0]
            def stageA(sti):
                ts0 = sti * 128
                i = sti % 2
                scc = sc[i]
                S.dma("sp", scc[:], SC[ts0:ts0 + 128, :, :], reads=[b_SC], writes=[b_sc[i]])
                for c0 in range(0, 16, 2):
                    for c16 in (c0, c0 + 1):
                        S.op("dve", lambda e, c16=c16: e.max(top[:, c16, 0:8], scc[:, c16, :]), reads=[b_sc[i]], writes=[b_top])
                    for c16 in (c0, c0 + 1):
                        S.op("dve", lambda e, c16=c16: e.match_replace(mrt[c16 % 2][:, 0:128], top[:, c16, 0:8], scc[:, c16, :], -1e30), reads=[b_sc[i], b_top], writes=[b_mrt[c16 % 2]])
                    for c16 in (c0, c0 + 1):
                        S.op("dve", lambda e, c16=c16: e.max(top[:, c16, 8:16], mrt[c16 % 2][:, 0:128]), reads=[b_mrt[c16 % 2]], writes=[b_top])
                    yield
                S.op("dve", lambda e: e.tensor_tensor(scc[:], scc[:], apx(top[:, 0, 0:1], [[16, 16], [0, 128]]), ALU.subtract), reads=[b_sc[i], b_top], writes=[b_sc[i]])
                S.dma("act", SCS[:, ts0:ts0 + 128, :].rearrange("c t k -> t c k"), scc[:], reads=[b_sc[i]], writes=[b_SCS[sti % 4]])
                S.op("dve", lambda e: e.tensor_tensor(tops[:], top[:], apx(top[:, 0, 0:1], [[16, 16], [0, 16]]), ALU.subtract), reads=[b_top], writes=[b_top])
                S.op("dve", lambda e: e.tensor_tensor(apx(cand[:, 0, 0:1], [[256, 8], [16, 16], [1, 16]]),
                                                      apx(tops[:, 0, 0:1], [[32, 8], [1, 16], [0, 16]]),
                                                      apx(tops[:, 1, 0:1], [[32, 8], [0, 16], [1, 16]]), ALU.add),
                     reads=[b_top], writes=[b_cand])
                yield
                for h0 in range(0, H, 2):
                    for h in (h0, h0 + 1):
                        S.op("dve", lambda e, h=h: e.max(c8[h % 2][:, 0:8], cand[:, h, :]), reads=[b_cand], writes=[b_c8[h % 2]])
                    for h in (h0, h0 + 1):
                        S.op("dve", lambda e, h=h: e.match_replace(mrt[h % 2][:], c8[h % 2][:, 0:8], cand[:, h, :], -1e30), reads=[b_cand, b_c8[h % 2]], writes=[b_mrt[h % 2]])
                    for h in (h0, h0 + 1):
                        S.op("dve", lambda e, h=h: e.max(c8[h % 2][:, 8:16], mrt[h % 2][:]), reads=[b_mrt[h % 2]], writes=[b_c8[h % 2]])
                    for h in (h0, h0 + 1):
                        S.op("dve", lambda e, h=h: e.tensor_copy(th[:, h:h + 1], c8[h % 2][:, 15:16]), reads=[b_c8[h % 2]], writes=[b_th])
                    yield
                S.op("act", lambda e: e.activation(ce[:], cand[:], AF.Exp), reads=[b_cand], writes=[b_cand])
                S.op("dve", lambda e: e.tensor_tensor(cand[:], cand[:], apx(th[:, 0:1], [[1, 8], [0, 256]]), ALU.is_ge), reads=[b_cand, b_th], writes=[b_cand])
                S.op("dve", lambda e: e.tensor_tensor(ce[:], ce[:], cand[:], ALU.mult), reads=[b_cand], writes=[b_cand])
                S.op("dve", lambda e: e.tensor_reduce(zz[:], ce[:], AX.X, ALU.add), reads=[b_cand, b_th], writes=[b_th])
                S.op("dve", lambda e: e.reciprocal(rz[:], zz[:]), reads=[b_th], writes=[b_th])
                s1tops = apx(tops[:, 0, 0:1], [[32, 8], [1, 16]])
                tk3 = lambda j: tk[:, j, :].rearrange("p (h a) -> p h a", h=8)
                S.op("dve", lambda e: e.tensor_tensor(tk3(0), apx(th[:, 0:1], [[1, 8], [0, 16]]), s1tops, ALU.subtract), reads=[b_th, b_top], writes=[b_tk])
                S.op("dve", lambda e: e.tensor_scalar(tk[:, 0, :], tk[:, 0, :], -1e-5, None, ALU.add), reads=[b_tk], writes=[b_tk])
                S.op("act", lambda e: e.activation(tk3(1), s1tops, AF.Exp), reads=[b_top], writes=[b_tk])
                S.op("dve", lambda e: e.tensor_tensor(tk3(1), tk3(1), apx(rz[:, 0:1], [[1, 8], [0, 16]]), ALU.mult), reads=[b_tk, b_th], writes=[b_tk])
                S.op("dve", lambda e: e.tensor_copy(tk3(2), s1tops), reads=[b_top], writes=[b_tk])

                def trk(e):
                    for j in range(3):
                        ins = e.transpose(ps_t[:, j * 128:(j + 1) * 128], tk[:, j, :], identf)
                    return ins
                S.op("pe", trk, reads=[b_tk, b_cst], writes=[b_ps_t])
                S.op("act", lambda e: e.copy(tkT[i][:], ps_t[:, 0:384].rearrange("p (j t) -> p j t", j=3)), reads=[b_ps_t], writes=[b_tkT[i]])
                S.op("act", lambda e: e.copy(kapb[i][:], ps_t[:, 128:256]), reads=[b_ps_t], writes=[b_tkT[i]])
                yield
            def front(sti, g, genA):
                ts0 = sti * 128
                i = sti % 2
                r = cnts[0] % 2; cnts[0] += 1
                tg0 = ts0 + g * 32
                for h in range(H):
                    S.dma("sp", s1r[r][h * 16:(h + 1) * 16, :, :].rearrange("p t k -> p (t k)"), bass.AP(SCS.tensor, ((2 * h) * NOWN + tg0) * 128, [[0, 16], [1, 4096]]),
                          reads=[b_SCS[sti % 4]], writes=[b_rep[r]])
                    S.dma("sp", s2r[r][h * 16:(h + 1) * 16, :, :].rearrange("p t k -> p (t k)"), bass.AP(SCS.tensor, ((2 * h + 1) * NOWN + tg0) * 128, [[0, 16], [1, 4096]]),
                          reads=[b_SCS[sti % 4]], writes=[b_rep[r]])
                bc = lambda j: apx(tkT[i][:, j, g * 32:g * 32 + 1], [[1, 32], [0, 128]])
                S.op("act", lambda e: e.activation(e2[:], s2r[r][:], AF.Exp), reads=[b_rep[r]], writes=[b_e2])
                S.op("dve", lambda e: e.tensor_tensor(mk[:], s2r[r][:], bc(0), ALU.is_ge), reads=[b_rep[r], b_tkT[i]], writes=[b_mk])
                S.op("pool", lambda e: e.tensor_tensor(Lt[r][:], mk[:], e2[:], ALU.mult), reads=[b_mk, b_e2], writes=[b_L[r]])
                S.op("dve", lambda e: e.tensor_tensor(eq[:], s1r[r][:], bc(2), ALU.is_equal), reads=[b_rep[r], b_tkT[i]], writes=[b_eq])
                S.op("pool", lambda e: e.tensor_tensor(Rt[r][:], eq[:], apx(kapb[i][:, g * 32:g * 32 + 1], [[1, 32], [0, 128]]), ALU.mult),
                     reads=[b_eq, b_tkT[i]], writes=[b_R[r]])
                if genA is not None:
                    for _ in range(5):
                        next(genA, None)
                    if g == 3:
                        for _ in genA:
                            pass
                return r

            def back(sti, g, r):
                ts0 = sti * 128
                for t4 in range(8):
                    k = cnts[1] % 3; cnts[1] += 1

                    def mmg(e, t4=t4, k=k):
                        for j in range(4):
                            tl = t4 * 4 + j
                            ins = e.matmul(ps_g[k][:, j * 128:(j + 1) * 128], Lt[r][:, tl, :], Rt[r][:, tl, :], start=True, stop=True)
                        return ins
                    S.op("pe", mmg, reads=[b_L[r], b_R[r]], writes=[b_ps_g[k]])
                    col = g * 32 + t4 * 4
                    S.op("act", lambda e, k=k, col=col: e.copy(gts[:, :, col:col + 4], apx(ps_g[k][:, 0:1], [[1, 128], [128, 4]])),
                         reads=[b_ps_g[k]], writes=[b_gts])
                if g == 3:
                    S.dma("act", GT[sti], gts[:], reads=[b_gts], writes=[b_GT])

            nst = NOWN // 128
            for _ in stageA(0):
                pass
            groups = [(sti, g) for sti in range(nst) for g in range(4)]
            gens = {}
            prev = None
            for (sti, g) in groups:
                if g == 0:
                    gens[sti] = stageA(sti + 1) if sti + 1 < nst else None
                r = front(sti, g, gens[sti])
                if prev is not None:
                    back(*prev)
                prev = (sti, g, r)
            back(*prev)
            S.barrier()
        if upto <= 5:
            return nc

        with ExitStack() as st:
            psb = [pst(st, f"p5psb{i}", [128, 1024], BF16) for i in range(2)]; b_psb = [Buf(), Buf()]
            ps_a = [pst(st, f"p5a{i}", [128, 512]) for i in range(2)]; b_ps_a = [Buf(), Buf()]
            ps_o = [pst(st, f"p5o{i}", [128, 512]) for i in range(3)]; b_ps_o = [Buf() for _ in range(3)]
            b_UT = Buf()
            with ExitStack() as st2:
                un = [sbt(st2, f"p5un{i}", [128, 4, D], BF16) for i in range(2)]; b_un = [Buf(), Buf()]
                uts = [sbt(st2, f"p5uts{i}", [128, KC, 512], BF16) for i in range(2)]; b_uts = [Buf(), Buf()]
                tcn = 0
                for g in range(32):
                    i = g % 2
                    S.dma("sp", un[i][:], UB[g * 512:(g + 1) * 512, :].rearrange("(j p) d -> p j d", p=128), reads=[b_w], writes=[b_un[i]])
                    for kc in range(KC):
                        pi = tcn % 2; tcn += 1

                        def tru(e, i=i, kc=kc, pi=pi):
                            for j in range(4):
                                ins = e.transpose(psb[pi][:, j * 128:(j + 1) * 128], un[i][:, j, kc * 128:(kc + 1) * 128], identb)
                            return ins
                        S.op("pe", tru, reads=[b_un[i], b_cstb], writes=[b_psb[pi]])
                        eng = "act" if kc % 2 == 0 else "dve"
                        if eng == "act":
                            S.op("act", lambda e, i=i, kc=kc, pi=pi: e.copy(uts[i][:, kc, :], psb[pi][:, 0:512]), reads=[b_psb[pi]], writes=[b_uts[i]])
                        else:
                            S.op("dve", lambda e, i=i, kc=kc, pi=pi: e.tensor_copy(uts[i][:, kc, :], psb[pi][:, 0:512]), reads=[b_psb[pi]], writes=[b_uts[i]])
                    S.dma("sp", UT[:, g * 512:(g + 1) * 512].rearrange("(c p) e -> p c e", p=128), uts[i][:], reads=[b_uts[i]], writes=[b_UT])
            h2T = sbt(st, "p5h2", [128, KC, T], BF16); b_h2 = Buf()
            acc = sbt(st, "p5acc", [128, 4, D]); b_acc = Buf()
            ug = [sbt(st, f"p5ug{i}", [128, KC, 512], BF16) for i in range(2)]
            vg = [sbt(st, f"p5vg{i}", [128, 4, D], BF16) for i in range(2)]
            gg = [sbt(st, f"p5gg{i}", [128, 4 * T], BF16) for i in range(2)]; b_g = [Buf(), Buf()]
            ga = [sbt(st, f"p5ga{i}", [128, T]) for i in range(2)]; b_ga = [Buf(), Buf()]
            wT = [sbt(st, f"p5wT{i}", [128, 4, T], BF16) for i in range(2)]; b_wT = [Buf(), Buf()]
            xm = [sbt(st, f"p5xm{i}", [128, D]) for i in range(2)]; b_xm = [Buf(), Buf()]
            g2 = sbt(st, "p5g2", [128, D]); b_g2 = Buf()
            ac = 0; oc = 0; xc = 0
            for (t0, seg) in own_tiles:
                S.dma("sp", h2T[:], H2T[:, t0:t0 + T].rearrange("(c p) t -> p c t", p=128), reads=[b_H2T], writes=[b_h2])
                S.dma("sp", g2[:], bass.AP(MOD.tensor, seg * 6 * D + 5 * D, [[0, 128], [1, D]]), reads=[b_MOD], writes=[b_g2])
                for g in range(32):
                    i = g % 2
                    S.dma("sp", ug[i][:], UT[:, g * 512:(g + 1) * 512].rearrange("(c p) e -> p c e", p=128), reads=[b_UT], writes=[b_g[i]])
                    S.dma("sp", vg[i][:], VB[g * 512:(g + 1) * 512, :].rearrange("(j p) d -> p j d", p=128), reads=[b_w], writes=[b_g[i]])
                    for s4 in range(4):
                        S.dma("sp", gg[i][:, s4 * 512:(s4 + 1) * 512], GT[t0 // 128 + s4, :, g * 4:(g + 1) * 4, :].rearrange("p c t -> p (c t)"), reads=[b_GT], writes=[b_g[i]])
                    for j in range(4):
                        ai = ac % 2; ac += 1

                        def mma(e, i=i, j=j, ai=ai):
                            for kc in range(KC):
                                ins = e.matmul(ps_a[ai][:], ug[i][:, kc, j * 128:(j + 1) * 128], h2T[:, kc, :], start=(kc == 0), stop=(kc == KC - 1))
                            return ins
                        S.op("pe", mma, reads=[b_g[i], b_h2], writes=[b_ps_a[ai]])
                        S.op("act", lambda e, ai=ai: e.activation(ga[ai][:], ps_a[ai][:], AF.Gelu_apprx_tanh), reads=[b_ps_a[ai]], writes=[b_ga[ai]])
                        S.op("dve", lambda e, i=i, j=j, ai=ai: e.tensor_tensor(wT[i][:, j, :].rearrange("p (s t) -> p s t", s=4), ga[ai][:].rearrange("p (s t) -> p s t", s=4), apx(gg[i][:, j * 128:j * 128 + 1], [[512, 4], [1, 128]]), ALU.mult),
                             reads=[b_ga[ai], b_g[i]], writes=[b_wT[i]])
                    for s in range(4):
                        for cb in range(4):
                            oi = oc % 3; oc += 1

                            def mmo5(e, i=i, s=s, cb=cb, oi=oi):
                                for j in range(4):
                                    ins = e.matmul(ps_o[oi][:], wT[i][:, j, s * 128:(s + 1) * 128], vg[i][:, j, cb * 512:(cb + 1) * 512], start=(j == 0), stop=(j == 3))
                                return ins
                            S.op("pe", mmo5, reads=[b_wT[i], b_g[i]], writes=[b_ps_o[oi]])
                            eng = "dve" if (s * 4 + cb) % 2 == 0 else "pool"
                            dst = acc[:, s, cb * 512:(cb + 1) * 512]
                            if g == 0:
                                S.op("act", lambda e, dst=dst, oi=oi: e.copy(dst, ps_o[oi][:]), reads=[b_ps_o[oi]], writes=[b_acc])
                            else:
                                S.op("dve", lambda e, dst=dst, oi=oi: e.tensor_tensor(dst, dst, ps_o[oi][:], ALU.add), reads=[b_ps_o[oi], b_acc], writes=[b_acc])
                for s in range(4):
                    xi = xc % 2; xc += 1
                    S.dma("pool", xm[xi][:], XMID[t0 + s * 128:t0 + (s + 1) * 128, :], reads=[b_XMID], writes=[b_xm[xi]])
                    S.op("dve", lambda e, s=s: e.tensor_tensor(acc[:, s, :], acc[:, s, :], g2[:], ALU.mult), reads=[b_acc, b_g2], writes=[b_acc])
                    S.op("pool", lambda e, s=s, xi=xi: e.tensor_tensor(xm[xi][:], xm[xi][:], acc[:, s, :], ALU.add), reads=[b_acc, b_xm[xi]], writes=[b_xm[xi]])
                    S.dma("pool", yout[t0 + s * 128:t0 + (s + 1) * 128, :], xm[xi][:], reads=[b_xm[xi]])
            S.barrier()
    return nc


def _tables(pos):
    pos = pos.astype(np.float32)
    r = np.arange(128)
    invM = (1.0 / (np.float32(10000.0) ** (np.arange(0, 64, 2, dtype=np.float32) / np.float32(64)))).astype(np.float32)
    angM = (pos[None, :] * invM[r % 32][:, None]).astype(np.float32)
    sgnM = np.where((r % 64) < 32, -1.0, 1.0).astype(np.float32)[:, None]
    cosM = np.cos(angM).astype(np.float32); sinM = (np.sin(angM).astype(np.float32) * sgnM).astype(np.float32)
    invR = (1.0 / (np.float32(10000.0) ** (np.arange(0, 128, 2, dtype=np.float32) / np.float32(128)))).astype(np.float32)
    angR = (pos[None, :] * invR[r % 64][:, None]).astype(np.float32)
    sgnR = np.where(r < 64, -1.0, 1.0).astype(np.float32)[:, None]
    cosR = np.cos(angR).astype(np.float32); sinR = (np.sin(angR).astype(np.float32) * sgnR).astype(np.float32)
    ks = np.float32(128.0 ** -0.5)
    return cosM, sinM, cosR, sinR, (cosR * ks).astype(np.float32), (sinR * ks).astype(np.float32)


def _consts():
    cst = np.zeros((128, 10, 128), np.float32)
    m = np.arange(128)
    cst[:, 0, :] = np.eye(128)
    cst[:, 1, :] = 1.0
    cst[m ^ 32, 2, m] = 1.0
    cst[m ^ 64, 3, m] = 1.0
    return cst


_PROG = {}


def kernel(x_prompt, x_sample, c_prompt, c_sample, norm1_w, norm2_w, w_ada, b_ada, w_in, q_a_norm, kv_a_norm,
           w_uq, w_uk, w_uv, q_norm, k_norm, ret_decay_logit, ret_gn_w, w_o, peer_wq, peer_sub_keys, peer_u, peer_v,
           _upto=9, _dbg=(), _trace=False):
    f = lambda a: np.ascontiguousarray(np.asarray(a, dtype=np.float32))
    x_prompt, x_sample, c_prompt, c_sample = f(x_prompt), f(x_sample), f(c_prompt), f(c_sample)
    SEQ = x_prompt.shape[1]; SS = x_sample.shape[1]; SA = SEQ // 2
    NOWN = SA + 2 * SS; NTOK = NOWN + SA
    key = (SA, SS, _upto, tuple(_dbg))
    if key not in _PROG:
        _PROG[key] = build(SA, SS, _upto, _dbg)
    nc = _PROG[key]
    shared = {
        "norm1_w": f(norm1_w).reshape(-1), "norm2_w": f(norm2_w).reshape(-1), "w_ada": f(w_ada)[0], "b_ada": f(b_ada).reshape(-1),
        "w_in": f(w_in)[0], "q_a_norm": f(q_a_norm).reshape(-1), "kv_a_norm": f(kv_a_norm).reshape(-1),
        "w_uq": f(w_uq)[0], "w_uk": f(w_uk)[0], "w_uv": f(w_uv)[0], "q_norm": f(q_norm).reshape(-1), "k_norm": f(k_norm).reshape(-1),
        "ret_decay_logit": f(ret_decay_logit).reshape(-1), "ret_gn_w": f(ret_gn_w).reshape(-1), "w_o": f(w_o)[0],
        "peer_wq": f(peer_wq)[0], "peer_sub_keys": f(peer_sub_keys).reshape(2048, 128),
        "peer_u": f(peer_u)[0], "peer_v": f(peer_v)[0], "cst": _consts(),
    }
    in_maps = []
    for c in range(NCORES):
        pb, par = c // 2, c % 2
        own = x_prompt[pb, par * SA:(par + 1) * SA]; oth = x_prompt[pb, (1 - par) * SA:(2 - par) * SA]
        xin = np.concatenate([own, x_sample[2 * c], x_sample[2 * c + 1], oth], axis=0)
        pos = np.concatenate([par * SA + np.arange(SA), np.arange(SS), np.arange(SS), (1 - par) * SA + np.arange(SA)]).astype(np.float32)
        cosM, sinM, cosR, sinR, cosRk, sinRk = _tables(pos)
        m = dict(shared)
        m.update({"xin": np.ascontiguousarray(xin), "c3": np.stack([c_prompt[pb], c_sample[2 * c], c_sample[2 * c + 1]]),
                  "cosM": cosM, "sinM": sinM, "cosR": cosR, "sinR": sinR, "cosRk": cosRk, "sinRk": sinRk,
                  "posrow": np.ascontiguousarray(np.broadcast_to(pos[None, :], (128, NTOK))),
                  "poscol": np.ascontiguousarray(pos.reshape(NTOK // 128, 128).T),
                  "cvec": np.zeros((128, 4), np.float32)})
        in_maps.append(m)
    res = run_bass_kernel_spmd(nc, in_maps, core_ids=list(range(NCORES)), **({'trace': True} if _trace else {}))
    if _trace:
        print('EXEC_TIME_NS', _upto, res.exec_time_ns)
    yp = np.zeros(x_prompt.shape, np.float32); ys = np.zeros(x_sample.shape, np.float32)
    for c in range(NCORES):
        y = res.results[c]["yout"]
        pb, par = c // 2, c % 2
        yp[pb, par * SA:(par + 1) * SA] = y[0:SA]
        ys[2 * c] = y[SA:SA + SS]; ys[2 * c + 1] = y[SA + SS:SA + 2 * SS]
    if _dbg:
        return (yp, ys), res
    return (yp, ys)
```
